# Optimizing a Trainium2 kernel written in Bass

```python
import jax
import jax.numpy as jnp
from jax import lax
import numpy as np

D_MODEL = 1024
BATCH = 16
SEQ = 256
DEPTH = 4
DEC_BATCH = 4
DEC_SEQ = 2048
PAST_LEN = 512

GRID_W = 64
HEAD_DIM = 64
BRANCH_W = 256
N_BRANCH = 4
NA_HEADS = 4
NA_WIN_H = 8
NA_WIN_W = 16
NA_QBW = 16
NA_KBW = 32
HG_HEADS = 4
HG_DK = 64
HG_DV = 64
HG_CHUNK = 64
F_FLOOR = 1e-30
MLA_HEADS = 4
MLA_Q_RANK = 256
MLA_KV_RANK = 128
MLA_NOPE = 64
MLA_ROPE = 32
MLA_V = 64
WG_HEADS = 4
WG_KV_HEADS = 2
WG_WINDOW = 128
WG_BLOCK = 128

Q_BLOCK = 128
ROPE_BASE = 10000.0
EPS = 1e-6
NEG_INF = -1e30

IN_LAYOUT = (
    ('na_q', NA_HEADS * HEAD_DIM), ('na_k', NA_HEADS * HEAD_DIM), ('na_v', NA_HEADS * HEAD_DIM), ('na_z', BRANCH_W),
    ('hg_q', HG_HEADS * HG_DK), ('hg_ff', HG_HEADS * HG_DK), ('hg_fb', HG_HEADS * HG_DK),
    ('hg_i', HG_HEADS * HG_DV), ('hg_og', HG_HEADS * HG_DV), ('hg_z', BRANCH_W),
    ('mla_qa', MLA_Q_RANK), ('mla_kva', MLA_KV_RANK + MLA_ROPE), ('mla_z', BRANCH_W),
    ('wg_q', WG_HEADS * HEAD_DIM), ('wg_k', WG_KV_HEADS * HEAD_DIM), ('wg_v', WG_KV_HEADS * HEAD_DIM), ('wg_z', BRANCH_W),
    ('merge', N_BRANCH * D_MODEL),
)
IN_WIDTH = sum(w for _, w in IN_LAYOUT)

kernel_name = 'hybrid_diffusion_prefix_step'

F32 = jnp.float32


def rms_norm(x, g):
    xf = x.astype(F32)
    y = xf * lax.rsqrt(jnp.mean(xf * xf, axis=-1, keepdims=True) + EPS)
    return (y * g.astype(F32)).astype(x.dtype)


def heads(a, n):
    return a.reshape(a.shape[:-1] + (n, a.shape[-1] // n))


def split_cols(p):
    out = {}
    off = 0
    for name, w in IN_LAYOUT:
        out[name] = p[..., off:off + w]
        off += w
    return out


def axial_rope(n_tok, rot_dim):
    t = jnp.arange(n_tok)
    row = (t // GRID_W).astype(F32)
    col = (t % GRID_W).astype(F32)
    nf = rot_dim // 4
    inv = ROPE_BASE ** (-jnp.arange(nf, dtype=F32) / nf)
    ang = jnp.concatenate([row[:, None] * inv, col[:, None] * inv], axis=-1)
    return jnp.cos(ang), jnp.sin(ang)


def apply_rope(x, cos, sin):
    x1, x2 = jnp.split(x.astype(F32), 2, axis=-1)
    c = cos[None, :, None, :]
    s = sin[None, :, None, :]
    return jnp.concatenate([x1 * c - x2 * s, x1 * s + x2 * c], axis=-1).astype(x.dtype)


def sink_softmax(s, sink=None):
    if sink is None:
        return jax.nn.softmax(s, axis=-1)
    sk = sink.astype(F32)
    m = jnp.maximum(jnp.max(s, axis=-1, keepdims=True), sk)
    e = jnp.exp(s - m)
    return e / (jnp.sum(e, axis=-1, keepdims=True) + jnp.exp(sk - m))


def block_attention(q, k, v, sink=None):
    B, Lq, H, dq = q.shape
    Hk = k.shape[2]
    G = H // Hk
    dv = v.shape[-1]
    nb = Lq // Q_BLOCK
    scale = dq ** -0.5
    qb = q.reshape(B, nb, Q_BLOCK, Hk, G, dq).transpose(1, 0, 2, 3, 4, 5)
    sk = None if sink is None else sink.reshape(1, Hk, G, 1, 1)

    def one(qblk):
        s = jnp.einsum('bqkgd,bskd->bkgqs', qblk, k).astype(F32) * scale
        p = sink_softmax(s, sk)
        return jnp.einsum('bkgqs,bskd->bqkgd', p.astype(v.dtype), v)

    o = lax.map(one, qb)
    return o.transpose(1, 0, 2, 3, 4, 5).reshape(B, Lq, H, dv)


def neighborhood_attention(q, k, v, kc, vc, rpb):
    B, T, H, d = q.shape
    rows = T // GRID_W
    wh = min(NA_WIN_H, rows)
    ncb = GRID_W // NA_QBW
    scale = d ** -0.5
    r = jnp.arange(rows)
    key_rows = jnp.clip(r - wh // 2, 0, rows - wh)[:, None] + jnp.arange(wh)[None, :]
    qcol = jnp.arange(GRID_W).reshape(ncb, NA_QBW)
    kcol = jnp.clip(jnp.arange(ncb) * NA_QBW - NA_WIN_W // 2, 0, GRID_W - NA_KBW)[:, None] + jnp.arange(NA_KBW)[None, :]
    qstart = jnp.clip(qcol - NA_WIN_W // 2, 0, GRID_W - NA_WIN_W)
    kc3 = kcol[:, None, :]
    col_ok = (kc3 >= qstart[..., None]) & (kc3 < qstart[..., None] + NA_WIN_W)
    drow = key_rows - r[:, None]
    dcol = jnp.clip(kc3 - qcol[:, :, None] + NA_WIN_W - 1, 0, 2 * NA_WIN_W - 2)
    bias = rpb[:, (drow + NA_WIN_H - 1)[:, None, None, :, None], dcol[None, :, :, None, :]]
    kg = k.reshape(B, rows, GRID_W, H, d)
    vg = v.reshape(B, rows, GRID_W, H, d)
    ri = key_rows[:, :, None, None]
    ci = kcol[None, None, :, :]
    kw = kg[:, ri, ci]
    vw = vg[:, ri, ci]
    qg = q.reshape(B, rows, ncb, NA_QBW, H, d)
    s_loc = jnp.einsum('brcqhd,brwckhd->bhrcqwk', qg, kw).astype(F32) * scale + bias[None].astype(F32)
    s_loc = jnp.where(col_ok[None, None, None, :, :, None, :], s_loc, NEG_INF)
    nloc = wh * NA_KBW
    s_loc = s_loc.reshape(B, H, rows, ncb, NA_QBW, nloc)
    s_ctx = jnp.einsum('brcqhd,blhd->bhrcql', qg, kc).astype(F32) * scale
    p = jax.nn.softmax(jnp.concatenate([s_loc, s_ctx], axis=-1), axis=-1)
    p_loc = p[..., :nloc].reshape(B, H, rows, ncb, NA_QBW, wh, NA_KBW).astype(v.dtype)
    o = (jnp.einsum('bhrcqwk,brwckhd->brcqhd', p_loc, vw)
         + jnp.einsum('bhrcql,blhd->brcqhd', p[..., nloc:].astype(vc.dtype), vc))
    return o.reshape(B, T, H, d)


def window_attention(q, k, v, kc, vc, sink):
    B, T, H, d = q.shape
    Hk = k.shape[2]
    G = H // Hk
    nb = T // WG_BLOCK
    scale = d ** -0.5
    qb = q.reshape(B, nb, WG_BLOCK, Hk, G, d)
    pad = ((0, 0), (WG_BLOCK, WG_BLOCK), (0, 0), (0, 0))
    kp = jnp.pad(k, pad).reshape(B, nb + 2, WG_BLOCK, Hk, d)
    vp = jnp.pad(v, pad).reshape(B, nb + 2, WG_BLOCK, Hk, d)
    kw = jnp.concatenate([kp[:, :-2], kp[:, 1:-1], kp[:, 2:]], axis=2)
    vw = jnp.concatenate([vp[:, :-2], vp[:, 1:-1], vp[:, 2:]], axis=2)
    a = jnp.arange(WG_BLOCK)
    m = jnp.arange(3 * WG_BLOCK)
    rel = m[None, :] - WG_BLOCK - a[:, None]
    kpos = jnp.arange(nb)[:, None] * WG_BLOCK - WG_BLOCK + m[None, :]
    valid = (jnp.abs(rel) <= WG_WINDOW)[None] & ((kpos >= 0) & (kpos < T))[:, None, :]
    s_loc = jnp.einsum('bnqkgd,bnmkd->bkgnqm', qb, kw).astype(F32) * scale
    s_loc = jnp.where(valid[None, None, None], s_loc, NEG_INF)
    s_ctx = jnp.einsum('bnqkgd,blkd->bkgnql', qb, kc).astype(F32) * scale
    p = sink_softmax(jnp.concatenate([s_loc, s_ctx], axis=-1), sink.reshape(1, Hk, G, 1, 1, 1))
    nloc = 3 * WG_BLOCK
    o = (jnp.einsum('bkgnqm,bnmkd->bnqkgd', p[..., :nloc].astype(v.dtype), vw)
         + jnp.einsum('bkgnql,blkd->bnqkgd', p[..., nloc:].astype(vc.dtype), vc))
    return o.reshape(B, T, H, d)


def chunk_scan(q, k, v, logf, s0):
    B, T, H, dk = q.shape
    dv = v.shape[-1]
    C = HG_CHUNK
    n = T // C
    causal = jnp.tril(jnp.ones((C, C), dtype=bool))

    def to_chunks(a):
        return a.reshape(B, n, C, H, a.shape[-1]).transpose(1, 0, 3, 2, 4)

    def step(S, inp):
        qc, kc, vc, gc = inp
        b = jnp.cumsum(gc, axis=2)
        dec = jnp.exp(jnp.where(causal[:, :, None], b[:, :, :, None, :] - b[:, :, None, :, :], NEG_INF))
        att = jnp.einsum('bhtd,bhsd,bhtsd->bhts', qc, kc, dec)
        o = jnp.einsum('bhts,bhsv->bhtv', att, vc) + jnp.einsum('bhtd,bhdv->bhtv', qc * jnp.exp(b), S)
        b_last = b[:, :, -1:, :]
        S = jnp.exp(b_last[:, :, 0, :, None]) * S + jnp.einsum('bhsd,bhsv->bhdv', kc * jnp.exp(b_last - b), vc)
        return S, o

    S, o = lax.scan(step, s0, (to_chunks(q), to_chunks(k), to_chunks(v), to_chunks(logf)))
    return o.transpose(1, 0, 3, 2, 4).reshape(B, T, H, dv), S


def hgrn_gates(fz, lb):
    fz = fz.astype(F32)
    lb = lb.astype(F32)
    f = lb + (1.0 - lb) * jax.nn.sigmoid(fz)
    logf = jnp.log(jnp.maximum(f, F_FLOOR))
    k = (1.0 - lb) * jax.nn.sigmoid(-fz)
    return heads(logf, HG_HEADS), heads(k, HG_HEADS)


def hgrn_branch(cols, lb_pair, norm_g, s0_f, s0_b):
    q = heads(jax.nn.silu(cols['hg_q']).astype(F32), HG_HEADS)
    i = heads(cols['hg_i'].astype(F32), HG_HEADS)
    outs = []
    finals = []
    for fz, lb, s0, rev in ((cols['hg_ff'], lb_pair[0], s0_f, False), (cols['hg_fb'], lb_pair[1], s0_b, True)):
        logf, k = hgrn_gates(fz, lb)
        seq = (q, k, i, logf)
        if rev:
            seq = tuple(jnp.flip(a, axis=1) for a in seq)
        o, s = chunk_scan(seq[0], seq[1], seq[2], seq[3], s0)
        outs.append(jnp.flip(o, axis=1) if rev else o)
        finals.append(s)
    o = rms_norm(outs[0] + outs[1], norm_g) * jax.nn.sigmoid(heads(cols['hg_og'], HG_HEADS).astype(F32))
    return o.astype(cols['hg_q'].dtype), finals[0], finals[1]


def mla_query(qa, g_qa, w_qb, rope):
    q = heads(rms_norm(qa, g_qa) @ w_qb, MLA_HEADS)
    if rope is not None:
        q = jnp.concatenate([q[..., :MLA_NOPE], apply_rope(q[..., MLA_NOPE:], rope[0], rope[1])], axis=-1)
    return q


def mla_compress(kva, g_kv, rope):
    ckv = rms_norm(kva[..., :MLA_KV_RANK], g_kv)
    kr = kva[..., MLA_KV_RANK:]
    if rope is not None:
        kr = apply_rope(kr[:, :, None, :], rope[0], rope[1])[:, :, 0, :]
    return ckv, kr


def mla_expand(ckv, kr, w_kvb):
    kv = heads(ckv @ w_kvb, MLA_HEADS)
    B, T, H, _ = kv.shape
    k = jnp.concatenate([kv[..., :MLA_NOPE], jnp.broadcast_to(kr[:, :, None, :], (B, T, H, MLA_ROPE))], axis=-1)
    return k, kv[..., MLA_NOPE:]


def layer_pre(x, cvec, W, l):
    mod = jax.nn.silu(cvec) @ W['w_mod'][l] + W['b_mod'][l]
    shift, scale, gate = jnp.split(mod[:, None, :], 3, axis=-1)
    h = rms_norm(x, W['norm_g'][l]) * (1 + scale) + shift
    return split_cols(h @ W['w_in'][l]), gate


def layer_post(x, cols, gate, branch_outs, W, l):
    B, T, _ = x.shape
    zs = (cols['na_z'], cols['hg_z'], cols['mla_z'], cols['wg_z'])
    y = jnp.stack([o.reshape(B, T, BRANCH_W) * jax.nn.silu(z) for o, z in zip(branch_outs, zs)], axis=2)
    proj = jnp.einsum('btnw,nwd->btnd', y, W['w_branch'][l])
    mg = jax.nn.sigmoid(cols['merge'].reshape(B, T, N_BRANCH, D_MODEL))
    merged = jnp.sum(mg * proj, axis=2)
    return x + gate * (merged @ W['w_out'][l])


def context_layer(x, cvec, lb, W, l):
    cols, gate = layer_pre(x, cvec, W, l)
    B = x.shape[0]
    na_k = heads(cols['na_k'], NA_HEADS)
    na_v = heads(cols['na_v'], NA_HEADS)
    o_na = block_attention(heads(cols['na_q'], NA_HEADS), na_k, na_v)
    zero = jnp.zeros((B, HG_HEADS, HG_DK, HG_DV), F32)
    o_hg, s_f, s_b = hgrn_branch(cols, lb, W['hg_norm_g'][l], zero, zero)
    q = mla_query(cols['mla_qa'], W['mla_q_norm_g'][l], W['mla_w_qb'][l], None)
    ckv, kr = mla_compress(cols['mla_kva'], W['mla_kv_norm_g'][l], None)
    k, v = mla_expand(ckv, kr, W['mla_w_kvb'][l])
    o_mla = block_attention(q, k, v)
    wk = heads(cols['wg_k'], WG_KV_HEADS)
    wv = heads(cols['wg_v'], WG_KV_HEADS)
    o_wg = block_attention(heads(cols['wg_q'], WG_HEADS), wk, wv, W['wg_sink'][l])
    x = layer_post(x, cols, gate, (o_na, o_hg, o_mla, o_wg), W, l)
    return x, (na_k, na_v, jnp.stack([s_f, s_b], axis=1), ckv, kr, wk, wv)


def latent_layer(x, c, cache, lb, rope_mla, rope_wg, W, l):
    kc_na, vc_na, st, ckv_c, kr_c, kc_wg, vc_wg = cache
    cols, gate = layer_pre(x, c, W, l)
    o_na = neighborhood_attention(heads(cols['na_q'], NA_HEADS), heads(cols['na_k'], NA_HEADS),
                                  heads(cols['na_v'], NA_HEADS), kc_na, vc_na, W['na_rpb'][l])
    o_hg, _, _ = hgrn_branch(cols, lb, W['hg_norm_g'][l], st[:, 0].astype(F32), st[:, 1].astype(F32))
    q = mla_query(cols['mla_qa'], W['mla_q_norm_g'][l], W['mla_w_qb'][l], rope_mla)
    ckv, kr = mla_compress(cols['mla_kva'], W['mla_kv_norm_g'][l], rope_mla)
    k_lat, v_lat = mla_expand(ckv, kr, W['mla_w_kvb'][l])
    k_ctx, v_ctx = mla_expand(ckv_c, kr_c, W['mla_w_kvb'][l])
    o_mla = block_attention(q, jnp.concatenate([k_lat, k_ctx], axis=1), jnp.concatenate([v_lat, v_ctx], axis=1))
    wq = apply_rope(heads(cols['wg_q'], WG_HEADS), rope_wg[0], rope_wg[1])
    wk = apply_rope(heads(cols['wg_k'], WG_KV_HEADS), rope_wg[0], rope_wg[1])
    o_wg = window_attention(wq, wk, heads(cols['wg_v'], WG_KV_HEADS), kc_wg, vc_wg, W['wg_sink'][l])
    return layer_post(x, cols, gate, (o_na, o_hg, o_mla, o_wg), W, l)


def setup_inputs(seed: int = 0) -> dict:
    key = jax.random.key(seed)
    ks = jax.random.split(key, 32)

    def nrm(k, shape, s=1.0):
        return jax.random.normal(k, shape, F32) * s

    L = PAST_LEN
    Bd = DEC_BATCH
    return {
        'x_prompt': nrm(ks[0], (BATCH, SEQ, D_MODEL)),
        'x_sample': nrm(ks[1], (DEC_BATCH, DEC_SEQ, D_MODEL)),
        'cache_na_k': nrm(ks[2], (Bd, DEPTH, L, NA_HEADS, HEAD_DIM)),
        'cache_na_v': nrm(ks[3], (Bd, DEPTH, L, NA_HEADS, HEAD_DIM)),
        'state_hgrn': nrm(ks[4], (Bd, DEPTH, 2, HG_HEADS, HG_DK, HG_DV)),
        'cache_mla_ckv': nrm(ks[5], (Bd, DEPTH, L, MLA_KV_RANK)),
        'cache_mla_krope': nrm(ks[6], (Bd, DEPTH, L, MLA_ROPE)),
        'cache_wg_k': nrm(ks[7], (Bd, DEPTH, L, WG_KV_HEADS, HEAD_DIM)),
        'cache_wg_v': nrm(ks[8], (Bd, DEPTH, L, WG_KV_HEADS, HEAD_DIM)),
        'c': nrm(ks[9], (DEC_BATCH, D_MODEL)),
        'c_ctx': nrm(ks[10], (D_MODEL,)),
        'norm_g': 1.0 + nrm(ks[11], (DEPTH, D_MODEL), 0.02),
        'w_mod': nrm(ks[12], (DEPTH, D_MODEL, 3 * D_MODEL), 0.5 * D_MODEL ** -0.5),
        'b_mod': nrm(ks[13], (DEPTH, 3 * D_MODEL), 0.01),
        'w_in': nrm(ks[14], (DEPTH, D_MODEL, IN_WIDTH), D_MODEL ** -0.5),
        'na_rpb': nrm(ks[15], (DEPTH, NA_HEADS, 2 * NA_WIN_H - 1, 2 * NA_WIN_W - 1), 0.1),
        'hg_lb_logits': nrm(ks[16], (DEPTH, 2, HG_HEADS * HG_DK), 0.5),
        'hg_norm_g': 1.0 + nrm(ks[17], (DEPTH, HG_DV), 0.02),
        'mla_q_norm_g': 1.0 + nrm(ks[18], (DEPTH, MLA_Q_RANK), 0.02),
        'mla_kv_norm_g': 1.0 + nrm(ks[19], (DEPTH, MLA_KV_RANK), 0.02),
        'mla_w_qb': nrm(ks[20], (DEPTH, MLA_Q_RANK, MLA_HEADS * (MLA_NOPE + MLA_ROPE)), MLA_Q_RANK ** -0.5),
        'mla_w_kvb': nrm(ks[21], (DEPTH, MLA_KV_RANK, MLA_HEADS * (MLA_NOPE + MLA_V)), MLA_KV_RANK ** -0.5),
        'wg_sink': nrm(ks[22], (DEPTH, WG_HEADS), 0.5),
        'w_branch': nrm(ks[23], (DEPTH, N_BRANCH, BRANCH_W, D_MODEL), BRANCH_W ** -0.5),
        'w_out': nrm(ks[24], (DEPTH, D_MODEL, D_MODEL), D_MODEL ** -0.5),
        'final_g': 1.0 + nrm(ks[25], (D_MODEL,), 0.02),
    }


def reference(x_prompt, x_sample, cache_na_k, cache_na_v, state_hgrn, cache_mla_ckv, cache_mla_krope,
              cache_wg_k, cache_wg_v, c, c_ctx, norm_g, w_mod, b_mod, w_in, na_rpb, hg_lb_logits, hg_norm_g,
              mla_q_norm_g, mla_kv_norm_g, mla_w_qb, mla_w_kvb, wg_sink, w_branch, w_out, final_g):
    W = {'norm_g': norm_g, 'w_mod': w_mod, 'b_mod': b_mod, 'w_in': w_in, 'na_rpb': na_rpb,
         'hg_norm_g': hg_norm_g, 'mla_q_norm_g': mla_q_norm_g, 'mla_kv_norm_g': mla_kv_norm_g,
         'mla_w_qb': mla_w_qb, 'mla_w_kvb': mla_w_kvb, 'wg_sink': wg_sink, 'w_branch': w_branch, 'w_out': w_out}
    p_lb = jax.nn.softmax(hg_lb_logits.astype(F32), axis=0)
    lbs = jnp.cumsum(p_lb, axis=0) - p_lb[0]

    Bp = x_prompt.shape[0]
    cvec = jnp.broadcast_to(c_ctx[None, :], (Bp, D_MODEL))
    h = x_prompt
    per_layer = []
    for l in range(DEPTH):
        h, st = context_layer(h, cvec, lbs[l], W, l)
        per_layer.append(st)
    y_prompt = rms_norm(h, final_g)
    new_na_k = jnp.stack([s[0] for s in per_layer], axis=1)
    new_na_v = jnp.stack([s[1] for s in per_layer], axis=1)
    new_state_hgrn = jnp.stack([s[2] for s in per_layer], axis=1)
    new_mla_ckv = jnp.stack([s[3] for s in per_layer], axis=1)
    new_mla_krope = jnp.stack([s[4] for s in per_layer], axis=1)
    new_wg_k = jnp.stack([s[5] for s in per_layer], axis=1)
    new_wg_v = jnp.stack([s[6] for s in per_layer], axis=1)

    T = x_sample.shape[1]
    rope_mla = axial_rope(T, MLA_ROPE)
    rope_wg = axial_rope(T, HEAD_DIM)
    h = x_sample
    for l in range(DEPTH):
        cache_l = (cache_na_k[:, l], cache_na_v[:, l], state_hgrn[:, l], cache_mla_ckv[:, l],
                   cache_mla_krope[:, l], cache_wg_k[:, l], cache_wg_v[:, l])
        h = latent_layer(h, c, cache_l, lbs[l], rope_mla, rope_wg, W, l)
    y_sample = rms_norm(h, final_g)
    return (y_prompt, y_sample, new_na_k, new_na_v, new_state_hgrn, new_mla_ckv, new_mla_krope, new_wg_k, new_wg_v)
```

```python
import contextlib
import math
import os
import numpy as np
import concourse.bass as bass
import concourse.mybir as mybir
from concourse.bass_utils import run_bass_kernel_spmd

F32 = mybir.dt.float32
BF16 = mybir.dt.bfloat16
AF = mybir.ActivationFunctionType
ALU = mybir.AluOpType

SEM_LIMIT = 30000
DEPTH = 4
EPS = 1e-6
NEG = -30000.0


class Tok:
    __slots__ = ("w", "rs", "excl")

    def __init__(self, w=None):
        self.w = w
        self.rs = []
        self.excl = False


class Op:
    __slots__ = ("eng", "fn", "deps", "inc", "sem", "val", "dma", "prev")

    def __init__(self, eng, fn, dma):
        self.eng = eng
        self.fn = fn
        self.deps = []
        self.inc = False
        self.sem = None
        self.val = 0
        self.dma = dma


class B:
    __slots__ = ("ap", "tok")

    def __init__(self, ap, tok=None):
        self.ap = ap
        self.tok = tok if tok is not None else Tok()

    def __getitem__(self, k):
        return B(self.ap[k], self.tok)

    def v(self, f):
        return B(f(self.ap), self.tok)


class Sched:
    ENGS = ("pe", "act", "dve", "pool", "sp")

    def __init__(self, nc, es):
        self.nc = nc
        self.es = es
        self.ops = {e: [] for e in self.ENGS}
        self.nsem = 0
        self.barrier_op = None

    def new_sem(self):
        s = self.es.enter_context(self.nc.semaphore("s%d" % self.nsem))
        self.nsem += 1
        return s

    def _add(self, eng, fn, r, w, dma):
        op = Op(eng, fn, dma)
        deps = []
        for t in r:
            if t.w is not None:
                deps.append(t.w)
            if t.excl:
                deps.extend(o for o in t.rs if o.eng != eng)
        for t in w:
            if t.w is not None:
                deps.append(t.w)
            deps.extend(t.rs)
        seen = set()
        for d in deps:
            if id(d) in seen:
                continue
            seen.add(id(d))
            if d.eng == "pe" and eng == "pe" and not d.dma and not dma:
                continue
            op.deps.append(d)
            d.inc = True
        for t in r:
            t.rs.append(op)
        for t in w:
            t.w = op
            t.rs = []
        self.ops[eng].append(op)
        return op

    def op(self, eng, fn, r=(), w=()):
        return self._add(eng, fn, [b.tok for b in r], [b.tok for b in w], False)

    def dma(self, eng, out, in_, r=(), w=()):
        oa = out.ap if isinstance(out, B) else out
        ia = in_.ap if isinstance(in_, B) else in_
        rr = [b.tok for b in r] + ([in_.tok] if isinstance(in_, B) else [])
        ww = [b.tok for b in w] + ([out.tok] if isinstance(out, B) else [])
        return self._add(eng, lambda e: e.dma_start(out=oa, in_=ia), rr, ww, True)

    def mm(self, out, lhsT, rhs, start=True, stop=True):
        return self.op("pe", lambda e: e.matmul(out.ap, lhsT=lhsT.ap, rhs=rhs.ap, start=start, stop=stop),
                       r=[lhsT, rhs], w=[out])

    def tr(self, out, in_, ident):
        return self.op("pe", lambda e: e.transpose(out.ap, in_.ap, ident.ap), r=[in_, ident], w=[out])

    def act(self, out, in_, func, scale=1.0, bias=None, accum=None, r=()):
        rr = [in_] + list(r)
        kw = {}
        if isinstance(scale, B):
            rr.append(scale)
            kw["scale"] = scale.ap
        else:
            kw["scale"] = scale
        if bias is not None:
            rr.append(bias)
            kw["bias"] = bias.ap
        ww = [out]
        if accum is not None:
            ww.append(accum)
            kw["accum_out"] = accum.ap
        return self.op("act", lambda e: e.activation(out=out.ap, in_=in_.ap, func=func, **kw), r=rr, w=ww)

    def tt(self, eng, out, in0, in1, op):
        return self.op(eng, lambda e: e.tensor_tensor(out=out.ap, in0=in0.ap, in1=in1.ap, op=op),
                       r=[in0, in1], w=[out])

    def ts(self, eng, out, in0, s1, s2=None, op0=ALU.mult, op1=None):
        rr = [in0]
        a1 = s1
        a2 = s2
        if isinstance(s1, B):
            rr.append(s1)
            a1 = s1.ap
        if isinstance(s2, B):
            rr.append(s2)
            a2 = s2.ap
        if op1 is None:
            return self.op(eng, lambda e: e.tensor_scalar(out=out.ap, in0=in0.ap, scalar1=a1, scalar2=None,
                                                          op0=op0), r=rr, w=[out])
        return self.op(eng, lambda e: e.tensor_scalar(out=out.ap, in0=in0.ap, scalar1=a1, scalar2=a2,
                                                      op0=op0, op1=op1), r=rr, w=[out])

    def stt(self, eng, out, in0, scalar, in1, op0, op1):
        rr = [in0, in1]
        a = scalar
        if isinstance(scalar, B):
            rr.append(scalar)
            a = scalar.ap
        return self.op(eng, lambda e: e.scalar_tensor_tensor(out=out.ap, in0=in0.ap, scalar=a, in1=in1.ap,
                                                             op0=op0, op1=op1), r=rr, w=[out])

    def copy(self, eng, out, in_):
        if eng == "act":
            return self.act(out, in_, AF.Identity)
        return self.op(eng, lambda e: e.tensor_copy(out=out.ap, in_=in_.ap), r=[in_], w=[out])

    def recip(self, out, in_):
        return self.op("dve", lambda e: e.reciprocal(out=out.ap, in_=in_.ap), r=[in_], w=[out])

    def memset(self, eng, out, val):
        return self.op(eng, lambda e: e.memset(out.ap, val), w=[out])

    def barrier(self, toks):
        bs = [B(None, t) for t in toks]
        scr = self.scratch
        op = self.op("pool", lambda e: e.memset(scr.ap, 0.0), r=bs, w=bs + [scr])
        return op

    def emit(self):
        nc = self.nc
        NRING = {"sp": 40, "pool": 40, "act": 16, "pe": 1, "dve": 1}
        for e in self.ENGS:
            csem = None
            ccnt = 0
            ring = []
            rcnt = []
            nd = 0
            for op in self.ops[e]:
                if op.dma:
                    op.inc = True
                    if len(ring) < NRING[e]:
                        ring.append(self.new_sem())
                        rcnt.append(0)
                    k = nd % NRING[e]
                    nd += 1
                    op.prev = (ring[k], rcnt[k])
                    rcnt[k] += 16
                    op.sem, op.val = ring[k], rcnt[k]
                elif op.inc:
                    if csem is None or ccnt + 1 > SEM_LIMIT:
                        csem = self.new_sem()
                        ccnt = 0
                    ccnt += 1
                    op.sem, op.val = csem, ccnt
        engobj = {"pe": "tensor", "act": "scalar", "dve": "vector", "pool": "gpsimd", "sp": "sync"}
        stats = {}
        with nc.Block() as block:
            for e in self.ENGS:
                ops = self.ops[e]

                def body(eng, ops=ops, e=e):
                    waited = {}
                    nw = 0
                    for op in ops:
                        need = {}
                        deps = [(d.sem, d.val) for d in op.deps]
                        if op.dma and op.prev[1] > 0:
                            deps.append(op.prev)
                        for (s_, v_) in deps:
                            k = id(s_)
                            if waited.get(k, 0) >= v_:
                                continue
                            if k not in need or need[k][1] < v_:
                                need[k] = (s_, v_)
                        for k, (s, v) in need.items():
                            eng.wait_ge(s, v)
                            waited[k] = v
                            nw += 1
                        ins = op.fn(eng)
                        if op.inc:
                            ins.then_inc(op.sem, 16 if op.dma else 1)
                    if e == "sp":
                        for ee in self.ENGS:
                            last = {}
                            for op in self.ops[ee]:
                                if op.dma:
                                    last[id(op.sem)] = (op.sem, op.val)
                            for k, (s, v) in last.items():
                                if waited.get(k, 0) < v:
                                    eng.wait_ge(s, v)
                    stats[e] = (len(ops), nw)

                getattr(block, engobj[e])(body)
        return stats


class Arena:
    def __init__(self, ap, nwords):
        self.ap = ap
        self.n = nwords
        self.top = 0
        self.peak = 0
        self.live = []
        self.dead = []
        self.last = None

    def release_to(self, top):
        keep = []
        for blk in self.live:
            (self.dead if blk[0] >= top else keep).append(blk)
        self.live = keep
        self.top = top

    def alloc(self, shape, dt, parts=128):
        nel = 1
        for s in shape:
            nel *= s
        words = (nel * (4 if dt == F32 else 2) + 3) // 4
        wal = (words + 7) // 8 * 8
        off = self.top
        self.top += wal
        self.peak = max(self.peak, self.top)
        assert self.top <= self.n, "arena overflow %d > %d" % (self.top, self.n)
        end = off + wal
        inherit = []
        nd = []
        for blk in self.dead:
            if blk[0] < end and off < blk[1]:
                for t in blk[2]:
                    if t.w is not None:
                        inherit.append(t.w)
                    inherit.extend(t.rs)
                if blk[0] >= off and blk[1] <= end:
                    continue
            nd.append(blk)
        self.dead = nd
        uniq = []
        seen = set()
        for o in inherit:
            if id(o) not in seen:
                seen.add(id(o))
                uniq.append(o)
        self.last = [off, end, [], uniq]
        self.live.append(self.last)
        v = self.ap[:, off:off + words]
        if dt != F32:
            v = v.bitcast(dt)
        if len(shape) == 2:
            v = v.rearrange("p (a b) -> p a b", a=shape[0])
        elif len(shape) == 3:
            v = v.rearrange("p (a b c) -> p a b c", a=shape[0], b=shape[1])
        elif len(shape) == 4:
            v = v.rearrange("p (a b c d) -> p a b c d", a=shape[0], b=shape[1], c=shape[2])
        if parts != 128:
            v = v[0:parts]
        return v

    def newtok(self):
        t = Tok()
        t.rs = list(self.last[3])
        self.last[2].append(t)
        return t


class Ring:
    def __init__(self, bufs):
        self.bufs = bufs
        self.i = 0

    def get(self):
        b = self.bufs[self.i % len(self.bufs)]
        self.i += 1
        return b


C_NA_Q, C_NA_K, C_NA_V, C_NA_Z = 0, 256, 512, 768
C_HG_Q, C_HG_FF, C_HG_FB, C_HG_I, C_HG_OG, C_HG_Z = 1024, 1280, 1536, 1792, 2048, 2304
C_ML_QA, C_ML_KVA, C_ML_Z = 2560, 2816, 2976
C_WG_Q, C_WG_K, C_WG_V, C_WG_Z = 3232, 3488, 3616, 3744
C_MERGE = 4000


class Builder:
    def __init__(self, run_ctx=True, run_lat=True, nlayers=DEPTH, dbg=None):
        self.dbg = dbg
        self.run_ctx = run_ctx
        self.run_lat = run_lat
        self.nlayers = nlayers

    def dram_in(self, name, shape):
        self.in_shapes[name] = shape
        return self.nc.dram_tensor(name, list(shape), F32, kind="ExternalInput").ap()

    def dram_out(self, name, shape):
        self.out_shapes[name] = shape
        return self.nc.dram_tensor(name, list(shape), F32, kind="ExternalOutput").ap()

    def build(self):
        nc = bass.Bass("TRN2", target_bir_lowering=False)
        self.nc = nc
        self.in_shapes = {}
        self.out_shapes = {}
        di, do = self.dram_in, self.dram_out
        I = {}
        for name, shape in [
            ("xT_c", (1024, 512)), ("xT_l", (1024, 2048)), ("cc", (1024, 2)),
            ("w_mod", (4, 1024, 3072)), ("bmodr", (128, 4, 24)), ("ng", (128, 4, 8)), ("fg", (128, 8)),
            ("w_in", (4, 1024, 8096)), ("w_mg", (4, 1024, 4096)),
            ("wbr", (4, 8, 128, 8, 128)), ("wout", (4, 8, 128, 8, 128)),
            ("lbl", (4, 512)), ("lblT", (128, 4, 4)), ("hgng", (128, 4)), ("qng", (128, 4, 2)), ("kvng", (4, 128)),
            ("wqb", (4, 256, 384)), ("wkvb", (4, 128, 512)), ("sink", (1, 16)), ("rpbp", (4, 60, 127)),
            ("cnakT", (4, 256, 512)), ("cnav", (4, 512, 256)), ("st0", (4, 128, 256)),
            ("cckvT", (4, 128, 512)), ("ckrT", (4, 32, 512)), ("cwgkT", (4, 128, 512)), ("cwgv", (4, 512, 128)),
            ("cmat", (128, 5, 128)), ("ind", (128, 4)), ("namask", (64, 64)), ("antiI", (64, 64)),
            ("wgmask", (128, 2, 128)), ("ropewg", (128, 2, 2048)), ("ropeml", (32, 2, 2048)),
        ]:
            I[name] = di(name, shape)
        O = {}
        for name, shape in [
            ("ypT", (1024, 512)), ("ysT", (1024, 2048)), ("o_nak", (4, 256, 512)), ("o_nav", (4, 512, 256)),
            ("o_st", (4, 128, 512)), ("o_kv", (4, 512, 160)), ("o_wgk", (4, 128, 512)), ("o_wgv", (4, 512, 128)),
        ]:
            O[name] = do(name, shape)
        self.I, self.O = I, O
        self.xscr = nc.dram_tensor("xscr", [1024, 2048], F32, kind="Internal").ap()

        with contextlib.ExitStack() as es:
            S = Sched(nc, es)
            self.S = S
            NW = 206 * 256
            arena_t = es.enter_context(nc.sbuf_tensor("arena", [128, NW], F32))
            self.A = Arena(arena_t[:, :], NW)
            self.ps = []
            for i in range(8):
                t = es.enter_context(nc.psum_tensor("ps%d" % i, [128, 512], F32))
                self.ps.append(B(t[:, :]))
                self.ps[-1].tok.excl = True
            self.psA = Ring(self.ps[0:2])
            self.psB = Ring(self.ps[2:4])
            self.psC = Ring(self.ps[4:6])
            self.psD = Ring(self.ps[6:8])
            self.psS = Ring(self.ps[0:4])
            self.psO = Ring(self.ps[4:8])
            self.psAll = Ring(self.ps)
            self.pendq = []
            self.setup_consts()
            base_top = self.A.top
            self.prologue_tables()
            self.prologue_mod()
            if self.run_ctx:
                self.A.release_to(base_top)
                self.run_pass(False)
            if self.run_lat:
                self.A.release_to(base_top)
                self.run_pass(True)
            self.stats = S.emit()
            self.stats["peak_words"] = self.A.peak
        return nc

    def nb(self, ap):
        return B(ap, self.A.newtok())

    def mark(self, name):
        if not hasattr(self, "marks"):
            self.marks = []
        self.marks.append((("lat " if getattr(self, "lat", False) else "ctx ") + name, len(self.S.ops["pe"])))

    def all_toks(self):
        return list(self.pass_toks) + [p.tok for p in self.ps]

    def alloc(self, shape, dt, parts=128):
        return self.nb(self.A.alloc(shape, dt, parts))

    def ring(self, n, shape, dt, parts=128):
        return Ring([self.alloc(shape, dt, parts) for _ in range(n)])

    def setup_consts(self):
        S, A, I = self.S, self.A, self.I
        S.scratch = B(A.alloc([8], F32))
        c = {}
        c["cmat"] = B(A.alloc([5, 128], F32))
        S.dma("sp", c["cmat"], I["cmat"])
        c["ident"] = c["cmat"][:, 0, :]
        c["ind"] = B(A.alloc([4], F32))
        S.dma("sp", c["ind"], I["ind"])
        c["onesD"] = B(A.alloc([128], F32))
        S.memset("dve", c["onesD"], 1.0 / 1024.0)
        c["ones256"] = B(A.alloc([128], F32))
        S.memset("dve", c["ones256"], 1.0 / 256.0)
        c["blk64"] = B(A.alloc([128], F32))
        S.memset("dve", c["blk64"], 0.0)
        S.memset("dve", c["blk64"][0:64, 0:64], 1.0 / 64.0)
        S.memset("dve", c["blk64"][64:128, 64:128], 1.0 / 64.0)
        c["eps"] = B(A.alloc([1], F32))
        S.memset("dve", c["eps"], EPS)
        c["onesb"] = B(A.alloc([128], BF16))
        S.memset("dve", c["onesb"], 1.0)
        c["onesrow"] = B(A.alloc([512], BF16))
        S.memset("pool", c["onesrow"], 1.0)
        c["identb"] = B(A.alloc([128], BF16))
        S.dma("pool", c["identb"], I["cmat"][:, 0, :])
        c["ng"] = B(A.alloc([4, 8], F32))
        S.dma("sp", c["ng"], I["ng"])
        c["fg"] = B(A.alloc([8], F32))
        S.dma("sp", c["fg"], I["fg"])
        c["bmodr"] = B(A.alloc([4, 24], F32))
        S.dma("sp", c["bmodr"], I["bmodr"])
        c["hgng"] = B(A.alloc([4], F32))
        S.dma("sp", c["hgng"], I["hgng"])
        c["qng"] = B(A.alloc([4, 2], F32))
        S.dma("sp", c["qng"], I["qng"])
        c["kvng"] = B(A.alloc([4, 128], F32))
        S.dma("sp", c["kvng"].v(lambda a: a.rearrange("p a b -> p (a b)")),
              I["kvng"].rearrange("l n -> (l n)").partition_broadcast(128))
        c["esink"] = B(A.alloc([16], F32))
        S.dma("sp", c["esink"], I["sink"].rearrange("a n -> (a n)").partition_broadcast(128))
        S.act(c["esink"], c["esink"], AF.Exp)
        c["lb_fm"] = B(A.alloc([4, 4], F32))
        c["oml_fm"] = B(A.alloc([4, 4], F32))
        c["noml_fm"] = B(A.alloc([4, 4], F32))
        self.Am = B(A.alloc([2, 4, 8], F32))
        self.Bm = B(A.alloc([2, 4, 8], F32))
        self.Gm = B(A.alloc([2, 4, 8], F32))
        self.c = c

    def lb_tables(self, n, src, lb, oml, layers):
        S = self.S
        raw = self.alloc([4, n], F32)
        S.dma("sp", raw.v(lambda a: a.rearrange("p a b -> p (a b)")), src)
        S.act(raw, raw, AF.Exp)
        tot = self.alloc([n], F32)
        S.tt("dve", tot, raw[:, 0, :], raw[:, 1, :], ALU.add)
        S.tt("dve", tot, tot, raw[:, 2, :], ALU.add)
        S.tt("dve", tot, tot, raw[:, 3, :], ALU.add)
        S.recip(tot, tot)
        acc = self.alloc([n], F32)
        S.memset("dve", acc, 0.0)
        pl = self.alloc([n], F32)
        for l in range(4):
            if l > 0:
                S.tt("dve", pl, raw[:, l, :], tot, ALU.mult)
                S.tt("dve", acc, acc, pl, ALU.add)
            if l in layers:
                S.copy("dve", lb(l), acc)
                S.ts("dve", oml(l), acc, -1.0, 1.0, ALU.mult, ALU.add)

    def prologue_tables(self):
        S, c, I = self.S, self.c, self.I
        self.lb_tables(4, I["lblT"].rearrange("p a b -> p (a b)"), lambda l: c["lb_fm"][:, l, :],
                       lambda l: c["oml_fm"][:, l, :], range(4))
        S.ts("dve", c["noml_fm"], c["oml_fm"], -1.0)

    def prologue_mod(self):
        S, A, I, c = self.S, self.A, self.I, self.c
        ccs = self.alloc([8, 2], F32)
        S.dma("sp", ccs, I["cc"].rearrange("(kc p) c -> p kc c", p=128))
        S.act(ccs, ccs, AF.Silu)
        mod = self.alloc([4, 24, 2], F32)
        wring = self.ring(2, [8, 768], F32)
        for l in range(DEPTH):
            for pc in range(4):
                wt = wring.get()
                S.dma("sp", wt, I["w_mod"][l][:, pc * 768:(pc + 1) * 768].rearrange("(kc p) n -> p kc n", p=128))
                ps = self.psA.get()
                for jj in range(6):
                    for kc in range(8):
                        S.mm(ps[:, jj * 2:(jj + 1) * 2], wt[:, kc, jj * 128:(jj + 1) * 128], ccs[:, kc, :],
                             start=(kc == 0), stop=(kc == 7))
                S.tt("dve", mod[:, l, pc * 6:(pc + 1) * 6, :],
                     ps[:, 0:12].v(lambda a: a.rearrange("p (j c) -> p j c", c=2)),
                     c["bmodr"][:, l, pc * 6:(pc + 1) * 6].v(lambda a: a.unsqueeze(2).broadcast_to([128, 6, 2])),
                     ALU.add)
        Am, Bm, Gm = self.Am, self.Bm, self.Gm
        for p in range(2):
            S.ts("dve", Am[:, p], mod[:, :, 8:16, p], 1.0, None, ALU.add)
            S.tt("dve", Am[:, p], Am[:, p], c["ng"], ALU.mult)
            S.copy("dve", Bm[:, p], mod[:, :, 0:8, p])
            S.copy("dve", Gm[:, p], mod[:, :, 16:24, p])

    def run_pass(self, lat):
        S, A, I, O, c = self.S, self.A, self.I, self.O, self.c
        TOK = 2048 if lat else 512
        NT = TOK // 512
        pidx = 1 if lat else 0
        self.lat, self.TOK, self.NT, self.pidx = lat, TOK, NT, pidx
        xsrc = I["xT_l"] if lat else I["xT_c"]
        self.xdram = lat
        if lat:
            self.xd = [[B(self.xscr[kc * 128:(kc + 1) * 128, t * 512:(t + 1) * 512]) for t in range(NT)]
                       for kc in range(8)]
            self.xin = [[B(xsrc[kc * 128:(kc + 1) * 128, t * 512:(t + 1) * 512]) for t in range(NT)]
                        for kc in range(8)]
            self.xcur = self.xin
        else:
            xs = A.alloc([8, TOK], F32)
            self.x = [[self.nb(xs[:, kc, t * 512:(t + 1) * 512]) for t in range(NT)] for kc in range(8)]
            for kc in range(8):
                for t in range(NT):
                    S.dma("sp", self.x[kc][t], xsrc[kc * 128:(kc + 1) * 128, t * 512:(t + 1) * 512])
        hs = A.alloc([8, TOK], BF16)
        self.h = [[self.nb(hs[:, kc, t * 512:(t + 1) * 512]) for t in range(NT)] for kc in range(8)]
        self.hall = hs
        self.y = []
        for b in range(4):
            ya = A.alloc([2, TOK], BF16)
            self.y.append([[self.nb(ya[:, fc, t * 512:(t + 1) * 512]) for t in range(NT)] for fc in range(2)])
        self.wpool = self.ring(3, [8, 512], BF16)
        self.f32pool = self.ring(4, [512], F32)
        self.rspool = self.ring(2, [512], F32)
        self.bfpool = self.ring(4, [512], BF16)
        self.A_l = lambda l: self.Am[:, pidx, l, :]
        self.B_l = lambda l: self.Bm[:, pidx, l, :]
        self.G_l = lambda l: self.Gm[:, pidx, l, :]
        ptop = A.top
        for l in range(self.nlayers):
            A.release_to(ptop)
            self.mark("L%d make_h" % l)
            self.make_h(l)
            on = (lambda k: True) if self.dbg is None else (lambda k: k in self.dbg.split(","))
            if on("na"):
                self.mark("L%d na" % l)
                self.branch_na(l)
            A.release_to(ptop)
            if on("hg"):
                self.mark("L%d hg" % l)
                self.branch_hg(l)
            A.release_to(ptop)
            if on("mla"):
                self.mark("L%d mla" % l)
                self.branch_mla(l)
            A.release_to(ptop)
            if on("wg"):
                self.mark("L%d wg" % l)
                self.branch_wg(l)
            A.release_to(ptop)
            if on("p3"):
                self.mark("L%d p3" % l)
                self.phase3(l)
                if self.xdram:
                    self.xcur = self.xd
        A.release_to(ptop)
        self.mark("final")
        self.final_norm(O["ysT"] if lat else O["ypT"])

    def rms_stats(self, chunks, ones, nparts=128):
        S, c = self.S, self.c
        n = chunks[0].ap.shape[-1]
        ps = self.psD.get()
        for i, ch in enumerate(chunks):
            sq = self.f32pool.get()
            S.act(sq[:, 0:n], ch, AF.Square)
            S.mm(ps[:, 0:n], ones, sq[:, 0:n], start=(i == 0), stop=(i == len(chunks) - 1))
        rs = self.rspool.get()
        S.act(rs[:, 0:n], ps[:, 0:n], AF.Sqrt, bias=c["eps"][:, 0:1])
        S.recip(rs[:, 0:n], rs[:, 0:n])
        return rs[:, 0:n]

    def x_tiles(self, t, ring):
        if not self.xdram:
            return [self.x[kc][t] for kc in range(8)]
        out = []
        for kc in range(8):
            xb = ring.get()
            self.S.dma("sp", xb, self.xcur[kc][t])
            out.append(xb)
        return out

    def make_h(self, l):
        S, c = self.S, self.c
        Al, Bl = self.A_l(l), self.B_l(l)
        mk = self.A.top
        ring = self.ring(8, [512], F32) if self.xdram else None
        for t in range(self.NT):
            xt = self.x_tiles(t, ring)
            rs = self.rms_stats(xt, c["onesD"])
            for kc in range(8):
                tmp = self.f32pool.get()
                S.tt("dve", tmp, xt[kc], rs, ALU.mult)
                S.act(self.h[kc][t], tmp, AF.Identity, scale=Al[:, kc:kc + 1], bias=Bl[:, kc:kc + 1])
        self.A.release_to(mk)

    def load_w(self, l, c0, n, src=None):
        wb = self.wpool.get()
        w = self.I["w_in"] if src is None else src
        self.S.dma("pool", wb[:, :, 0:n], w[l][:, c0:c0 + n].rearrange("(kc p) n -> p kc n", p=128))
        return wb

    def proj_fm(self, wb, co, ncols, t, ps=None, pcol=0, ntok=512, tok0=0):
        S = self.S
        if ps is None:
            ps = self.psAll.get()
        for kc in range(8):
            S.mm(ps[0:ncols, pcol:pcol + ntok], wb[:, kc, co:co + ncols], self.h[kc][t][:, tok0:tok0 + ntok],
                 start=(kc == 0), stop=(kc == 7))
        return ps

    def proj_tm(self, wb, co, ncols, t, sub, ps=None, pcol=0, shift=0):
        S = self.S
        if ps is None:
            ps = self.psAll.get()
        a = sub * 128 + shift
        for kc in range(8):
            S.mm(ps[:, pcol:pcol + ncols], self.h[kc][t][:, a:a + 128], wb[:, kc, co:co + ncols],
                 start=(kc == 0), stop=(kc == 7))
        return ps

    def attend(self, sc_list, v_list, nq, scale, fin, tail=None):
        S, c = self.S, self.c
        g = 512 // nq
        nk = len(sc_list)
        O = self.psO.get()
        Dn = None
        for k0 in range(0, nk, g):
            k1 = min(nk, k0 + g)
            sp = self.psS.get()
            for k in range(k0, k1):
                terms = sc_list[k]
                for i, (lt, rh) in enumerate(terms):
                    S.mm(sp[:, (k - k0) * nq:(k - k0 + 1) * nq], lt, rh, start=(i == 0), stop=(i == len(terms) - 1))

            def stage2(k0=k0, k1=k1, sp=sp):
                w = (k1 - k0) * nq
                pt = self.bfpool.get()
                S.act(pt[:, 0:w], sp[:, 0:w], AF.Exp, scale=scale)
                for k in range(k0, k1):
                    blk = pt[:, (k - k0) * nq:(k - k0 + 1) * nq]
                    S.mm(O[:, 0:nq], v_list[k], blk, start=(k == 0), stop=(k == nk - 1 and tail is None))
                if k1 == nk and tail is not None:
                    S.mm(O[:, 0:nq], tail[0], tail[1], start=False, stop=True)
                if k1 == nk:
                    fin(O, Dn)

            self.pendq.append(stage2)
            while len(self.pendq) > self.PIPE_DEPTH:
                self.pendq.pop(0)()

    PIPE_DEPTH = 3

    def flush(self):
        while self.pendq:
            self.pendq.pop(0)()

    def finish_head(self, O, Dn, parity, nq, ydst, sz, extra=None, add=None):
        S = self.S
        r0 = parity * 64
        o0 = 64 - r0
        if add is not None:
            tot = self.f32pool.get()
            S.tt("dve", tot[:, 0:nq], O[:, 0:nq], add, ALU.add)
            O = tot
        rec = self.f32pool.get()
        if extra is not None:
            S.copy("dve", rec[r0:r0 + 64, 0:nq], O[o0:o0 + 64, 0:nq])
            S.ts("dve", rec[r0:r0 + 64, 0:nq], rec[r0:r0 + 64, 0:nq], extra[r0:r0 + 64], None, ALU.add)
            S.recip(rec[r0:r0 + 64, 0:nq], rec[r0:r0 + 64, 0:nq])
        else:
            S.recip(rec[r0:r0 + 64, 0:nq], O[o0:o0 + 64, 0:nq])
        S.tt("dve", rec[r0:r0 + 64, 0:nq], rec[r0:r0 + 64, 0:nq], sz[r0:r0 + 64], ALU.mult)
        S.tt("dve", ydst[r0:r0 + 64], O[r0:r0 + 64, 0:nq], rec[r0:r0 + 64, 0:nq], ALU.mult)

    def alloc_vaug(self, n):
        a = self.A.alloc([n, 2, 2, 128], BF16)
        tiles = [self.nb(a[:, i]) for i in range(n)]
        for t in tiles:
            self.S.memset("dve", t, 1.0)
        return tiles

    def fill_vaug(self, eng, vt, src):
        sv = src.v(lambda a: a.rearrange("p (f q e) -> p f q e", f=2, q=2))
        for par in range(2):
            self.S.copy(eng, vt[:, :, par, par * 64:(par + 1) * 64], sv[:, :, par, :])

    def fm_group(self, wb, co, nch, dst, func=AF.Identity, scale=1.0, f32dst=None):
        S = self.S
        for t in range(self.NT):
            for fc in range(nch):
                ps = self.proj_fm(wb, co + fc * 128, 128, t)
                S.act(dst[fc][t], ps, func, scale=scale)
                if f32dst is not None:
                    S.copy("dve", f32dst[fc][t], ps)

    def tile2(self, shape_cols, dt, nch, parts=128, shared=False):
        a = self.A.alloc([nch, self.TOK], dt)
        if not shared:
            return [[self.nb(a[0:parts, fc, t * 512:(t + 1) * 512]) for t in range(self.NT)] for fc in range(nch)]
        full = [self.nb(a[0:parts, fc, :]) for fc in range(nch)]
        self.last_full = full
        return [[full[fc][:, t * 512:(t + 1) * 512] for t in range(self.NT)] for fc in range(nch)]

    def branch_na(self, l):
        S, A, I, O, c = self.S, self.A, self.I, self.O, self.c
        lat, NT, TOK = self.lat, self.NT, self.TOK
        y = self.y[0]
        qT = self.tile2(512, BF16, 2)
        kT = self.tile2(512, BF16, 2, shared=lat)
        kTf = self.last_full if lat else None
        wb = self.load_w(l, C_NA_Q, 512)
        self.fm_group(wb, 0, 2, qT, scale=(0.125 if lat else 1.0))
        if lat:
            self.fm_group(wb, 256, 2, kT)
        else:
            stg = self.ring(2, [512], F32)
            for t in range(NT):
                for fc in range(2):
                    ps = self.proj_fm(wb, 256 + fc * 128, 128, t)
                    st = stg.get()
                    S.copy("dve", st, ps)
                    S.copy("dve", kT[fc][t], st)
                    S.dma("sp", O["o_nak"][l, fc * 128:(fc + 1) * 128, :], st)
        if os.environ.get("NA_STOP") == "1":
            return
        wb = self.load_w(l, C_NA_V, 512)
        self.fm_group(wb, 256, 2, y, func=AF.Silu)
        ntile = TOK // 128
        V = self.alloc_vaug(ntile)
        if not lat:
            stg = self.ring(2, [256], F32)
        for i in range(ntile):
            ps = self.proj_tm(wb, 0, 256, i // 4, i % 4)
            self.fill_vaug("act", V[i], ps[:, 0:256])
            if not lat:
                st = stg.get()
                S.copy("dve", st, ps[:, 0:256])
                S.dma("sp", O["o_nav"][l, i * 128:(i + 1) * 128, :], st)
        if os.environ.get("NA_STOP") == "2":
            return
        if not lat:
            for s in range(2):
                for hh in range(4):
                    fc, par = hh // 2, hh % 2
                    r = slice(par * 64, par * 64 + 64)
                    q = qT[fc][0][r, s * 256:(s + 1) * 256]
                    sc = [[(kT[fc][0][r, s * 256 + kt * 128: s * 256 + (kt + 1) * 128], q)] for kt in range(2)]
                    vl = [V[s * 2 + kt][:, fc, par, :] for kt in range(2)]
                    yd = y[fc][0][:, s * 256:(s + 1) * 256]
                    self.attend(sc, vl, 256, 0.125,
                                fin=lambda O_, D_, par=par, yd=yd, hh=hh: self.finish_head(O_, D_, par, 256, yd, yd))
            self.flush()
            return
        Vodd = self.alloc_vaug(ntile - 1)
        for i in range(ntile - 1):
            tok0 = i * 128 + 64
            t, off = tok0 // 512, tok0 % 512
            if off + 128 <= 512:
                ps = self.psA.get()
                for kc in range(8):
                    S.mm(ps[:, 0:256], self.h[kc][t][:, off:off + 128], wb[:, kc, 0:256], start=(kc == 0), stop=(kc == 7))
                self.fill_vaug("act", Vodd[i], ps[:, 0:256])
            else:
                ps = self.psA.get()
                for kc in range(8):
                    S.mm(ps[0:64, 0:256], self.h[kc][t][:, 448:512], wb[:, kc, 0:256], start=(kc == 0), stop=(kc == 7))
                ps2 = self.psA.get()
                for kc in range(8):
                    S.mm(ps2[0:64, 0:256], self.h[kc][t + 1][:, 0:64], wb[:, kc, 0:256], start=(kc == 0), stop=(kc == 7))
                self.fill_vaug("act", Vodd[i][0:64], ps[0:64, 0:256])
                self.fill_vaug("act", Vodd[i][64:128], ps2[0:64, 0:256])
        if os.environ.get("NA_STOP") == "3":
            return
        kc_a = A.alloc([2, 512], BF16)
        KcT = [self.nb(kc_a[:, fc, :]) for fc in range(2)]
        for fc in range(2):
            S.dma("pool", KcT[fc], I["cnakT"][l, fc * 128:(fc + 1) * 128, :])
        Vc = self.alloc_vaug(4)
        vtmp = self.ring(2, [256], BF16)
        for i in range(4):
            vt_ = vtmp.get()
            S.dma("pool", vt_, I["cnav"][l, i * 128:(i + 1) * 128, :])
            self.fill_vaug("dve", Vc[i], vt_)
        if os.environ.get("NA_STOP") == "4":
            return
        msk = self.nb(A.alloc([64], F32))
        tab = self.nb(A.alloc([60, 64], BF16))
        anti = self.nb(A.alloc([64], BF16))
        for hf in range(2):
            S.dma("sp", msk[hf * 64:(hf + 1) * 64], I["namask"])
            S.dma("pool", anti[hf * 64:(hf + 1) * 64], I["antiI"])
        tabf_r = self.ring(2, [15, 64], F32)
        for hh in range(4):
            tabf = tabf_r.get()
            src = bass.AP(tensor=I["rpbp"].tensor, offset=I["rpbp"][l, hh * 15, 0].offset,
                          ap=[[1, 64], [127, 15], [1, 64]])
            for hf in range(2):
                S.dma("sp", tabf[hf * 64:(hf + 1) * 64], src)
            S.tt("dve", tab[:, hh * 15:(hh + 1) * 15, :], tabf,
                 msk.v(lambda a: a.unsqueeze(1).broadcast_to([128, 15, 64])), ALU.add)
        if os.environ.get("NA_STOP") == "5":
            return
        octx_r = self.ring(8, [512], BF16)
        for t in range(NT):
            oc = []
            for hh in range(4):
                fc, par = hh // 2, hh % 2
                rr = slice(par * 64, par * 64 + 64)
                ob = octx_r.get()
                oc.append(ob)
                q = qT[fc][t][rr, :]
                sc = [[(KcT[fc][rr, ct * 128:(ct + 1) * 128], q)] for ct in range(4)]
                vl = [Vc[ct][:, fc, par, :] for ct in range(4)]
                self.attend(sc, vl, 512, 1.0, fin=lambda O_, D_, ob=ob: S.copy("act", ob, O_[:, 0:512]))
            for r_ in range(t * 8, t * 8 + 8):
                ws = min(max(r_ - 4, 0), 24)
                qo = (r_ * 64) % 512
                for hh in range(4):
                    fc, par = hh // 2, hh % 2
                    rr = slice(par * 64, par * 64 + 64)
                    q = qT[fc][t][rr, qo:qo + 64]
                    sc, vl = [], []
                    for kt in range(4):
                        row0 = ws + 2 * kt
                        tk0 = row0 * 64
                        dr = row0 - r_ + 7
                        sc.append([(kTf[fc][rr, tk0:tk0 + 128], q),
                                   (tab[rr, hh * 15 + dr: hh * 15 + dr + 2, :].v(lambda a: a.rearrange("p a b -> p (a b)")), anti[rr])])
                        vt = V[row0 // 2] if row0 % 2 == 0 else Vodd[row0 // 2]
                        vl.append(vt[:, fc, par, :])
                    yd = y[fc][t][:, qo:qo + 64]
                    self.attend(sc, vl, 64, 1.0,
                                fin=lambda O_, D_, par=par, yd=yd: self.finish_head(O_, D_, par, 64, yd, yd),
                                tail=(c["identb"], oc[hh][:, qo:qo + 64]))
        self.flush()

    def rope_fm(self, dst, ps_x, ps_r, cs, sn):
        S = self.S
        t1 = self.f32pool.get()
        t2 = self.f32pool.get()
        n = ps_x.ap.shape[-1]
        p = ps_x.ap.shape[0]
        S.tt("dve", t1[0:p, 0:n], ps_x, cs, ALU.mult)
        S.tt("dve", t2[0:p, 0:n], ps_r, sn, ALU.mult)
        S.tt("dve", dst, t1[0:p, 0:n], t2[0:p, 0:n], ALU.add)

    def make_rot(self, wb, co, n, half):
        S = self.S
        wr = self.alloc([8, n], BF16)
        src = wb[:, :, co:co + n].v(lambda a: a.rearrange("p kc (g two h) -> p kc g two h", two=2, h=half))
        dst = wr.v(lambda a: a.rearrange("p kc (g two h) -> p kc g two h", two=2, h=half))
        S.copy("dve", dst[:, :, :, 0, :], src[:, :, :, 1, :])
        S.copy("act", dst[:, :, :, 1, :], src[:, :, :, 0, :])
        return wr

    def branch_wg(self, l):
        S, A, I, O, c = self.S, self.A, self.I, self.O, self.c
        lat, NT, TOK = self.lat, self.NT, self.TOK
        y = self.y[3]
        qT = self.tile2(512, BF16, 2)
        nkt = (TOK + 512) // 512 if lat else NT
        ka = A.alloc([TOK + (512 if lat else 0)], BF16)
        kT = [self.nb(ka[:, t * 512:(t + 1) * 512]) for t in range(nkt)]
        wb6 = self.load_w(l, C_ML_Z, 512)
        wb7 = self.load_w(l, C_WG_K, 512)
        if lat:
            wr6 = self.make_rot(wb6, 256, 256, 32)
            wr7 = self.make_rot(wb7, 0, 128, 32)
            rp = self.ring(2, [2, 512], F32)
            for t in range(NT):
                rt = rp.get()
                S.dma("sp", rt, I["ropewg"][:, :, t * 512:(t + 1) * 512])
                for fc in range(2):
                    ps = self.proj_fm(wb6, 256 + fc * 128, 128, t)
                    ps2 = self.proj_fm(wr6, fc * 128, 128, t)
                    self.rope_fm(qT[fc][t], ps, ps2, rt[:, 0, :], rt[:, 1, :])
                ps = self.proj_fm(wb7, 0, 128, t)
                ps2 = self.proj_fm(wr7, 0, 128, t)
                self.rope_fm(kT[t], ps, ps2, rt[:, 0, :], rt[:, 1, :])
            S.dma("pool", kT[NT], I["cwgkT"][l])
        else:
            self.fm_group(wb6, 256, 2, qT)
            stg = self.ring(2, [512], F32)
            for t in range(NT):
                ps = self.proj_fm(wb7, 0, 128, t)
                S.act(kT[t], ps, AF.Identity)
                st = stg.get()
                S.copy("dve", st, ps)
                S.dma("sp", O["o_wgk"][l], st)
        self.fm_group(wb7, 256, 2, y, func=AF.Silu)
        ntile = TOK // 128
        nvt = ntile + (4 if lat else 0)
        va = A.alloc([nvt, 128], BF16)
        V = [self.nb(va[:, i, :]) for i in range(nvt)]
        if not lat:
            stg2 = self.ring(2, [128], F32)
        for i in range(ntile):
            ps = self.proj_tm(wb7, 128, 128, i // 4, i % 4)
            S.copy("act", V[i], ps[:, 0:128])
            if not lat:
                st = stg2.get()
                S.copy("dve", st, ps[:, 0:128])
                S.dma("sp", O["o_wgv"][l, i * 128:(i + 1) * 128, :], st)
        if lat:
            for i in range(4):
                S.dma("pool", V[ntile + i], I["cwgv"][l, i * 128:(i + 1) * 128, :])
            wm = self.nb(A.alloc([2, 128], BF16))
            S.dma("pool", wm, I["wgmask"])
        v2a = A.alloc([nvt, 2, 2, 128], BF16)
        V2 = [self.nb(v2a[:, i]) for i in range(nvt)]
        for i in range(nvt):
            S.memset("dve", V2[i], 1.0)
            for kvh in range(2):
                for par in range(2):
                    S.copy("dve", V2[i][:, kvh, par, par * 64:(par + 1) * 64], V[i][:, kvh * 64:(kvh + 1) * 64])
        esink = c["esink"]
        qalt_a = A.alloc([TOK], BF16)
        qalt = [self.nb(qalt_a[:, t * 512:(t + 1) * 512]) for t in range(NT)]
        for t in range(NT):
            S.copy("act", qalt[t][0:64], qT[0][t][64:128])
            S.copy("dve", qalt[t][64:128], qT[1][t][0:64])
        self.wg_qalt = qalt
        srow = self.nb(A.alloc([4, 128], BF16, parts=1))
        S.memset("dve", srow, 0.0)
        for hh in range(4):
            dh = 64 - (hh % 2) * 64
            S.copy("dve", srow[:, hh, dh:dh + 64],
                   esink[0:1, l * 4 + hh:l * 4 + hh + 1].v(lambda a: a.broadcast_to([1, 64])))
        if not lat:
            for s in range(2):
                for hh in range(4):
                    fc, par, kvh = hh // 2, hh % 2, hh // 2
                    r = slice(par * 64, par * 64 + 64)
                    kr = slice(kvh * 64, kvh * 64 + 64)
                    q = self.wgq_rows(qT, hh, 0, s * 256, 256)
                    sc = [[(kT[0][kr, s * 256 + kt * 128: s * 256 + (kt + 1) * 128], q)] for kt in range(2)]
                    vl = [V2[s * 2 + kt][:, kvh, par, :] for kt in range(2)]
                    yd = y[fc][0][:, s * 256:(s + 1) * 256]
                    self.attend(sc, vl, 256, 0.125,
                                fin=lambda O_, D_, par=par, yd=yd, hh=hh: self.finish_head(O_, D_, par, 256, yd, yd),
                                tail=(srow[:, hh, :], c["onesrow"][0:1, 0:256]))
            self.flush()
            return
        for n in range(TOK // 128):
            t, qo = n // 4, (n % 4) * 128
            for hh in range(4):
                fc, par, kvh = hh // 2, hh % 2, hh // 2
                kr = slice(kvh * 64, kvh * 64 + 64)
                q = self.wgq_rows(qT, hh, t, qo, 128)
                sc, vl = [], []
                for dn in (-1, 0, 1):
                    kb = n + dn
                    if kb < 0 or kb >= TOK // 128:
                        continue
                    terms = [(kT[kb // 4][kr, (kb % 4) * 128:(kb % 4 + 1) * 128], q)]
                    if dn != 0:
                        terms.append((wm[:, 0 if dn < 0 else 1, :], c["identb"]))
                    sc.append(terms)
                    vl.append(V2[kb][:, kvh, par, :])
                for ct in range(4):
                    sc.append([(kT[NT][kr, ct * 128:(ct + 1) * 128], q)])
                    vl.append(V2[ntile + ct][:, kvh, par, :])
                yd = y[fc][t][:, qo:qo + 128]
                self.attend(sc, vl, 128, 0.125,
                            fin=lambda O_, D_, par=par, yd=yd, hh=hh: self.finish_head(O_, D_, par, 128, yd, yd),
                            tail=(srow[:, hh, :], c["onesrow"][0:1, 0:128]))
        self.flush()

    def wgq_rows(self, qT, hh, t, o, n):
        if hh == 0:
            return qT[0][t][0:64, o:o + n]
        if hh == 1:
            return self.wg_qalt[t][0:64, o:o + n]
        if hh == 2:
            return self.wg_qalt[t][64:128, o:o + n]
        return qT[1][t][64:128, o:o + n]

    def branch_mla(self, l):
        S, A, I, O, c = self.S, self.A, self.I, self.O, self.c
        lat, NT, TOK = self.lat, self.NT, self.TOK
        y = self.y[2]
        ntile = TOK // 128
        nkt = NT + (1 if lat else 0)
        wb5 = self.load_w(l, C_ML_QA, 416)
        wb6 = self.load_w(l, C_ML_Z, 256)
        wqn = self.nb(A.alloc([2, 4, 64], BF16))
        wqr = self.nb(A.alloc([2, 4, 32], BF16))
        wq4 = I["wqb"][l].rearrange("(kc p) (h e) -> p kc h e", p=128, e=96)
        for kc in range(2):
            S.dma("pool", wqn[:, kc], wq4[:, kc, :, 0:64])
            S.dma("pool", wqr[:, kc], wq4[:, kc, :, 64:96])
        wkn = self.nb(A.alloc([4, 64], BF16))
        wkv = self.nb(A.alloc([4, 64], BF16))
        wk4 = I["wkvb"][l].rearrange("p (h e) -> p h e", e=128)
        S.dma("pool", wkn, wk4[:, :, 0:64])
        S.dma("pool", wkv, wk4[:, :, 64:128])
        if lat:
            wqrr = self.nb(A.alloc([2, 4, 32], BF16))
            S.copy("dve", wqrr[:, :, :, 0:16], wqr[:, :, :, 16:32])
            S.copy("dve", wqrr[:, :, :, 16:32], wqr[:, :, :, 0:16])
            wr5 = self.make_rot(wb5, 384, 32, 16)
        qha = A.alloc([4, TOK], BF16)
        qh = [[self.nb(qha[0:96, hh, t * 512:(t + 1) * 512]) for t in range(NT)] for hh in range(4)]
        kha = A.alloc([4, nkt * 512], BF16)
        kh = [[self.nb(kha[0:96, hh, t * 512:(t + 1) * 512]) for t in range(nkt)] for hh in range(4)]
        kr_r = self.ring(2, [512], BF16, parts=32)
        nvt = ntile + (4 if lat else 0)
        V = self.alloc_vaug(nvt)
        ckvT = self.ring(2, [512], BF16)
        qan_r = self.ring(3, [512], BF16)
        qaf_r = self.ring(3, [512], F32)
        if lat:
            rp = self.ring(2, [2, 512], F32)
        stg = self.ring(2, [160], F32)
        gq = c["qng"]
        for t in range(NT):
            if lat:
                rt = rp.get()
                S.dma("sp", rt[0:32], I["ropeml"][:, :, t * 512:(t + 1) * 512])
            qaf = []
            for kc in range(2):
                ps = self.proj_fm(wb5, kc * 128, 128, t)
                f = qaf_r.get()
                S.copy("act", f, ps)
                qaf.append(f)
            rs = self.rms_stats(qaf, c["ones256"])
            qan = []
            for kc in range(2):
                qb = qan_r.get()
                S.stt("dve", qb, qaf[kc], gq[:, l, kc:kc + 1], rs, ALU.mult, ALU.mult)
                qan.append(qb)
            for fc in range(2):
                ps = self.psA.get()
                for kc in range(2):
                    S.mm(ps, wqn[:, kc, 2 * fc:2 * fc + 2, :].v(lambda a: a.rearrange("p a b -> p (a b)")), qan[kc],
                         start=(kc == 0), stop=(kc == 1))
                S.copy("act", qh[2 * fc][t][0:64], ps[0:64])
                S.copy("dve", qh[2 * fc + 1][t][0:64], ps[64:128])
            for hh in range(4):
                ps = self.psA.get()
                for kc in range(2):
                    S.mm(ps[0:32], wqr[:, kc, hh, :], qan[kc], start=(kc == 0), stop=(kc == 1))
                if lat:
                    ps2 = self.psA.get()
                    for kc in range(2):
                        S.mm(ps2[0:32], wqrr[:, kc, hh, :], qan[kc], start=(kc == 0), stop=(kc == 1))
                    tq = kr_r.get()
                    self.rope_fm(tq, ps[0:32], ps2[0:32], rt[0:32, 0, :], rt[0:32, 1, :])
                    S.copy("act", qh[hh][t][64:96], tq)
                else:
                    S.copy("act", qh[hh][t][64:96], ps[0:32])
            ck = ckvT.get()
            krs = kr_r.get()
            for sub in range(4):
                i = t * 4 + sub
                ps = self.proj_tm(wb5, 256, 160, t, sub)
                ssq = self.f32pool.get()
                junk = self.f32pool.get()
                S.memset("dve", ssq[:, 0:1], 0.0)
                S.act(junk[:, 0:128], ps[:, 0:128], AF.Square, accum=ssq[:, 0:1])
                S.act(ssq[:, 1:2], ssq[:, 0:1], AF.Sqrt, scale=1.0 / 128.0, bias=c["eps"][:, 0:1])
                S.recip(ssq[:, 2:3], ssq[:, 1:2])
                st = stg.get()
                S.stt("dve", st[:, 0:128], ps[:, 0:128], ssq[:, 2:3], c["kvng"][:, l, :], ALU.mult, ALU.mult)
                S.copy("dve", st[:, 128:160], ps[:, 128:160])
                if not lat:
                    S.dma("sp", O["o_kv"][l, i * 128:(i + 1) * 128, :], st)
                pt = self.psA.get()
                S.tr(pt[:, 0:128], st[:, 0:128], c["ident"])
                S.copy("act", ck[:, sub * 128:(sub + 1) * 128], pt[:, 0:128])
                if not lat:
                    pt2 = self.psA.get()
                    S.tr(pt2[0:32, 0:128], st[:, 128:160], c["ident"])
                    S.copy("act", krs[:, sub * 128:(sub + 1) * 128], pt2[0:32, 0:128])
            if lat:
                ps = self.proj_fm(wb5, 384, 32, t)
                ps2 = self.proj_fm(wr5, 0, 32, t)
                self.rope_fm(krs, ps[0:32], ps2[0:32], rt[0:32, 0, :], rt[0:32, 1, :])
            for hh in range(4):
                S.copy("act" if hh % 2 else "dve", kh[hh][t][64:96], krs)
            self.mla_expand(ck, t, kh, V, wkn, wkv)
        if lat:
            ck = ckvT.get()
            S.dma("pool", ck, I["cckvT"][l])
            for hh in range(4):
                S.dma("pool", kh[hh][NT][64:96], I["ckrT"][l])
            self.mla_expand(ck, NT, kh, V, wkn, wkv)
        self.fm_group(wb6, 0, 2, y, func=AF.Silu)
        scale = 96.0 ** -0.5
        if not lat:
            for s in range(2):
                for hh in range(4):
                    fc, par = hh // 2, hh % 2
                    r = slice(par * 64, par * 64 + 64)
                    q1 = qh[hh][0][:, s * 256:(s + 1) * 256]
                    sc = [[(kh[hh][0][:, s * 256 + kt * 128: s * 256 + (kt + 1) * 128], q1)] for kt in range(2)]
                    vl = [V[s * 2 + kt][:, fc, par, :] for kt in range(2)]
                    yd = y[fc][0][:, s * 256:(s + 1) * 256]
                    self.attend(sc, vl, 256, scale,
                                fin=lambda O_, D_, par=par, yd=yd, hh=hh: self.finish_head(O_, D_, par, 256, yd, yd))
            self.flush()
            return
        for t in range(NT):
            for hh in range(4):
                fc, par = hh // 2, hh % 2
                r = slice(par * 64, par * 64 + 64)
                q1 = qh[hh][t]
                sc, vl = [], []
                for kt in range(nkt * 4):
                    k5, ko = kt // 4, (kt % 4) * 128
                    sc.append([(kh[hh][k5][:, ko:ko + 128], q1)])
                    vl.append(V[kt][:, fc, par, :])
                self.attend(sc, vl, 512, scale,
                            fin=lambda O_, D_, par=par, yd=y[fc][t]: self.finish_head(O_, D_, par, 512, yd, yd))
        self.flush()

    def mla_expand(self, ck, t, kh, V, wkn, wkv):
        S = self.S
        for fc in range(2):
            ps = self.psA.get()
            S.mm(ps, wkn[:, 2 * fc:2 * fc + 2, :].v(lambda a: a.rearrange("p a b -> p (a b)")), ck)
            S.copy("act", kh[2 * fc][t][0:64], ps[0:64])
            S.copy("dve", kh[2 * fc + 1][t][0:64], ps[64:128])
        for sub in range(4):
            ps = self.psA.get()
            S.mm(ps[:, 0:256], ck[:, sub * 128:(sub + 1) * 128], wkv.v(lambda a: a.rearrange("p a b -> p (a b)")))
            self.fill_vaug("dve", V[t * 4 + sub], ps[:, 0:256])

    def branch_hg(self, l):
        S, A, I, O, c = self.S, self.A, self.I, self.O, self.c
        lat, NT, TOK = self.lat, self.NT, self.TOK
        y = self.y[1]
        ntile = TOK // 128
        nseq = 1 if lat else 2
        tps = ntile // nseq
        wb2 = self.load_w(l, C_HG_Q, 512)
        wb3 = self.load_w(l, C_HG_FB, 512)
        wb4 = self.load_w(l, C_HG_OG, 512)
        qs = self.tile2(512, BF16, 2)
        self.fm_group(wb2, 0, 2, qs, func=AF.Silu)
        va = A.alloc([ntile, 256], BF16)
        V = [self.nb(va[:, i, :]) for i in range(ntile)]
        for i in range(ntile):
            ps = self.proj_tm(wb3, 256, 256, i // 4, i % 4)
            S.copy("act", V[i], ps[:, 0:256])
        oacc_a = A.alloc([2, TOK], F32)
        oacc = [[self.nb(oacc_a[:, fc, i * 128:(i + 1) * 128]) for i in range(ntile)] for fc in range(2)]
        st0 = [self.alloc([2, 64], F32) for _ in range(2)]
        r_sst = self.ring(8 * (1 if lat else 2) + 4, [2, 64], F32)
        lbt1 = self.alloc([512], F32)
        omlt1 = self.alloc([512], F32)
        mk = A.top
        self.lb_tables(512, I["lbl"].rearrange("l n -> (l n)").partition_broadcast(128),
                       lambda l_: lbt1, lambda l_: omlt1, [l])
        A.release_to(mk)
        omlf, nomlf = c["oml_fm"], c["noml_fm"]
        r_sg = self.ring(2, [256], F32)
        r_t = self.ring(2, [256], F32)
        r_k = self.ring(3, [256], F32)
        r_lf = self.ring(3, [256], F32)
        r_eq = self.ring(3, [2, 128], F32)
        r_ek = self.ring(2, [2, 128], F32)
        r_ee = self.ring(2, [256], F32)
        r_kf = self.ring(3, [2, 128], F32)
        r_qd = self.ring(4, [2, 128], BF16)
        r_kd = self.ring(4, [2, 128], BF16)
        r_kh = self.ring(3, [256], BF16)
        r_vb = self.ring(3, [4, 4, 64], BF16)
        r_us = self.ring(2, [2, 4, 64], F32)
        r_sb = self.ring(3, [2, 4, 128], BF16)
        for sbz in r_sb.bufs:
            S.memset("dve", sbz, 0.0)
        r_at = self.ring(6, [128], BF16)
        hg_pending = [None]
        oacc_init = set()
        states = {}
        for s in range(nseq):
            for d in range(2):
                if lat:
                    S.dma("sp", st0[d], I["st0"][l][:, d * 128:(d + 1) * 128].rearrange("p (a b) -> p a b", a=2))
                    states[(s, d)] = st0[d]
                else:
                    z0 = self.alloc([2, 64], F32)
                    S.memset("dve", z0, 0.0)
                    states[(s, d)] = z0
        q2, q3, q4 = [], [], []

        def run_queues(drain=False):
            lim = 0 if drain else 1
            while len(q2) > lim:
                q2.pop(0)()
            while len(q3) > lim:
                q3.pop(0)()
            while len(q4) > lim:
                q4.pop(0)()

        for step in range(tps):
            for s in range(nseq):
                for d in range(2):
                    wbd, cod = (wb2, 256) if d == 0 else (wb3, 0)
                    ti = step if d == 0 else tps - 1 - step
                    i = s * tps + ti
                    t5, sub = i // 4, i % 4
                    Um = c["cmat"][:, 1 + d, :]
                    Lm = c["cmat"][:, 3 + d, :]
                    ps = self.proj_tm(wbd, cod, 256, t5, sub)
                    sg = r_sg.get()
                    S.act(sg, ps[:, 0:256], AF.Sigmoid)
                    kf = r_kf.get()
                    for fc in range(2):
                        pk = self.proj_fm(wbd, cod + fc * 128, 128, t5, ntok=128, tok0=sub * 128)
                        sgf = self.f32pool.get()
                        S.act(sgf[:, 0:128], pk[:, 0:128], AF.Sigmoid)
                        S.ts("dve", kf[:, fc, :], sgf[:, 0:128], nomlf[:, l, d * 2 + fc:d * 2 + fc + 1],
                             omlf[:, l, d * 2 + fc:d * 2 + fc + 1], ALU.mult, ALU.add)
                    tt_ = r_t.get()
                    S.tt("dve", tt_, sg, omlt1[:, d * 256:(d + 1) * 256], ALU.mult)
                    ktm = r_k.get()
                    S.tt("dve", ktm, omlt1[:, d * 256:(d + 1) * 256], tt_, ALU.subtract)
                    lf = r_lf.get()
                    S.stt("dve", lf, tt_, 1e-30, lbt1[:, d * 256:(d + 1) * 256], ALU.max, ALU.add)
                    S.act(lf, lf, AF.Ln)

                    def stage2(s=s, d=d, i=i, t5=t5, sub=sub, Um=Um, Lm=Lm, lf=lf, kf=kf, ktm=ktm):
                        pb = self.psB.get()
                        for fc in range(2):
                            S.mm(pb[:, fc * 128:(fc + 1) * 128], lf[:, fc * 128:(fc + 1) * 128], Um)
                        S.mm(pb[:, 256:512], Lm, lf)
                        eq = r_eq.get()
                        ek = r_ek.get()
                        ee = r_ee.get()
                        pbv = pb[:, 0:256].v(lambda a: a.rearrange("p (a b) -> p a b", a=2))
                        S.act(eq, pbv, AF.Exp)
                        S.act(ek, pbv, AF.Exp, scale=-1.0)
                        S.act(ee, pb[:, 256:512], AF.Exp)
                        qd = r_qd.get()
                        kd = r_kd.get()
                        for fc in range(2):
                            S.tt("dve", qd[:, fc, :], qs[fc][t5][:, sub * 128:(sub + 1) * 128], eq[:, fc, :], ALU.mult)
                        S.tt("dve", kd, kf, ek, ALU.mult)
                        kh = r_kh.get()
                        S.tt("dve", kh, ktm, ee, ALU.mult)
                        vb = r_vb.get()
                        for cch in range(4):
                            S.ts("dve", vb[:, :, cch, :], V[i].v(lambda a: a.rearrange("p (h e) -> p h e", h=4)),
                                 c["ind"][:, cch:cch + 1], None, ALU.mult)

                        def stage3():
                            us = r_us.get()
                            for pr in range(2):
                                pu = self.psA.get()
                                S.mm(pu, kh[:, pr * 128:(pr + 1) * 128],
                                     vb[:, 2 * pr:2 * pr + 2].v(lambda a: a.rearrange("p a b c -> p (a b c)")))
                                puv = pu.v(lambda a: a.rearrange("p (a b c) -> p a b c", a=2, b=4))
                                S.copy("act", us[0:64, pr], puv[0:64, 0])
                                S.copy("act", us[64:128, pr], puv[64:128, 1])
                            sb = r_sb.get()
                            state = states[(s, d)]
                            corder = range(4) if d == 0 else range(3, -1, -1)
                            for cch in corder:
                                S.copy("act", sb[0:64, :, cch, 0:64], state[0:64])
                                S.copy("act", sb[64:128, :, cch, 64:128], state[64:128])
                                dcol = cch * 32 + (31 if d == 0 else 0)
                                nxt = r_sst.get()
                                for pr in range(2):
                                    S.stt("dve", nxt[:, pr, :], state[:, pr, :], eq[:, pr, dcol:dcol + 1],
                                          us[:, pr, cch, :], ALU.mult, ALU.add)
                                state = nxt
                            states[(s, d)] = state

                            def stageC():
                                ats = []
                                for hh in range(4):
                                    fc, par = hh // 2, hh % 2
                                    r = slice(par * 64, par * 64 + 64)
                                    pa = self.psAll.get()
                                    S.mm(pa[:, 0:128], kd[r, fc, :], qd[r, fc, :])
                                    at = r_at.get()
                                    S.tt("dve", at, pa[:, 0:128], Um, ALU.mult)
                                    ats.append(at)
                                for hh in range(4):
                                    fc, par = hh // 2, hh % 2
                                    r = slice(par * 64, par * 64 + 64)
                                    at = ats[hh]
                                    po = self.psAll.get()
                                    for cch in range(4):
                                        cs = slice(cch * 32, (cch + 1) * 32)
                                        S.mm(po[:, cs], V[i][:, fc * 128:(fc + 1) * 128], at[:, cs], start=True, stop=False)
                                        S.mm(po[:, cs], sb[r, fc, cch, :], qd[r, fc, cs], start=False, stop=True)
                                    if (fc, i, par) not in oacc_init:
                                        oacc_init.add((fc, i, par))
                                        S.copy("act", oacc[fc][i][r], po[r, 0:128])
                                    else:
                                        S.tt("dve", oacc[fc][i][r], oacc[fc][i][r], po[r, 0:128], ALU.add)

                            q4.append(stageC)

                        q3.append(stage3)

                    q2.append(stage2)
                    run_queues()
        run_queues(drain=True)
        if not lat:
            for s in range(nseq):
                for d in range(2):
                    S.dma("sp", O["o_st"][l][:, (s * 2 + d) * 128:(s * 2 + d + 1) * 128].rearrange("p (a b) -> p a b", a=2),
                          states[(s, d)])
        ogz = self.ring(2, [512], F32)
        for t in range(NT):
            for fc in range(2):
                src = B(oacc_a[:, fc, t * 512:(t + 1) * 512])
                toks = [oacc[fc][t * 4 + j] for j in range(4)]
                sq = self.f32pool.get()
                S.op("act", lambda e, o=sq, i_=src: e.activation(out=o.ap, in_=i_.ap, func=AF.Square), r=toks, w=[sq])
                pn = self.psD.get()
                S.mm(pn, c["blk64"], sq)
                rs = self.f32pool.get()
                S.act(rs, pn, AF.Sqrt, bias=c["eps"][:, 0:1])
                S.recip(rs, rs)
                on = self.f32pool.get()
                S.op("dve", lambda e, o=on, i_=src, g=c["hgng"][:, l:l + 1], r_=rs:
                     e.scalar_tensor_tensor(out=o.ap, in0=i_.ap, scalar=g.ap, in1=r_.ap, op0=ALU.mult, op1=ALU.mult),
                     r=toks + [rs, c["hgng"]], w=[on])
                pg = self.proj_fm(wb4, fc * 128, 128, t)
                g1 = ogz.get()
                S.act(g1, pg, AF.Sigmoid)
                pz = self.proj_fm(wb4, 256 + fc * 128, 128, t)
                g2 = ogz.get()
                S.act(g2, pz, AF.Silu)
                S.tt("dve", g1, g1, g2, ALU.mult)
                S.tt("dve", y[fc][t], on, g1, ALU.mult)

    def phase3(self, l):
        S, A, I, c = self.S, self.A, self.I, self.c
        NT = self.NT
        Gl = self.G_l(l)
        GT = 2
        wbr_r = self.ring(2, [8, 128], BF16)
        wo_r = self.ring(2, [8, 128], BF16)
        g_r = self.ring(3, [512], BF16)
        p_r = self.ring(4, [512], F32)
        acc_r = self.ring(3, [512], F32)
        x_r = self.ring(6, [512], F32) if self.xdram else None
        ngt = min(NT, GT)
        ma = A.alloc([8, ngt * 512], BF16)
        mg = [[self.nb(ma[:, j, k * 512:(k + 1) * 512]) for k in range(ngt)] for j in range(8)]
        def ldj(j):
            wm_ = self.load_w(l, j * 512, 512, src=I["w_mg"])
            wbt_ = wbr_r.get()
            S.dma("pool", wbt_, I["wbr"][l, j])
            return wm_, wbt_

        def ldo(jo):
            wo_ = wo_r.get()
            S.dma("pool", wo_, I["wout"][l, jo])
            return wo_

        for g0 in range(0, NT, GT):
            tiles = list(range(g0, min(NT, g0 + GT)))
            nxt = ldj(0)
            for j in range(8):
                wm, wbt = nxt
                if j + 1 < 8:
                    nxt = ldj(j + 1)
                for k, t in enumerate(tiles):
                    acc = acc_r.get()
                    for b in range(4):
                        pg = self.proj_fm(wm, b * 128, 128, t, ps=self.psAll.get())
                        gs = g_r.get()
                        S.act(gs, pg, AF.Sigmoid)
                        pp = self.psAll.get()
                        for kc in range(2):
                            S.mm(pp, wbt[:, b * 2 + kc, :], self.y[b][kc][t], start=(kc == 0), stop=(kc == 1))
                        if b == 0:
                            S.tt("dve", acc, pp, gs, ALU.mult)
                        else:
                            pr = p_r.get()
                            S.tt("dve", pr, pp, gs, ALU.mult)
                            if b < 3:
                                S.tt("dve", acc, acc, pr, ALU.add)
                            else:
                                S.tt("dve", mg[j][k], acc, pr, ALU.add)
            nxo = ldo(0)
            for jo in range(8):
                wo = nxo
                if jo + 1 < 8:
                    nxo = ldo(jo + 1)
                for k, t in enumerate(tiles):
                    po = self.psAll.get()
                    for kc in range(8):
                        S.mm(po, wo[:, kc, :], mg[kc][k], start=(kc == 0), stop=(kc == 7))
                    if self.xdram:
                        xb = x_r.get()
                        S.dma("sp", xb, self.xcur[jo][t])
                        S.stt("dve", xb, po, Gl[:, jo:jo + 1], xb, ALU.mult, ALU.add)
                        S.dma("act", self.xd[jo][t], xb)
                    else:
                        S.stt("dve", self.x[jo][t], po, Gl[:, jo:jo + 1], self.x[jo][t], ALU.mult, ALU.add)

    def final_norm(self, dst):
        S, c = self.S, self.c
        stg = self.ring(3, [512], F32)
        ring = self.ring(8, [512], F32) if self.xdram else None
        for t in range(self.NT):
            xt = self.x_tiles(t, ring)
            rs = self.rms_stats(xt, c["onesD"])
            for kc in range(8):
                st = stg.get()
                S.stt("dve", st, xt[kc], c["fg"][:, kc:kc + 1], rs, ALU.mult, ALU.mult)
                S.dma("pool" if self.xdram else "sp", dst[kc * 128:(kc + 1) * 128, t * 512:(t + 1) * 512], st)


def _consts():
    idx = np.arange(128)
    ch = idx // 32
    same = ch[:, None] == ch[None, :]
    ident = np.eye(128, dtype=np.float32)
    Uf = (same & (idx[:, None] <= idx[None, :])).astype(np.float32)
    Ub = (same & (idx[:, None] >= idx[None, :])).astype(np.float32)
    Lf = (same & (idx[:, None] > idx[None, :])).astype(np.float32)
    Lb = (same & (idx[:, None] < idx[None, :])).astype(np.float32)
    cmat = np.stack([ident, Uf, Ub, Lf, Lb], axis=1).astype(np.float32)
    ind = (ch[:, None] == np.arange(4)[None, :]).astype(np.float32)
    cq = 63 - np.arange(64)
    qstart = np.clip(cq - 8, 0, 48)
    ck = np.arange(64)
    ok = (ck[None, :] >= qstart[:, None]) & (ck[None, :] < qstart[:, None] + 16)
    namask = np.where(ok, 0.0, NEG).astype(np.float32)
    antiI = np.zeros((64, 64), np.float32)
    antiI[63 - np.arange(64), np.arange(64)] = 1.0
    a = np.arange(128)
    prevT = np.where(a[:, None] <= a[None, :], 0.0, NEG).astype(np.float32)
    nextT = np.where(a[None, :] <= a[:, None], 0.0, NEG).astype(np.float32)
    wgmask = np.stack([prevT, nextT], axis=1).astype(np.float32)
    tpos = np.arange(2048)
    row = (tpos // 64).astype(np.float32)
    col = (tpos % 64).astype(np.float32)

    def tab(rot):
        nf = rot // 4
        inv = (np.float32(10000.0) ** (-np.arange(nf, dtype=np.float32) / np.float32(nf))).astype(np.float32)
        ang = np.concatenate([row[:, None] * inv, col[:, None] * inv], axis=-1).astype(np.float32)
        cs, sn = np.cos(ang).astype(np.float32), np.sin(ang).astype(np.float32)
        cs2 = np.concatenate([cs, cs], axis=1).T
        sn2 = np.concatenate([-sn, sn], axis=1).T
        return cs2, sn2

    c64, s64 = tab(64)
    ropewg = np.stack([np.concatenate([c64, c64], 0), np.concatenate([s64, s64], 0)], axis=1)
    c32, s32 = tab(32)
    ropeml = np.stack([c32, s32], axis=1)
    return dict(cmat=cmat, ind=ind, namask=namask, antiI=antiI, wgmask=wgmask,
                ropewg=np.ascontiguousarray(ropewg, np.float32), ropeml=np.ascontiguousarray(ropeml, np.float32))


_CACHE = {}


def _get_nc(key, **kw):
    if key not in _CACHE:
        b = Builder(**kw)
        nc = b.build()
        _CACHE[key] = (b, nc)
    return _CACHE[key]


def _host_inputs(inp):
    f = lambda a: np.ascontiguousarray(a, dtype=np.float32)
    sh = {}
    sh["w_mod"] = f(inp["w_mod"])
    sh["bmodr"] = f(inp["b_mod"].reshape(4, 24, 128).transpose(2, 0, 1))
    sh["ng"] = f(inp["norm_g"].reshape(4, 8, 128).transpose(2, 0, 1))
    sh["fg"] = f(inp["final_g"].reshape(8, 128).T)
    sh["w_in"] = f(inp["w_in"])
    wm = inp["w_in"][:, :, C_MERGE:].reshape(4, 1024, 4, 8, 128).transpose(0, 1, 3, 2, 4).reshape(4, 1024, 4096)
    sh["w_mg"] = f(wm)
    wb = inp["w_branch"].reshape(4, 4, 2, 128, 8, 128).transpose(0, 4, 3, 1, 2, 5).reshape(4, 8, 128, 8, 128)
    sh["wbr"] = f(wb)
    wo = inp["w_out"].reshape(4, 8, 128, 8, 128).transpose(0, 3, 2, 1, 4)
    sh["wout"] = f(wo)
    sh["lbl"] = f(inp["hg_lb_logits"].reshape(4, 512))
    sh["lblT"] = f(inp["hg_lb_logits"].reshape(4, 2, 2, 128).transpose(3, 0, 1, 2).reshape(128, 4, 4))
    sh["hgng"] = f(np.concatenate([inp["hg_norm_g"], inp["hg_norm_g"]], axis=1).T)
    sh["qng"] = f(inp["mla_q_norm_g"].reshape(4, 2, 128).transpose(2, 0, 1))
    sh["kvng"] = f(inp["mla_kv_norm_g"])
    sh["wqb"] = f(inp["mla_w_qb"])
    sh["wkvb"] = f(inp["mla_w_kvb"])
    sh["sink"] = f(inp["wg_sink"].reshape(1, 16))
    rp = np.zeros((4, 60, 127), np.float32)
    rp[:, :, 48:48 + 31] = inp["na_rpb"].reshape(4, 60, 31)
    sh["rpbp"] = rp
    sh.update(_consts())
    maps = []
    for i in range(8):
        b = i // 2
        m = dict(sh)
        m["xT_c"] = f(inp["x_prompt"][2 * i:2 * i + 2].reshape(512, 1024).T)
        m["xT_l"] = f(inp["x_sample"][b].T)
        m["cc"] = f(np.stack([inp["c_ctx"], inp["c"][b]], axis=1))
        m["cnakT"] = f(inp["cache_na_k"][b].reshape(4, 512, 256).transpose(0, 2, 1))
        m["cnav"] = f(inp["cache_na_v"][b].reshape(4, 512, 256))
        st = inp["state_hgrn"][b].reshape(4, 2, 2, 2, 64, 64).transpose(0, 3, 4, 1, 2, 5).reshape(4, 128, 256)
        m["st0"] = f(st)
        m["cckvT"] = f(inp["cache_mla_ckv"][b].transpose(0, 2, 1))
        m["ckrT"] = f(inp["cache_mla_krope"][b].transpose(0, 2, 1))
        m["cwgkT"] = f(inp["cache_wg_k"][b].reshape(4, 512, 128).transpose(0, 2, 1))
        m["cwgv"] = f(inp["cache_wg_v"][b].reshape(4, 512, 128))
        maps.append(m)
    return maps


def _assemble(res):
    y_prompt = np.empty((16, 256, 1024), np.float32)
    y_sample = np.empty((4, 2048, 1024), np.float32)
    na_k = np.empty((16, 4, 256, 4, 64), np.float32)
    na_v = np.empty((16, 4, 256, 4, 64), np.float32)
    st = np.empty((16, 4, 2, 4, 64, 64), np.float32)
    ckv = np.empty((16, 4, 256, 128), np.float32)
    kr = np.empty((16, 4, 256, 32), np.float32)
    wgk = np.empty((16, 4, 256, 2, 64), np.float32)
    wgv = np.empty((16, 4, 256, 2, 64), np.float32)
    for i in range(8):
        r = res[i]
        y_prompt[2 * i:2 * i + 2] = r["ypT"].T.reshape(2, 256, 1024)
        if i % 2 == 0:
            y_sample[i // 2] = r["ysT"].T
        na_k[2 * i:2 * i + 2] = r["o_nak"].transpose(0, 2, 1).reshape(4, 2, 256, 4, 64).transpose(1, 0, 2, 3, 4)
        na_v[2 * i:2 * i + 2] = r["o_nav"].reshape(4, 2, 256, 4, 64).transpose(1, 0, 2, 3, 4)
        s_ = r["o_st"].reshape(4, 2, 64, 2, 2, 2, 64)
        s_ = s_.transpose(3, 0, 4, 5, 1, 2, 6).reshape(2, 4, 2, 4, 64, 64)
        st[2 * i:2 * i + 2] = s_
        kv = r["o_kv"].reshape(4, 2, 256, 160).transpose(1, 0, 2, 3)
        ckv[2 * i:2 * i + 2] = kv[..., :128]
        kr[2 * i:2 * i + 2] = kv[..., 128:]
        wgk[2 * i:2 * i + 2] = r["o_wgk"].transpose(0, 2, 1).reshape(4, 2, 256, 2, 64).transpose(1, 0, 2, 3, 4)
        wgv[2 * i:2 * i + 2] = r["o_wgv"].reshape(4, 2, 256, 2, 64).transpose(1, 0, 2, 3, 4)
    return (y_prompt, y_sample, na_k, na_v, st, ckv, kr, wgk, wgv)


def kernel(**inputs):
    inp = {k: np.asarray(v) for k, v in inputs.items()}
    b, nc = _get_nc("full")
    maps = _host_inputs(inp)
    maps = [{k: m[k] for k in b.in_shapes} for m in maps]
    res = run_bass_kernel_spmd(nc, maps, core_ids=list(range(8)))
    return _assemble(res.results)
```

```python
import contextlib
import math
import os
import numpy as np
import concourse.bass as bass
import concourse.mybir as mybir
from concourse.bass_utils import run_bass_kernel_spmd

F32 = mybir.dt.float32
BF16 = mybir.dt.bfloat16
AF = mybir.ActivationFunctionType
ALU = mybir.AluOpType

SEM_LIMIT = 30000
DEPTH = 4
EPS = 1e-6
NEG = -30000.0


class Tok:
    __slots__ = ("w", "rs", "excl")

    def __init__(self, w=None):
        self.w = w
        self.rs = []
        self.excl = False


class Op:
    __slots__ = ("eng", "fn", "deps", "inc", "sem", "val", "dma", "prev")

    def __init__(self, eng, fn, dma):
        self.eng = eng
        self.fn = fn
        self.deps = []
        self.inc = False
        self.sem = None
        self.val = 0
        self.dma = dma


class B:
    __slots__ = ("ap", "tok")

    def __init__(self, ap, tok=None):
        self.ap = ap
        self.tok = tok if tok is not None else Tok()

    def __getitem__(self, k):
        return B(self.ap[k], self.tok)

    def v(self, f):
        return B(f(self.ap), self.tok)


class Sched:
    ENGS = ("pe", "act", "dve", "pool", "sp")

    def __init__(self, nc, es):
        self.nc = nc
        self.es = es
        self.ops = {e: [] for e in self.ENGS}
        self.nsem = 0
        self.barrier_op = None

    def new_sem(self):
        s = self.es.enter_context(self.nc.semaphore("s%d" % self.nsem))
        self.nsem += 1
        return s

    def _add(self, eng, fn, r, w, dma):
        op = Op(eng, fn, dma)
        deps = []
        for t in r:
            if t.w is not None:
                deps.append(t.w)
            if t.excl:
                deps.extend(o for o in t.rs if o.eng != eng)
        for t in w:
            if t.w is not None:
                deps.append(t.w)
            deps.extend(t.rs)
        seen = set()
        for d in deps:
            if id(d) in seen:
                continue
            seen.add(id(d))
            if d.eng == "pe" and eng == "pe" and not d.dma and not dma:
                continue
            op.deps.append(d)
            d.inc = True
        for t in r:
            t.rs.append(op)
        for t in w:
            t.w = op
            t.rs = []
        self.ops[eng].append(op)
        return op

    def op(self, eng, fn, r=(), w=()):
        return self._add(eng, fn, [b.tok for b in r], [b.tok for b in w], False)

    def dma(self, eng, out, in_, r=(), w=()):
        oa = out.ap if isinstance(out, B) else out
        ia = in_.ap if isinstance(in_, B) else in_
        rr = [b.tok for b in r] + ([in_.tok] if isinstance(in_, B) else [])
        ww = [b.tok for b in w] + ([out.tok] if isinstance(out, B) else [])
        return self._add(eng, lambda e: e.dma_start(out=oa, in_=ia), rr, ww, True)

    def mm(self, out, lhsT, rhs, start=True, stop=True):
        return self.op("pe", lambda e: e.matmul(out.ap, lhsT=lhsT.ap, rhs=rhs.ap, start=start, stop=stop),
                       r=[lhsT, rhs], w=[out])

    def tr(self, out, in_, ident):
        return self.op("pe", lambda e: e.transpose(out.ap, in_.ap, ident.ap), r=[in_, ident], w=[out])

    def act(self, out, in_, func, scale=1.0, bias=None, accum=None, r=()):
        rr = [in_] + list(r)
        kw = {}
        if isinstance(scale, B):
            rr.append(scale)
            kw["scale"] = scale.ap
        else:
            kw["scale"] = scale
        if bias is not None:
            rr.append(bias)
            kw["bias"] = bias.ap
        ww = [out]
        if accum is not None:
            ww.append(accum)
            kw["accum_out"] = accum.ap
        return self.op("act", lambda e: e.activation(out=out.ap, in_=in_.ap, func=func, **kw), r=rr, w=ww)

    def tt(self, eng, out, in0, in1, op):
        return self.op(eng, lambda e: e.tensor_tensor(out=out.ap, in0=in0.ap, in1=in1.ap, op=op),
                       r=[in0, in1], w=[out])

    def ts(self, eng, out, in0, s1, s2=None, op0=ALU.mult, op1=None):
        rr = [in0]
        a1 = s1
        a2 = s2
        if isinstance(s1, B):
            rr.append(s1)
            a1 = s1.ap
        if isinstance(s2, B):
            rr.append(s2)
            a2 = s2.ap
        if op1 is None:
            return self.op(eng, lambda e: e.tensor_scalar(out=out.ap, in0=in0.ap, scalar1=a1, scalar2=None,
                                                          op0=op0), r=rr, w=[out])
        return self.op(eng, lambda e: e.tensor_scalar(out=out.ap, in0=in0.ap, scalar1=a1, scalar2=a2,
                                                      op0=op0, op1=op1), r=rr, w=[out])

    def stt(self, eng, out, in0, scalar, in1, op0, op1):
        rr = [in0, in1]
        a = scalar
        if isinstance(scalar, B):
            rr.append(scalar)
            a = scalar.ap
        return self.op(eng, lambda e: e.scalar_tensor_tensor(out=out.ap, in0=in0.ap, scalar=a, in1=in1.ap,
                                                             op0=op0, op1=op1), r=rr, w=[out])

    def copy(self, eng, out, in_):
        if eng == "act":
            return self.act(out, in_, AF.Identity)
        return self.op(eng, lambda e: e.tensor_copy(out=out.ap, in_=in_.ap), r=[in_], w=[out])

    def recip(self, out, in_):
        return self.op("dve", lambda e: e.reciprocal(out=out.ap, in_=in_.ap), r=[in_], w=[out])

    def memset(self, eng, out, val):
        return self.op(eng, lambda e: e.memset(out.ap, val), w=[out])

    def barrier(self, toks):
        bs = [B(None, t) for t in toks]
        scr = self.scratch
        op = self.op("pool", lambda e: e.memset(scr.ap, 0.0), r=bs, w=bs + [scr])
        return op

    def emit(self):
        nc = self.nc
        NRING = {"sp": 40, "pool": 40, "act": 16, "pe": 1, "dve": 1}
        for e in self.ENGS:
            csem = None
            ccnt = 0
            ring = []
            rcnt = []
            nd = 0
            for op in self.ops[e]:
                if op.dma:
                    op.inc = True
                    if len(ring) < NRING[e]:
                        ring.append(self.new_sem())
                        rcnt.append(0)
                    k = nd % NRING[e]
                    nd += 1
                    op.prev = (ring[k], rcnt[k])
                    rcnt[k] += 16
                    op.sem, op.val = ring[k], rcnt[k]
                elif op.inc:
                    if csem is None or ccnt + 1 > SEM_LIMIT:
                        csem = self.new_sem()
                        ccnt = 0
                    ccnt += 1
                    op.sem, op.val = csem, ccnt
        engobj = {"pe": "tensor", "act": "scalar", "dve": "vector", "pool": "gpsimd", "sp": "sync"}
        stats = {}
        with nc.Block() as block:
            for e in self.ENGS:
                ops = self.ops[e]

                def body(eng, ops=ops, e=e):
                    waited = {}
                    nw = 0
                    for op in ops:
                        need = {}
                        deps = [(d.sem, d.val) for d in op.deps]
                        if op.dma and op.prev[1] > 0:
                            deps.append(op.prev)
                        for (s_, v_) in deps:
                            k = id(s_)
                            if waited.get(k, 0) >= v_:
                                continue
                            if k not in need or need[k][1] < v_:
                                need[k] = (s_, v_)
                        for k, (s, v) in need.items():
                            eng.wait_ge(s, v)
                            waited[k] = v
                            nw += 1
                        ins = op.fn(eng)
                        if op.inc:
                            ins.then_inc(op.sem, 16 if op.dma else 1)
                    if e == "sp":
                        for ee in self.ENGS:
                            last = {}
                            for op in self.ops[ee]:
                                if op.dma:
                                    last[id(op.sem)] = (op.sem, op.val)
                            for k, (s, v) in last.items():
                                if waited.get(k, 0) < v:
                                    eng.wait_ge(s, v)
                    stats[e] = (len(ops), nw)

                getattr(block, engobj[e])(body)
        return stats


class Arena:
    def __init__(self, ap, nwords):
        self.ap = ap
        self.n = nwords
        self.top = 0
        self.peak = 0
        self.live = []
        self.dead = []
        self.last = None

    def release_to(self, top):
        keep = []
        for blk in self.live:
            (self.dead if blk[0] >= top else keep).append(blk)
        self.live = keep
        self.top = top

    def alloc(self, shape, dt, parts=128):
        nel = 1
        for s in shape:
            nel *= s
        words = (nel * (4 if dt == F32 else 2) + 3) // 4
        wal = (words + 7) // 8 * 8
        off = self.top
        self.top += wal
        self.peak = max(self.peak, self.top)
        assert self.top <= self.n, "arena overflow %d > %d" % (self.top, self.n)
        end = off + wal
        inherit = []
        nd = []
        for blk in self.dead:
            if blk[0] < end and off < blk[1]:
                for t in blk[2]:
                    if t.w is not None:
                        inherit.append(t.w)
                    inherit.extend(t.rs)
                if blk[0] >= off and blk[1] <= end:
                    continue
            nd.append(blk)
        self.dead = nd
        uniq = []
        seen = set()
        for o in inherit:
            if id(o) not in seen:
                seen.add(id(o))
                uniq.append(o)
        self.last = [off, end, [], uniq]
        self.live.append(self.last)
        v = self.ap[:, off:off + words]
        if dt != F32:
            v = v.bitcast(dt)
        if len(shape) == 2:
            v = v.rearrange("p (a b) -> p a b", a=shape[0])
        elif len(shape) == 3:
            v = v.rearrange("p (a b c) -> p a b c", a=shape[0], b=shape[1])
        elif len(shape) == 4:
            v = v.rearrange("p (a b c d) -> p a b c d", a=shape[0], b=shape[1], c=shape[2])
        if parts != 128:
            v = v[0:parts]
        return v

    def newtok(self):
        t = Tok()
        t.rs = list(self.last[3])
        self.last[2].append(t)
        return t


class Ring:
    def __init__(self, bufs):
        self.bufs = bufs
        self.i = 0

    def get(self):
        b = self.bufs[self.i % len(self.bufs)]
        self.i += 1
        return b


C_NA_Q, C_NA_K, C_NA_V, C_NA_Z = 0, 256, 512, 768
C_HG_Q, C_HG_FF, C_HG_FB, C_HG_I, C_HG_OG, C_HG_Z = 1024, 1280, 1536, 1792, 2048, 2304
C_ML_QA, C_ML_KVA, C_ML_Z = 2560, 2816, 2976
C_WG_Q, C_WG_K, C_WG_V, C_WG_Z = 3232, 3488, 3616, 3744
C_MERGE = 4000


class Builder:
    def __init__(self, run_ctx=True, run_lat=True, nlayers=DEPTH, dbg=None):
        self.dbg = dbg
        self.run_ctx = run_ctx
        self.run_lat = run_lat
        self.nlayers = nlayers

    def dram_in(self, name, shape):
        self.in_shapes[name] = shape
        return self.nc.dram_tensor(name, list(shape), F32, kind="ExternalInput").ap()

    def dram_out(self, name, shape):
        self.out_shapes[name] = shape
        return self.nc.dram_tensor(name, list(shape), F32, kind="ExternalOutput").ap()

    def build(self):
        nc = bass.Bass("TRN2", target_bir_lowering=False)
        self.nc = nc
        self.in_shapes = {}
        self.out_shapes = {}
        di, do = self.dram_in, self.dram_out
        I = {}
        for name, shape in [
            ("xT_c", (1024, 512)), ("xT_l", (1024, 2048)), ("cc", (1024, 2)),
            ("w_mod", (4, 1024, 3072)), ("bmodr", (128, 4, 24)), ("ng", (128, 4, 8)), ("fg", (128, 8)),
            ("w_in", (4, 1024, 8096)), ("w_mg", (4, 1024, 4096)),
            ("wbr", (4, 8, 128, 8, 128)), ("wout", (4, 8, 128, 8, 128)),
            ("lbl", (4, 512)), ("lblT", (128, 4, 4)), ("hgng", (128, 4)), ("qng", (128, 4, 2)), ("kvng", (4, 128)),
            ("wqb", (4, 256, 384)), ("wkvb", (4, 128, 512)), ("sink", (1, 16)), ("rpbp", (4, 60, 127)),
            ("cnakT", (4, 256, 512)), ("cnav", (4, 512, 256)), ("st0", (4, 128, 256)),
            ("cckvT", (4, 128, 512)), ("ckrT", (4, 32, 512)), ("cwgkT", (4, 128, 512)), ("cwgv", (4, 512, 128)),
            ("cmat", (128, 5, 128)), ("ind", (128, 4)), ("namask", (64, 64)), ("antiI", (64, 64)),
            ("wgmask", (128, 2, 128)), ("ropewg", (128, 2, 2048)), ("ropeml", (32, 2, 2048)),
        ]:
            I[name] = di(name, shape)
        O = {}
        for name, shape in [
            ("ypT", (1024, 512)), ("ysT", (1024, 2048)), ("o_nak", (4, 256, 512)), ("o_nav", (4, 512, 256)),
            ("o_st", (4, 128, 512)), ("o_kv", (4, 512, 160)), ("o_wgk", (4, 128, 512)), ("o_wgv", (4, 512, 128)),
        ]:
            O[name] = do(name, shape)
        self.I, self.O = I, O
        self.xscr = nc.dram_tensor("xscr", [1024, 2048], F32, kind="Internal").ap()

        with contextlib.ExitStack() as es:
            S = Sched(nc, es)
            self.S = S
            NW = 206 * 256
            arena_t = es.enter_context(nc.sbuf_tensor("arena", [128, NW], F32))
            self.A = Arena(arena_t[:, :], NW)
            self.ps = []
            for i in range(8):
                t = es.enter_context(nc.psum_tensor("ps%d" % i, [128, 512], F32))
                self.ps.append(B(t[:, :]))
                self.ps[-1].tok.excl = True
            self.psA = Ring(self.ps[0:2])
            self.psB = Ring(self.ps[2:4])
            self.psC = Ring(self.ps[4:6])
            self.psD = Ring(self.ps[6:8])
            self.psS = Ring(self.ps[0:4])
            self.psO = Ring(self.ps[4:8])
            self.psAll = Ring(self.ps)
            self.pendq = []
            self.setup_consts()
            base_top = self.A.top
            self.prologue_tables()
            self.prologue_mod([0] if self.run_ctx else list(range(DEPTH)))
            if self.run_ctx:
                self.A.release_to(base_top)
                self.run_pass(False)
            if self.run_lat:
                self.A.release_to(base_top)
                self.run_pass(True)
            self.stats = S.emit()
            self.stats["peak_words"] = self.A.peak
        return nc

    def nb(self, ap):
        return B(ap, self.A.newtok())

    def mark(self, name):
        if not hasattr(self, "marks"):
            self.marks = []
        self.marks.append((("lat " if getattr(self, "lat", False) else "ctx ") + name, len(self.S.ops["pe"])))

    def all_toks(self):
        return list(self.pass_toks) + [p.tok for p in self.ps]

    def alloc(self, shape, dt, parts=128):
        return self.nb(self.A.alloc(shape, dt, parts))

    def ring(self, n, shape, dt, parts=128):
        return Ring([self.alloc(shape, dt, parts) for _ in range(n)])

    def setup_consts(self):
        S, A, I = self.S, self.A, self.I
        S.scratch = B(A.alloc([8], F32))
        c = {}
        c["cmat"] = B(A.alloc([5, 128], F32))
        S.dma("sp", c["cmat"], I["cmat"])
        c["ident"] = c["cmat"][:, 0, :]
        c["ind"] = B(A.alloc([4], F32))
        S.dma("sp", c["ind"], I["ind"])
        c["onesD"] = B(A.alloc([128], F32))
        S.memset("dve", c["onesD"], 1.0 / 1024.0)
        c["ones256"] = B(A.alloc([128], F32))
        S.memset("dve", c["ones256"], 1.0 / 256.0)
        c["blk64"] = B(A.alloc([128], F32))
        S.memset("dve", c["blk64"], 0.0)
        S.memset("dve", c["blk64"][0:64, 0:64], 1.0 / 64.0)
        S.memset("dve", c["blk64"][64:128, 64:128], 1.0 / 64.0)
        c["eps"] = B(A.alloc([1], F32))
        S.memset("dve", c["eps"], EPS)
        c["onesb"] = B(A.alloc([128], BF16))
        S.memset("dve", c["onesb"], 1.0)
        c["onesrow"] = B(A.alloc([512], BF16))
        S.memset("pool", c["onesrow"], 1.0)
        c["identb"] = B(A.alloc([128], BF16))
        S.dma("pool", c["identb"], I["cmat"][:, 0, :])
        c["ng"] = B(A.alloc([4, 8], F32))
        S.dma("sp", c["ng"], I["ng"])
        c["fg"] = B(A.alloc([8], F32))
        S.dma("sp", c["fg"], I["fg"])
        c["bmodr"] = B(A.alloc([4, 24], F32))
        S.dma("sp", c["bmodr"], I["bmodr"])
        c["hgng"] = B(A.alloc([4], F32))
        S.dma("sp", c["hgng"], I["hgng"])
        c["qng"] = B(A.alloc([4, 2], F32))
        S.dma("sp", c["qng"], I["qng"])
        c["kvng"] = B(A.alloc([4, 128], F32))
        S.dma("sp", c["kvng"].v(lambda a: a.rearrange("p a b -> p (a b)")),
              I["kvng"].rearrange("l n -> (l n)").partition_broadcast(128))
        c["esink"] = B(A.alloc([16], F32))
        S.dma("sp", c["esink"], I["sink"].rearrange("a n -> (a n)").partition_broadcast(128))
        S.act(c["esink"], c["esink"], AF.Exp)
        c["lb_fm"] = B(A.alloc([4, 4], F32))
        c["oml_fm"] = B(A.alloc([4, 4], F32))
        c["noml_fm"] = B(A.alloc([4, 4], F32))
        c["ccs"] = B(A.alloc([8, 2], F32))
        S.dma("sp", c["ccs"], I["cc"].rearrange("(kc p) c -> p kc c", p=128))
        S.act(c["ccs"], c["ccs"], AF.Silu)
        self.Am = B(A.alloc([2, 4, 8], F32))
        self.Bm = B(A.alloc([2, 4, 8], F32))
        self.Gm = B(A.alloc([2, 4, 8], F32))
        self.c = c

    def lb_tables(self, n, src, lb, oml, layers):
        S = self.S
        raw = self.alloc([4, n], F32)
        S.dma("sp", raw.v(lambda a: a.rearrange("p a b -> p (a b)")), src)
        S.act(raw, raw, AF.Exp)
        tot = self.alloc([n], F32)
        S.tt("dve", tot, raw[:, 0, :], raw[:, 1, :], ALU.add)
        S.tt("dve", tot, tot, raw[:, 2, :], ALU.add)
        S.tt("dve", tot, tot, raw[:, 3, :], ALU.add)
        S.recip(tot, tot)
        acc = self.alloc([n], F32)
        S.memset("dve", acc, 0.0)
        pl = self.alloc([n], F32)
        for l in range(4):
            if l > 0:
                S.tt("dve", pl, raw[:, l, :], tot, ALU.mult)
                S.tt("dve", acc, acc, pl, ALU.add)
            if l in layers:
                S.copy("dve", lb(l), acc)
                S.ts("dve", oml(l), acc, -1.0, 1.0, ALU.mult, ALU.add)

    def prologue_tables(self):
        S, c, I = self.S, self.c, self.I
        self.lb_tables(4, I["lblT"].rearrange("p a b -> p (a b)"), lambda l: c["lb_fm"][:, l, :],
                       lambda l: c["oml_fm"][:, l, :], range(4))
        S.ts("dve", c["noml_fm"], c["oml_fm"], -1.0)

    def prologue_mod(self, layers):
        S, A, I, c = self.S, self.A, self.I, self.c
        ccs = c["ccs"]
        mod = self.alloc([4, 24, 2], F32)
        wring = self.ring(2, [8, 768], F32)
        for l in layers:
            for pc in range(4):
                wt = wring.get()
                S.dma("sp", wt, I["w_mod"][l][:, pc * 768:(pc + 1) * 768].rearrange("(kc p) n -> p kc n", p=128))
                ps = self.psAll.get()
                for jj in range(6):
                    for kc in range(8):
                        S.mm(ps[:, jj * 2:(jj + 1) * 2], wt[:, kc, jj * 128:(jj + 1) * 128], ccs[:, kc, :],
                             start=(kc == 0), stop=(kc == 7))
                S.tt("dve", mod[:, l, pc * 6:(pc + 1) * 6, :],
                     ps[:, 0:12].v(lambda a: a.rearrange("p (j c) -> p j c", c=2)),
                     c["bmodr"][:, l, pc * 6:(pc + 1) * 6].v(lambda a: a.unsqueeze(2).broadcast_to([128, 6, 2])),
                     ALU.add)
        Am, Bm, Gm = self.Am, self.Bm, self.Gm
        for p in range(2):
            for l in layers:
                S.ts("dve", Am[:, p, l], mod[:, l, 8:16, p], 1.0, None, ALU.add)
                S.tt("dve", Am[:, p, l], Am[:, p, l], c["ng"][:, l], ALU.mult)
                S.copy("dve", Bm[:, p, l], mod[:, l, 0:8, p])
                S.copy("dve", Gm[:, p, l], mod[:, l, 16:24, p])

    def run_pass(self, lat):
        S, A, I, O, c = self.S, self.A, self.I, self.O, self.c
        TOK = 2048 if lat else 512
        NT = TOK // 512
        pidx = 1 if lat else 0
        self.lat, self.TOK, self.NT, self.pidx = lat, TOK, NT, pidx
        xsrc = I["xT_l"] if lat else I["xT_c"]
        self.xdram = lat
        if lat:
            self.xd = [[B(self.xscr[kc * 128:(kc + 1) * 128, t * 512:(t + 1) * 512]) for t in range(NT)]
                       for kc in range(8)]
            self.xin = [[B(xsrc[kc * 128:(kc + 1) * 128, t * 512:(t + 1) * 512]) for t in range(NT)]
                        for kc in range(8)]
            self.xcur = self.xin
        else:
            xs = A.alloc([8, TOK], F32)
            self.x = [[self.nb(xs[:, kc, t * 512:(t + 1) * 512]) for t in range(NT)] for kc in range(8)]
            for kc in range(8):
                for t in range(NT):
                    S.dma("sp", self.x[kc][t], xsrc[kc * 128:(kc + 1) * 128, t * 512:(t + 1) * 512])
        hs = A.alloc([8, TOK], BF16)
        self.h = [[self.nb(hs[:, kc, t * 512:(t + 1) * 512]) for t in range(NT)] for kc in range(8)]
        self.hall = hs
        self.y = []
        for b in range(4):
            ya = A.alloc([2, TOK], BF16)
            self.y.append([[self.nb(ya[:, fc, t * 512:(t + 1) * 512]) for t in range(NT)] for fc in range(2)])
        self.wpool = self.ring(3, [8, 512], BF16)
        self.f32pool = self.ring(4, [512], F32)
        self.rspool = self.ring(2, [512], F32)
        self.bfpool = self.ring(4, [512], BF16)
        self.A_l = lambda l: self.Am[:, pidx, l, :]
        self.B_l = lambda l: self.Bm[:, pidx, l, :]
        self.G_l = lambda l: self.Gm[:, pidx, l, :]
        ptop = A.top
        for l in range(self.nlayers):
            A.release_to(ptop)
            self.mark("L%d make_h" % l)
            self.make_h(l)
            on = (lambda k: True) if self.dbg is None else (lambda k: k in self.dbg.split(","))
            if on("na"):
                self.mark("L%d na" % l)
                self.branch_na(l)
            A.release_to(ptop)
            if (not lat) and l == 0 and self.run_ctx:
                self.prologue_mod([1, 2, 3])
                A.release_to(ptop)
            if on("hg"):
                self.mark("L%d hg" % l)
                self.branch_hg(l)
            A.release_to(ptop)
            if on("mla"):
                self.mark("L%d mla" % l)
                self.branch_mla(l)
            A.release_to(ptop)
            if on("wg"):
                self.mark("L%d wg" % l)
                self.branch_wg(l)
            A.release_to(ptop)
            if on("p3"):
                self.mark("L%d p3" % l)
                self.phase3(l)
                if self.xdram:
                    self.xcur = self.xd
        A.release_to(ptop)
        self.mark("final")
        self.final_norm(O["ysT"] if lat else O["ypT"])

    def rms_stats(self, chunks, ones, nparts=128):
        S, c = self.S, self.c
        n = chunks[0].ap.shape[-1]
        ps = self.psD.get()
        for i, ch in enumerate(chunks):
            sq = self.f32pool.get()
            S.act(sq[:, 0:n], ch, AF.Square)
            S.mm(ps[:, 0:n], ones, sq[:, 0:n], start=(i == 0), stop=(i == len(chunks) - 1))
        rs = self.rspool.get()
        S.act(rs[:, 0:n], ps[:, 0:n], AF.Sqrt, bias=c["eps"][:, 0:1])
        S.recip(rs[:, 0:n], rs[:, 0:n])
        return rs[:, 0:n]

    def x_tiles(self, t, ring):
        if not self.xdram:
            return [self.x[kc][t] for kc in range(8)]
        out = []
        for kc in range(8):
            xb = ring.get()
            self.S.dma("sp", xb, self.xcur[kc][t])
            out.append(xb)
        return out

    def make_h(self, l):
        S, c = self.S, self.c
        Al, Bl = self.A_l(l), self.B_l(l)
        mk = self.A.top
        ring = self.ring(8, [512], F32) if self.xdram else None
        for t in range(self.NT):
            xt = self.x_tiles(t, ring)
            rs = self.rms_stats(xt, c["onesD"])
            for kc in range(8):
                tmp = self.f32pool.get()
                S.tt("dve", tmp, xt[kc], rs, ALU.mult)
                S.act(self.h[kc][t], tmp, AF.Identity, scale=Al[:, kc:kc + 1], bias=Bl[:, kc:kc + 1])
        self.A.release_to(mk)

    def load_w(self, l, c0, n, src=None):
        wb = self.wpool.get()
        w = self.I["w_in"] if src is None else src
        self.S.dma("pool", wb[:, :, 0:n], w[l][:, c0:c0 + n].rearrange("(kc p) n -> p kc n", p=128))
        return wb

    def proj_fm(self, wb, co, ncols, t, ps=None, pcol=0, ntok=512, tok0=0):
        S = self.S
        if ps is None:
            ps = self.psAll.get()
        for kc in range(8):
            S.mm(ps[0:ncols, pcol:pcol + ntok], wb[:, kc, co:co + ncols], self.h[kc][t][:, tok0:tok0 + ntok],
                 start=(kc == 0), stop=(kc == 7))
        return ps

    def proj_tm(self, wb, co, ncols, t, sub, ps=None, pcol=0, shift=0):
        S = self.S
        if ps is None:
            ps = self.psAll.get()
        a = sub * 128 + shift
        for kc in range(8):
            S.mm(ps[:, pcol:pcol + ncols], self.h[kc][t][:, a:a + 128], wb[:, kc, co:co + ncols],
                 start=(kc == 0), stop=(kc == 7))
        return ps

    def attend(self, sc_list, v_list, nq, scale, fin, tail=None):
        S, c = self.S, self.c
        g = 512 // nq
        nk = len(sc_list)
        O = self.psO.get()
        Dn = None
        for k0 in range(0, nk, g):
            k1 = min(nk, k0 + g)
            sp = self.psS.get()
            for k in range(k0, k1):
                terms = sc_list[k]
                for i, (lt, rh) in enumerate(terms):
                    S.mm(sp[:, (k - k0) * nq:(k - k0 + 1) * nq], lt, rh, start=(i == 0), stop=(i == len(terms) - 1))

            def stage2(k0=k0, k1=k1, sp=sp):
                w = (k1 - k0) * nq
                pt = self.bfpool.get()
                S.act(pt[:, 0:w], sp[:, 0:w], AF.Exp, scale=scale)
                for k in range(k0, k1):
                    blk = pt[:, (k - k0) * nq:(k - k0 + 1) * nq]
                    S.mm(O[:, 0:nq], v_list[k], blk, start=(k == 0), stop=(k == nk - 1 and tail is None))
                if k1 == nk and tail is not None:
                    S.mm(O[:, 0:nq], tail[0], tail[1], start=False, stop=True)
                if k1 == nk:
                    fin(O, Dn)

            self.pendq.append(stage2)
            while len(self.pendq) > self.PIPE_DEPTH:
                self.pendq.pop(0)()

    PIPE_DEPTH = 3

    def flush(self):
        while self.pendq:
            self.pendq.pop(0)()

    def finish_head(self, O, Dn, parity, nq, ydst, sz, extra=None):
        S = self.S
        r0 = parity * 64
        o0 = 64 - r0
        rec = self.f32pool.get()
        if extra is not None:
            S.copy("dve", rec[r0:r0 + 64, 0:nq], O[o0:o0 + 64, 0:nq])
            S.ts("dve", rec[r0:r0 + 64, 0:nq], rec[r0:r0 + 64, 0:nq], extra[r0:r0 + 64], None, ALU.add)
            S.recip(rec[r0:r0 + 64, 0:nq], rec[r0:r0 + 64, 0:nq])
        else:
            S.recip(rec[r0:r0 + 64, 0:nq], O[o0:o0 + 64, 0:nq])
        S.tt("dve", rec[r0:r0 + 64, 0:nq], rec[r0:r0 + 64, 0:nq], sz[r0:r0 + 64], ALU.mult)
        S.tt("dve", ydst[r0:r0 + 64], O[r0:r0 + 64, 0:nq], rec[r0:r0 + 64, 0:nq], ALU.mult)

    def alloc_vaug(self, n):
        a = self.A.alloc([n, 2, 2, 128], BF16)
        tiles = [self.nb(a[:, i]) for i in range(n)]
        for t in tiles:
            self.S.memset("dve", t, 1.0)
        return tiles

    def fill_vaug(self, eng, vt, src):
        sv = src.v(lambda a: a.rearrange("p (f q e) -> p f q e", f=2, q=2))
        for par in range(2):
            self.S.copy(eng, vt[:, :, par, par * 64:(par + 1) * 64], sv[:, :, par, :])

    def fm_group(self, wb, co, nch, dst, func=AF.Identity, scale=1.0, f32dst=None):
        S = self.S
        for t in range(self.NT):
            for fc in range(nch):
                ps = self.proj_fm(wb, co + fc * 128, 128, t)
                S.act(dst[fc][t], ps, func, scale=scale)
                if f32dst is not None:
                    S.copy("dve", f32dst[fc][t], ps)

    def tile2(self, shape_cols, dt, nch, parts=128, shared=False):
        a = self.A.alloc([nch, self.TOK], dt)
        if not shared:
            return [[self.nb(a[0:parts, fc, t * 512:(t + 1) * 512]) for t in range(self.NT)] for fc in range(nch)]
        full = [self.nb(a[0:parts, fc, :]) for fc in range(nch)]
        self.last_full = full
        return [[full[fc][:, t * 512:(t + 1) * 512] for t in range(self.NT)] for fc in range(nch)]

    def branch_na(self, l):
        S, A, I, O, c = self.S, self.A, self.I, self.O, self.c
        lat, NT, TOK = self.lat, self.NT, self.TOK
        y = self.y[0]
        qT = self.tile2(512, BF16, 2)
        kT = self.tile2(512, BF16, 2, shared=lat)
        kTf = self.last_full if lat else None
        wb = self.load_w(l, C_NA_Q, 512)
        self.fm_group(wb, 0, 2, qT, scale=(0.125 if lat else 1.0))
        if lat:
            self.fm_group(wb, 256, 2, kT)
        else:
            stg = self.ring(2, [512], F32)
            for t in range(NT):
                for fc in range(2):
                    ps = self.proj_fm(wb, 256 + fc * 128, 128, t)
                    st = stg.get()
                    S.copy("dve", st, ps)
                    S.copy("dve", kT[fc][t], st)
                    S.dma("sp", O["o_nak"][l, fc * 128:(fc + 1) * 128, :], st)
        if os.environ.get("NA_STOP") == "1":
            return
        wb = self.load_w(l, C_NA_V, 512)
        self.fm_group(wb, 256, 2, y, func=AF.Silu)
        ntile = TOK // 128
        V = self.alloc_vaug(ntile)
        if not lat:
            stg = self.ring(2, [256], F32)
        for i in range(ntile):
            ps = self.proj_tm(wb, 0, 256, i // 4, i % 4)
            self.fill_vaug("act", V[i], ps[:, 0:256])
            if not lat:
                st = stg.get()
                S.copy("dve", st, ps[:, 0:256])
                S.dma("sp", O["o_nav"][l, i * 128:(i + 1) * 128, :], st)
        if os.environ.get("NA_STOP") == "2":
            return
        if not lat:
            for s in range(2):
                for hh in range(4):
                    fc, par = hh // 2, hh % 2
                    r = slice(par * 64, par * 64 + 64)
                    q = qT[fc][0][r, s * 256:(s + 1) * 256]
                    sc = [[(kT[fc][0][r, s * 256 + kt * 128: s * 256 + (kt + 1) * 128], q)] for kt in range(2)]
                    vl = [V[s * 2 + kt][:, fc, par, :] for kt in range(2)]
                    yd = y[fc][0][:, s * 256:(s + 1) * 256]
                    self.attend(sc, vl, 256, 0.125,
                                fin=lambda O_, D_, par=par, yd=yd, hh=hh: self.finish_head(O_, D_, par, 256, yd, yd))
            self.flush()
            return
        Vodd = self.alloc_vaug(ntile - 1)
        for i in range(ntile - 1):
            tok0 = i * 128 + 64
            t, off = tok0 // 512, tok0 % 512
            if off + 128 <= 512:
                ps = self.psA.get()
                for kc in range(8):
                    S.mm(ps[:, 0:256], self.h[kc][t][:, off:off + 128], wb[:, kc, 0:256], start=(kc == 0), stop=(kc == 7))
                self.fill_vaug("act", Vodd[i], ps[:, 0:256])
            else:
                ps = self.psA.get()
                for kc in range(8):
                    S.mm(ps[0:64, 0:256], self.h[kc][t][:, 448:512], wb[:, kc, 0:256], start=(kc == 0), stop=(kc == 7))
                ps2 = self.psA.get()
                for kc in range(8):
                    S.mm(ps2[0:64, 0:256], self.h[kc][t + 1][:, 0:64], wb[:, kc, 0:256], start=(kc == 0), stop=(kc == 7))
                self.fill_vaug("act", Vodd[i][0:64], ps[0:64, 0:256])
                self.fill_vaug("act", Vodd[i][64:128], ps2[0:64, 0:256])
        if os.environ.get("NA_STOP") == "3":
            return
        kc_a = A.alloc([2, 512], BF16)
        KcT = [self.nb(kc_a[:, fc, :]) for fc in range(2)]
        for fc in range(2):
            S.dma("pool", KcT[fc], I["cnakT"][l, fc * 128:(fc + 1) * 128, :])
        Vc = self.alloc_vaug(4)
        vtmp = self.ring(2, [256], BF16)
        for i in range(4):
            vt_ = vtmp.get()
            S.dma("pool", vt_, I["cnav"][l, i * 128:(i + 1) * 128, :])
            self.fill_vaug("dve", Vc[i], vt_)
        if os.environ.get("NA_STOP") == "4":
            return
        msk = self.nb(A.alloc([64], F32))
        tab = self.nb(A.alloc([60, 64], BF16))
        anti = self.nb(A.alloc([64], BF16))
        for hf in range(2):
            S.dma("sp", msk[hf * 64:(hf + 1) * 64], I["namask"])
            S.dma("pool", anti[hf * 64:(hf + 1) * 64], I["antiI"])
        tabf_r = self.ring(2, [15, 64], F32)
        for hh in range(4):
            tabf = tabf_r.get()
            src = bass.AP(tensor=I["rpbp"].tensor, offset=I["rpbp"][l, hh * 15, 0].offset,
                          ap=[[1, 64], [127, 15], [1, 64]])
            for hf in range(2):
                S.dma("sp", tabf[hf * 64:(hf + 1) * 64], src)
            S.tt("dve", tab[:, hh * 15:(hh + 1) * 15, :], tabf,
                 msk.v(lambda a: a.unsqueeze(1).broadcast_to([128, 15, 64])), ALU.add)
        if os.environ.get("NA_STOP") == "5":
            return
        for r_ in range(32 if os.environ.get("NA_VAR") != "1row" else 1):
            ws = min(max(r_ - 4, 0), 24)
            t, qo = (r_ * 64) // 512, (r_ * 64) % 512
            for hh in range(4):
                fc, par = hh // 2, hh % 2
                rr = slice(par * 64, par * 64 + 64)
                q = qT[fc][t][rr, qo:qo + 64]
                sc, vl = [], []
                for kt in range(4):
                    row0 = ws + 2 * kt
                    tk0 = row0 * 64
                    dr = row0 - r_ + 7
                    terms = [(kTf[fc][rr, tk0:tk0 + 128], q)]
                    if os.environ.get("NA_VAR") != "nobias":
                        terms.append((tab[rr, hh * 15 + dr: hh * 15 + dr + 2, :].v(lambda a: a.rearrange("p a b -> p (a b)")), anti[rr]))
                    sc.append(terms)
                    vt = V[row0 // 2] if row0 % 2 == 0 else Vodd[row0 // 2]
                    vl.append(vt[:, fc, par, :])
                for ct in range(4):
                    sc.append([(KcT[fc][rr, ct * 128:(ct + 1) * 128], q)])
                    vl.append(Vc[ct][:, fc, par, :])
                yd = y[fc][t][:, qo:qo + 64]
                self.attend(sc, vl, 64, 1.0,
                            fin=lambda O_, D_, par=par, yd=yd, hh=hh: self.finish_head(O_, D_, par, 64, yd, yd))
        self.flush()

    def rope_fm(self, dst, ps_x, ps_r, cs, sn):
        S = self.S
        t1 = self.f32pool.get()
        t2 = self.f32pool.get()
        n = ps_x.ap.shape[-1]
        p = ps_x.ap.shape[0]
        S.tt("dve", t1[0:p, 0:n], ps_x, cs, ALU.mult)
        S.tt("dve", t2[0:p, 0:n], ps_r, sn, ALU.mult)
        S.tt("dve", dst, t1[0:p, 0:n], t2[0:p, 0:n], ALU.add)

    def make_rot(self, wb, co, n, half):
        S = self.S
        wr = self.alloc([8, n], BF16)
        src = wb[:, :, co:co + n].v(lambda a: a.rearrange("p kc (g two h) -> p kc g two h", two=2, h=half))
        dst = wr.v(lambda a: a.rearrange("p kc (g two h) -> p kc g two h", two=2, h=half))
        S.copy("dve", dst[:, :, :, 0, :], src[:, :, :, 1, :])
        S.copy("act", dst[:, :, :, 1, :], src[:, :, :, 0, :])
        return wr

    def branch_wg(self, l):
        S, A, I, O, c = self.S, self.A, self.I, self.O, self.c
        lat, NT, TOK = self.lat, self.NT, self.TOK
        y = self.y[3]
        qT = self.tile2(512, BF16, 2)
        nkt = (TOK + 512) // 512 if lat else NT
        ka = A.alloc([TOK + (512 if lat else 0)], BF16)
        kT = [self.nb(ka[:, t * 512:(t + 1) * 512]) for t in range(nkt)]
        wb6 = self.load_w(l, C_ML_Z, 512)
        wb7 = self.load_w(l, C_WG_K, 512)
        if lat:
            wr6 = self.make_rot(wb6, 256, 256, 32)
            wr7 = self.make_rot(wb7, 0, 128, 32)
            rp = self.ring(2, [2, 512], F32)
            for t in range(NT):
                rt = rp.get()
                S.dma("sp", rt, I["ropewg"][:, :, t * 512:(t + 1) * 512])
                for fc in range(2):
                    ps = self.proj_fm(wb6, 256 + fc * 128, 128, t)
                    ps2 = self.proj_fm(wr6, fc * 128, 128, t)
                    self.rope_fm(qT[fc][t], ps, ps2, rt[:, 0, :], rt[:, 1, :])
                ps = self.proj_fm(wb7, 0, 128, t)
                ps2 = self.proj_fm(wr7, 0, 128, t)
                self.rope_fm(kT[t], ps, ps2, rt[:, 0, :], rt[:, 1, :])
            S.dma("pool", kT[NT], I["cwgkT"][l])
        else:
            self.fm_group(wb6, 256, 2, qT)
            stg = self.ring(2, [512], F32)
            for t in range(NT):
                ps = self.proj_fm(wb7, 0, 128, t)
                S.act(kT[t], ps, AF.Identity)
                st = stg.get()
                S.copy("dve", st, ps)
                S.dma("sp", O["o_wgk"][l], st)
        self.fm_group(wb7, 256, 2, y, func=AF.Silu)
        ntile = TOK // 128
        nvt = ntile + (4 if lat else 0)
        va = A.alloc([nvt, 128], BF16)
        V = [self.nb(va[:, i, :]) for i in range(nvt)]
        if not lat:
            stg2 = self.ring(2, [128], F32)
        for i in range(ntile):
            ps = self.proj_tm(wb7, 128, 128, i // 4, i % 4)
            S.copy("act", V[i], ps[:, 0:128])
            if not lat:
                st = stg2.get()
                S.copy("dve", st, ps[:, 0:128])
                S.dma("sp", O["o_wgv"][l, i * 128:(i + 1) * 128, :], st)
        if lat:
            for i in range(4):
                S.dma("pool", V[ntile + i], I["cwgv"][l, i * 128:(i + 1) * 128, :])
            wm = self.nb(A.alloc([2, 128], BF16))
            S.dma("pool", wm, I["wgmask"])
        v2a = A.alloc([nvt, 2, 2, 128], BF16)
        V2 = [self.nb(v2a[:, i]) for i in range(nvt)]
        for i in range(nvt):
            S.memset("dve", V2[i], 1.0)
            for kvh in range(2):
                for par in range(2):
                    S.copy("dve", V2[i][:, kvh, par, par * 64:(par + 1) * 64], V[i][:, kvh * 64:(kvh + 1) * 64])
        esink = c["esink"]
        qalt_a = A.alloc([TOK], BF16)
        qalt = [self.nb(qalt_a[:, t * 512:(t + 1) * 512]) for t in range(NT)]
        for t in range(NT):
            S.copy("act", qalt[t][0:64], qT[0][t][64:128])
            S.copy("dve", qalt[t][64:128], qT[1][t][0:64])
        self.wg_qalt = qalt
        srow = self.nb(A.alloc([4, 128], BF16, parts=1))
        S.memset("dve", srow, 0.0)
        for hh in range(4):
            dh = 64 - (hh % 2) * 64
            S.copy("dve", srow[:, hh, dh:dh + 64],
                   esink[0:1, l * 4 + hh:l * 4 + hh + 1].v(lambda a: a.broadcast_to([1, 64])))
        if not lat:
            for s in range(2):
                for hh in range(4):
                    fc, par, kvh = hh // 2, hh % 2, hh // 2
                    r = slice(par * 64, par * 64 + 64)
                    kr = slice(kvh * 64, kvh * 64 + 64)
                    q = self.wgq_rows(qT, hh, 0, s * 256, 256)
                    sc = [[(kT[0][kr, s * 256 + kt * 128: s * 256 + (kt + 1) * 128], q)] for kt in range(2)]
                    vl = [V2[s * 2 + kt][:, kvh, par, :] for kt in range(2)]
                    yd = y[fc][0][:, s * 256:(s + 1) * 256]
                    self.attend(sc, vl, 256, 0.125,
                                fin=lambda O_, D_, par=par, yd=yd, hh=hh: self.finish_head(O_, D_, par, 256, yd, yd),
                                tail=(srow[:, hh, :], c["onesrow"][0:1, 0:256]))
            self.flush()
            return
        for n in range(TOK // 128):
            t, qo = n // 4, (n % 4) * 128
            for hh in range(4):
                fc, par, kvh = hh // 2, hh % 2, hh // 2
                kr = slice(kvh * 64, kvh * 64 + 64)
                q = self.wgq_rows(qT, hh, t, qo, 128)
                sc, vl = [], []
                for dn in (-1, 0, 1):
                    kb = n + dn
                    if kb < 0 or kb >= TOK // 128:
                        continue
                    terms = [(kT[kb // 4][kr, (kb % 4) * 128:(kb % 4 + 1) * 128], q)]
                    if dn != 0:
                        terms.append((wm[:, 0 if dn < 0 else 1, :], c["identb"]))
                    sc.append(terms)
                    vl.append(V2[kb][:, kvh, par, :])
                for ct in range(4):
                    sc.append([(kT[NT][kr, ct * 128:(ct + 1) * 128], q)])
                    vl.append(V2[ntile + ct][:, kvh, par, :])
                yd = y[fc][t][:, qo:qo + 128]
                self.attend(sc, vl, 128, 0.125,
                            fin=lambda O_, D_, par=par, yd=yd, hh=hh: self.finish_head(O_, D_, par, 128, yd, yd),
                            tail=(srow[:, hh, :], c["onesrow"][0:1, 0:128]))
        self.flush()

    def wgq_rows(self, qT, hh, t, o, n):
        if hh == 0:
            return qT[0][t][0:64, o:o + n]
        if hh == 1:
            return self.wg_qalt[t][0:64, o:o + n]
        if hh == 2:
            return self.wg_qalt[t][64:128, o:o + n]
        return qT[1][t][64:128, o:o + n]

    def branch_mla(self, l):
        S, A, I, O, c = self.S, self.A, self.I, self.O, self.c
        lat, NT, TOK = self.lat, self.NT, self.TOK
        y = self.y[2]
        ntile = TOK // 128
        nkt = NT + (1 if lat else 0)
        wb5 = self.load_w(l, C_ML_QA, 416)
        wb6 = self.load_w(l, C_ML_Z, 256)
        wqn = self.nb(A.alloc([2, 4, 64], BF16))
        wqr = self.nb(A.alloc([2, 4, 32], BF16))
        wq4 = I["wqb"][l].rearrange("(kc p) (h e) -> p kc h e", p=128, e=96)
        for kc in range(2):
            S.dma("pool", wqn[:, kc], wq4[:, kc, :, 0:64])
            S.dma("pool", wqr[:, kc], wq4[:, kc, :, 64:96])
        wkn = self.nb(A.alloc([4, 64], BF16))
        wkv = self.nb(A.alloc([4, 64], BF16))
        wk4 = I["wkvb"][l].rearrange("p (h e) -> p h e", e=128)
        S.dma("pool", wkn, wk4[:, :, 0:64])
        S.dma("pool", wkv, wk4[:, :, 64:128])
        if lat:
            wqrr = self.nb(A.alloc([2, 4, 32], BF16))
            S.copy("dve", wqrr[:, :, :, 0:16], wqr[:, :, :, 16:32])
            S.copy("dve", wqrr[:, :, :, 16:32], wqr[:, :, :, 0:16])
            wr5 = self.make_rot(wb5, 384, 32, 16)
        qha = A.alloc([4, TOK], BF16)
        qh = [[self.nb(qha[0:96, hh, t * 512:(t + 1) * 512]) for t in range(NT)] for hh in range(4)]
        kha = A.alloc([4, nkt * 512], BF16)
        kh = [[self.nb(kha[0:96, hh, t * 512:(t + 1) * 512]) for t in range(nkt)] for hh in range(4)]
        kr_r = self.ring(2, [512], BF16, parts=32)
        nvt = ntile + (4 if lat else 0)
        V = self.alloc_vaug(nvt)
        ckvT = self.ring(2, [512], BF16)
        qan_r = self.ring(3, [512], BF16)
        qaf_r = self.ring(3, [512], F32)
        if lat:
            rp = self.ring(2, [2, 512], F32)
        stg = self.ring(2, [160], F32)
        gq = c["qng"]
        for t in range(NT):
            if lat:
                rt = rp.get()
                S.dma("sp", rt[0:32], I["ropeml"][:, :, t * 512:(t + 1) * 512])
            qaf = []
            for kc in range(2):
                ps = self.proj_fm(wb5, kc * 128, 128, t)
                f = qaf_r.get()
                S.copy("act", f, ps)
                qaf.append(f)
            rs = self.rms_stats(qaf, c["ones256"])
            qan = []
            for kc in range(2):
                qb = qan_r.get()
                S.stt("dve", qb, qaf[kc], gq[:, l, kc:kc + 1], rs, ALU.mult, ALU.mult)
                qan.append(qb)
            for fc in range(2):
                ps = self.psA.get()
                for kc in range(2):
                    S.mm(ps, wqn[:, kc, 2 * fc:2 * fc + 2, :].v(lambda a: a.rearrange("p a b -> p (a b)")), qan[kc],
                         start=(kc == 0), stop=(kc == 1))
                S.copy("act", qh[2 * fc][t][0:64], ps[0:64])
                S.copy("dve", qh[2 * fc + 1][t][0:64], ps[64:128])
            for hh in range(4):
                ps = self.psA.get()
                for kc in range(2):
                    S.mm(ps[0:32], wqr[:, kc, hh, :], qan[kc], start=(kc == 0), stop=(kc == 1))
                if lat:
                    ps2 = self.psA.get()
                    for kc in range(2):
                        S.mm(ps2[0:32], wqrr[:, kc, hh, :], qan[kc], start=(kc == 0), stop=(kc == 1))
                    tq = kr_r.get()
                    self.rope_fm(tq, ps[0:32], ps2[0:32], rt[0:32, 0, :], rt[0:32, 1, :])
                    S.copy("act", qh[hh][t][64:96], tq)
                else:
                    S.copy("act", qh[hh][t][64:96], ps[0:32])
            ck = ckvT.get()
            krs = kr_r.get()
            for sub in range(4):
                i = t * 4 + sub
                ps = self.proj_tm(wb5, 256, 160, t, sub)
                ssq = self.f32pool.get()
                junk = self.f32pool.get()
                S.memset("dve", ssq[:, 0:1], 0.0)
                S.act(junk[:, 0:128], ps[:, 0:128], AF.Square, accum=ssq[:, 0:1])
                S.act(ssq[:, 1:2], ssq[:, 0:1], AF.Sqrt, scale=1.0 / 128.0, bias=c["eps"][:, 0:1])
                S.recip(ssq[:, 2:3], ssq[:, 1:2])
                st = stg.get()
                S.stt("dve", st[:, 0:128], ps[:, 0:128], ssq[:, 2:3], c["kvng"][:, l, :], ALU.mult, ALU.mult)
                S.copy("dve", st[:, 128:160], ps[:, 128:160])
                if not lat:
                    S.dma("sp", O["o_kv"][l, i * 128:(i + 1) * 128, :], st)
                pt = self.psA.get()
                S.tr(pt[:, 0:128], st[:, 0:128], c["ident"])
                S.copy("act", ck[:, sub * 128:(sub + 1) * 128], pt[:, 0:128])
                if not lat:
                    pt2 = self.psA.get()
                    S.tr(pt2[0:32, 0:128], st[:, 128:160], c["ident"])
                    S.copy("act", krs[:, sub * 128:(sub + 1) * 128], pt2[0:32, 0:128])
            if lat:
                ps = self.proj_fm(wb5, 384, 32, t)
                ps2 = self.proj_fm(wr5, 0, 32, t)
                self.rope_fm(krs, ps[0:32], ps2[0:32], rt[0:32, 0, :], rt[0:32, 1, :])
            for hh in range(4):
                S.copy("act" if hh % 2 else "dve", kh[hh][t][64:96], krs)
            self.mla_expand(ck, t, kh, V, wkn, wkv)
        if lat:
            ck = ckvT.get()
            S.dma("pool", ck, I["cckvT"][l])
            for hh in range(4):
                S.dma("pool", kh[hh][NT][64:96], I["ckrT"][l])
            self.mla_expand(ck, NT, kh, V, wkn, wkv)
        self.fm_group(wb6, 0, 2, y, func=AF.Silu)
        scale = 96.0 ** -0.5
        if not lat:
            for s in range(2):
                for hh in range(4):
                    fc, par = hh // 2, hh % 2
                    r = slice(par * 64, par * 64 + 64)
                    q1 = qh[hh][0][:, s * 256:(s + 1) * 256]
                    sc = [[(kh[hh][0][:, s * 256 + kt * 128: s * 256 + (kt + 1) * 128], q1)] for kt in range(2)]
                    vl = [V[s * 2 + kt][:, fc, par, :] for kt in range(2)]
                    yd = y[fc][0][:, s * 256:(s + 1) * 256]
                    self.attend(sc, vl, 256, scale,
                                fin=lambda O_, D_, par=par, yd=yd, hh=hh: self.finish_head(O_, D_, par, 256, yd, yd))
            self.flush()
            return
        for t in range(NT):
            for hh in range(4):
                fc, par = hh // 2, hh % 2
                r = slice(par * 64, par * 64 + 64)
                q1 = qh[hh][t]
                sc, vl = [], []
                for kt in range(nkt * 4):
                    k5, ko = kt // 4, (kt % 4) * 128
                    sc.append([(kh[hh][k5][:, ko:ko + 128], q1)])
                    vl.append(V[kt][:, fc, par, :])
                self.attend(sc, vl, 512, scale,
                            fin=lambda O_, D_, par=par, yd=y[fc][t]: self.finish_head(O_, D_, par, 512, yd, yd))
        self.flush()

    def mla_expand(self, ck, t, kh, V, wkn, wkv):
        S = self.S
        for fc in range(2):
            ps = self.psA.get()
            S.mm(ps, wkn[:, 2 * fc:2 * fc + 2, :].v(lambda a: a.rearrange("p a b -> p (a b)")), ck)
            S.copy("act", kh[2 * fc][t][0:64], ps[0:64])
            S.copy("dve", kh[2 * fc + 1][t][0:64], ps[64:128])
        for sub in range(4):
            ps = self.psA.get()
            S.mm(ps[:, 0:256], ck[:, sub * 128:(sub + 1) * 128], wkv.v(lambda a: a.rearrange("p a b -> p (a b)")))
            self.fill_vaug("dve", V[t * 4 + sub], ps[:, 0:256])

    def branch_hg(self, l):
        S, A, I, O, c = self.S, self.A, self.I, self.O, self.c
        lat, NT, TOK = self.lat, self.NT, self.TOK
        y = self.y[1]
        ntile = TOK // 128
        nseq = 1 if lat else 2
        tps = ntile // nseq
        wb2 = self.load_w(l, C_HG_Q, 512)
        wb3 = self.load_w(l, C_HG_FB, 512)
        wb4 = self.load_w(l, C_HG_OG, 512)
        qs = self.tile2(512, BF16, 2)
        self.fm_group(wb2, 0, 2, qs, func=AF.Silu)
        va = A.alloc([ntile, 256], BF16)
        V = [self.nb(va[:, i, :]) for i in range(ntile)]
        for i in range(ntile):
            ps = self.proj_tm(wb3, 256, 256, i // 4, i % 4)
            S.copy("act", V[i], ps[:, 0:256])
        oacc_a = A.alloc([2, TOK], F32)
        oacc = [[self.nb(oacc_a[:, fc, i * 128:(i + 1) * 128]) for i in range(ntile)] for fc in range(2)]
        st0 = [self.alloc([2, 64], F32) for _ in range(2)]
        r_sst = self.ring(8 * (1 if lat else 2) + 4, [2, 64], F32)
        lbt1 = self.alloc([512], F32)
        omlt1 = self.alloc([512], F32)
        mk = A.top
        self.lb_tables(512, I["lbl"].rearrange("l n -> (l n)").partition_broadcast(128),
                       lambda l_: lbt1, lambda l_: omlt1, [l])
        A.release_to(mk)
        omlf, nomlf = c["oml_fm"], c["noml_fm"]
        r_sg = self.ring(2, [256], F32)
        r_t = self.ring(2, [256], F32)
        r_k = self.ring(3, [256], F32)
        r_lf = self.ring(3, [256], F32)
        r_eq = self.ring(3, [2, 128], F32)
        r_ek = self.ring(2, [2, 128], F32)
        r_ee = self.ring(2, [256], F32)
        r_kf = self.ring(3, [2, 128], F32)
        r_qd = self.ring(4, [2, 128], BF16)
        r_kd = self.ring(4, [2, 128], BF16)
        r_kh = self.ring(3, [256], BF16)
        r_vb = self.ring(3, [4, 4, 64], BF16)
        r_us = self.ring(2, [2, 4, 64], F32)
        r_sb = self.ring(3, [2, 4, 128], BF16)
        for sbz in r_sb.bufs:
            S.memset("dve", sbz, 0.0)
        r_at = self.ring(6, [128], BF16)
        hg_pending = [None]
        oacc_init = set()
        states = {}
        for s in range(nseq):
            for d in range(2):
                if lat:
                    S.dma("sp", st0[d], I["st0"][l][:, d * 128:(d + 1) * 128].rearrange("p (a b) -> p a b", a=2))
                    states[(s, d)] = st0[d]
                else:
                    z0 = self.alloc([2, 64], F32)
                    S.memset("dve", z0, 0.0)
                    states[(s, d)] = z0
        q2, q3, q4 = [], [], []

        def run_queues(drain=False):
            lim = 0 if drain else 1
            while len(q2) > lim:
                q2.pop(0)()
            while len(q3) > lim:
                q3.pop(0)()
            while len(q4) > lim:
                q4.pop(0)()

        for step in range(tps):
            for s in range(nseq):
                for d in range(2):
                    wbd, cod = (wb2, 256) if d == 0 else (wb3, 0)
                    ti = step if d == 0 else tps - 1 - step
                    i = s * tps + ti
                    t5, sub = i // 4, i % 4
                    Um = c["cmat"][:, 1 + d, :]
                    Lm = c["cmat"][:, 3 + d, :]
                    ps = self.proj_tm(wbd, cod, 256, t5, sub)
                    sg = r_sg.get()
                    S.act(sg, ps[:, 0:256], AF.Sigmoid)
                    kf = r_kf.get()
                    for fc in range(2):
                        pk = self.proj_fm(wbd, cod + fc * 128, 128, t5, ntok=128, tok0=sub * 128)
                        sgf = self.f32pool.get()
                        S.act(sgf[:, 0:128], pk[:, 0:128], AF.Sigmoid)
                        S.ts("dve", kf[:, fc, :], sgf[:, 0:128], nomlf[:, l, d * 2 + fc:d * 2 + fc + 1],
                             omlf[:, l, d * 2 + fc:d * 2 + fc + 1], ALU.mult, ALU.add)
                    tt_ = r_t.get()
                    S.tt("dve", tt_, sg, omlt1[:, d * 256:(d + 1) * 256], ALU.mult)
                    ktm = r_k.get()
                    S.tt("dve", ktm, omlt1[:, d * 256:(d + 1) * 256], tt_, ALU.subtract)
                    lf = r_lf.get()
                    S.stt("dve", lf, tt_, 1e-30, lbt1[:, d * 256:(d + 1) * 256], ALU.max, ALU.add)
                    S.act(lf, lf, AF.Ln)

                    def stage2(s=s, d=d, i=i, t5=t5, sub=sub, Um=Um, Lm=Lm, lf=lf, kf=kf, ktm=ktm):
                        pb = self.psB.get()
                        for fc in range(2):
                            S.mm(pb[:, fc * 128:(fc + 1) * 128], lf[:, fc * 128:(fc + 1) * 128], Um)
                        S.mm(pb[:, 256:512], Lm, lf)
                        eq = r_eq.get()
                        ek = r_ek.get()
                        ee = r_ee.get()
                        pbv = pb[:, 0:256].v(lambda a: a.rearrange("p (a b) -> p a b", a=2))
                        S.act(eq, pbv, AF.Exp)
                        S.act(ek, pbv, AF.Exp, scale=-1.0)
                        S.act(ee, pb[:, 256:512], AF.Exp)
                        qd = r_qd.get()
                        kd = r_kd.get()
                        for fc in range(2):
                            S.tt("dve", qd[:, fc, :], qs[fc][t5][:, sub * 128:(sub + 1) * 128], eq[:, fc, :], ALU.mult)
                        S.tt("dve", kd, kf, ek, ALU.mult)
                        kh = r_kh.get()
                        S.tt("dve", kh, ktm, ee, ALU.mult)
                        vb = r_vb.get()
                        for cch in range(4):
                            S.ts("dve", vb[:, :, cch, :], V[i].v(lambda a: a.rearrange("p (h e) -> p h e", h=4)),
                                 c["ind"][:, cch:cch + 1], None, ALU.mult)

                        def stage3():
                            us = r_us.get()
                            for pr in range(2):
                                pu = self.psA.get()
                                S.mm(pu, kh[:, pr * 128:(pr + 1) * 128],
                                     vb[:, 2 * pr:2 * pr + 2].v(lambda a: a.rearrange("p a b c -> p (a b c)")))
                                puv = pu.v(lambda a: a.rearrange("p (a b c) -> p a b c", a=2, b=4))
                                S.copy("act", us[0:64, pr], puv[0:64, 0])
                                S.copy("act", us[64:128, pr], puv[64:128, 1])
                            sb = r_sb.get()
                            state = states[(s, d)]
                            corder = range(4) if d == 0 else range(3, -1, -1)
                            for cch in corder:
                                S.copy("act", sb[0:64, :, cch, 0:64], state[0:64])
                                S.copy("act", sb[64:128, :, cch, 64:128], state[64:128])
                                dcol = cch * 32 + (31 if d == 0 else 0)
                                nxt = r_sst.get()
                                for pr in range(2):
                                    S.stt("dve", nxt[:, pr, :], state[:, pr, :], eq[:, pr, dcol:dcol + 1],
                                          us[:, pr, cch, :], ALU.mult, ALU.add)
                                state = nxt
                            states[(s, d)] = state

                            def stageC():
                                ats = []
                                for hh in range(4):
                                    fc, par = hh // 2, hh % 2
                                    r = slice(par * 64, par * 64 + 64)
                                    pa = self.psAll.get()
                                    S.mm(pa[:, 0:128], kd[r, fc, :], qd[r, fc, :])
                                    at = r_at.get()
                                    S.tt("dve", at, pa[:, 0:128], Um, ALU.mult)
                                    ats.append(at)
                                for hh in range(4):
                                    fc, par = hh // 2, hh % 2
                                    r = slice(par * 64, par * 64 + 64)
                                    at = ats[hh]
                                    po = self.psAll.get()
                                    for cch in range(4):
                                        cs = slice(cch * 32, (cch + 1) * 32)
                                        S.mm(po[:, cs], V[i][:, fc * 128:(fc + 1) * 128], at[:, cs], start=True, stop=False)
                                        S.mm(po[:, cs], sb[r, fc, cch, :], qd[r, fc, cs], start=False, stop=True)
                                    if (fc, i, par) not in oacc_init:
                                        oacc_init.add((fc, i, par))
                                        S.copy("act", oacc[fc][i][r], po[r, 0:128])
                                    else:
                                        S.tt("dve", oacc[fc][i][r], oacc[fc][i][r], po[r, 0:128], ALU.add)

                            q4.append(stageC)

                        q3.append(stage3)

                    q2.append(stage2)
                    run_queues()
        run_queues(drain=True)
        if not lat:
            for s in range(nseq):
                for d in range(2):
                    S.dma("sp", O["o_st"][l][:, (s * 2 + d) * 128:(s * 2 + d + 1) * 128].rearrange("p (a b) -> p a b", a=2),
                          states[(s, d)])
        ogz = self.ring(2, [512], F32)
        for t in range(NT):
            for fc in range(2):
                src = B(oacc_a[:, fc, t * 512:(t + 1) * 512])
                toks = [oacc[fc][t * 4 + j] for j in range(4)]
                sq = self.f32pool.get()
                S.op("act", lambda e, o=sq, i_=src: e.activation(out=o.ap, in_=i_.ap, func=AF.Square), r=toks, w=[sq])
                pn = self.psD.get()
                S.mm(pn, c["blk64"], sq)
                rs = self.f32pool.get()
                S.act(rs, pn, AF.Sqrt, bias=c["eps"][:, 0:1])
                S.recip(rs, rs)
                on = self.f32pool.get()
                S.op("dve", lambda e, o=on, i_=src, g=c["hgng"][:, l:l + 1], r_=rs:
                     e.scalar_tensor_tensor(out=o.ap, in0=i_.ap, scalar=g.ap, in1=r_.ap, op0=ALU.mult, op1=ALU.mult),
                     r=toks + [rs, c["hgng"]], w=[on])
                pg = self.proj_fm(wb4, fc * 128, 128, t)
                g1 = ogz.get()
                S.act(g1, pg, AF.Sigmoid)
                pz = self.proj_fm(wb4, 256 + fc * 128, 128, t)
                g2 = ogz.get()
                S.act(g2, pz, AF.Silu)
                S.tt("dve", g1, g1, g2, ALU.mult)
                S.tt("dve", y[fc][t], on, g1, ALU.mult)

    def phase3(self, l):
        S, A, I, c = self.S, self.A, self.I, self.c
        NT = self.NT
        Gl = self.G_l(l)
        GT = 2
        wbr_r = self.ring(2, [8, 128], BF16)
        wo_r = self.ring(2, [8, 128], BF16)
        g_r = self.ring(3, [512], BF16)
        p_r = self.ring(4, [512], F32)
        acc_r = self.ring(3, [512], F32)
        x_r = self.ring(6, [512], F32) if self.xdram else None
        ngt = min(NT, GT)
        ma = A.alloc([8, ngt * 512], BF16)
        mg = [[self.nb(ma[:, j, k * 512:(k + 1) * 512]) for k in range(ngt)] for j in range(8)]
        def ldj(j):
            wm_ = self.load_w(l, j * 512, 512, src=I["w_mg"])
            wbt_ = wbr_r.get()
            S.dma("pool", wbt_, I["wbr"][l, j])
            return wm_, wbt_

        def ldo(jo):
            wo_ = wo_r.get()
            S.dma("pool", wo_, I["wout"][l, jo])
            return wo_

        for g0 in range(0, NT, GT):
            tiles = list(range(g0, min(NT, g0 + GT)))
            nxt = ldj(0)
            for j in range(8):
                wm, wbt = nxt
                if j + 1 < 8:
                    nxt = ldj(j + 1)
                for k, t in enumerate(tiles):
                    acc = acc_r.get()
                    for b in range(4):
                        pg = self.proj_fm(wm, b * 128, 128, t, ps=self.psAll.get())
                        gs = g_r.get()
                        S.act(gs, pg, AF.Sigmoid)
                        pp = self.psAll.get()
                        for kc in range(2):
                            S.mm(pp, wbt[:, b * 2 + kc, :], self.y[b][kc][t], start=(kc == 0), stop=(kc == 1))
                        if b == 0:
                            S.tt("dve", acc, pp, gs, ALU.mult)
                        else:
                            pr = p_r.get()
                            S.tt("dve", pr, pp, gs, ALU.mult)
                            if b < 3:
                                S.tt("dve", acc, acc, pr, ALU.add)
                            else:
                                S.tt("dve", mg[j][k], acc, pr, ALU.add)
            nxo = ldo(0)
            for jo in range(8):
                wo = nxo
                if jo + 1 < 8:
                    nxo = ldo(jo + 1)
                for k, t in enumerate(tiles):
                    po = self.psAll.get()
                    for kc in range(8):
                        S.mm(po, wo[:, kc, :], mg[kc][k], start=(kc == 0), stop=(kc == 7))
                    if self.xdram:
                        xb = x_r.get()
                        S.dma("sp", xb, self.xcur[jo][t])
                        S.stt("dve", xb, po, Gl[:, jo:jo + 1], xb, ALU.mult, ALU.add)
                        S.dma("act", self.xd[jo][t], xb)
                    else:
                        S.stt("dve", self.x[jo][t], po, Gl[:, jo:jo + 1], self.x[jo][t], ALU.mult, ALU.add)

    def final_norm(self, dst):
        S, c = self.S, self.c
        stg = self.ring(3, [512], F32)
        ring = self.ring(8, [512], F32) if self.xdram else None
        for t in range(self.NT):
            xt = self.x_tiles(t, ring)
            rs = self.rms_stats(xt, c["onesD"])
            for kc in range(8):
                st = stg.get()
                S.stt("dve", st, xt[kc], c["fg"][:, kc:kc + 1], rs, ALU.mult, ALU.mult)
                S.dma("pool" if self.xdram else "sp", dst[kc * 128:(kc + 1) * 128, t * 512:(t + 1) * 512], st)


def _consts():
    idx = np.arange(128)
    ch = idx // 32
    same = ch[:, None] == ch[None, :]
    ident = np.eye(128, dtype=np.float32)
    Uf = (same & (idx[:, None] <= idx[None, :])).astype(np.float32)
    Ub = (same & (idx[:, None] >= idx[None, :])).astype(np.float32)
    Lf = (same & (idx[:, None] > idx[None, :])).astype(np.float32)
    Lb = (same & (idx[:, None] < idx[None, :])).astype(np.float32)
    cmat = np.stack([ident, Uf, Ub, Lf, Lb], axis=1).astype(np.float32)
    ind = (ch[:, None] == np.arange(4)[None, :]).astype(np.float32)
    cq = 63 - np.arange(64)
    qstart = np.clip(cq - 8, 0, 48)
    ck = np.arange(64)
    ok = (ck[None, :] >= qstart[:, None]) & (ck[None, :] < qstart[:, None] + 16)
    namask = np.where(ok, 0.0, NEG).astype(np.float32)
    antiI = np.zeros((64, 64), np.float32)
    antiI[63 - np.arange(64), np.arange(64)] = 1.0
    a = np.arange(128)
    prevT = np.where(a[:, None] <= a[None, :], 0.0, NEG).astype(np.float32)
    nextT = np.where(a[None, :] <= a[:, None], 0.0, NEG).astype(np.float32)
    wgmask = np.stack([prevT, nextT], axis=1).astype(np.float32)
    tpos = np.arange(2048)
    row = (tpos // 64).astype(np.float32)
    col = (tpos % 64).astype(np.float32)

    def tab(rot):
        nf = rot // 4
        inv = (np.float32(10000.0) ** (-np.arange(nf, dtype=np.float32) / np.float32(nf))).astype(np.float32)
        ang = np.concatenate([row[:, None] * inv, col[:, None] * inv], axis=-1).astype(np.float32)
        cs, sn = np.cos(ang).astype(np.float32), np.sin(ang).astype(np.float32)
        cs2 = np.concatenate([cs, cs], axis=1).T
        sn2 = np.concatenate([-sn, sn], axis=1).T
        return cs2, sn2

    c64, s64 = tab(64)
    ropewg = np.stack([np.concatenate([c64, c64], 0), np.concatenate([s64, s64], 0)], axis=1)
    c32, s32 = tab(32)
    ropeml = np.stack([c32, s32], axis=1)
    return dict(cmat=cmat, ind=ind, namask=namask, antiI=antiI, wgmask=wgmask,
                ropewg=np.ascontiguousarray(ropewg, np.float32), ropeml=np.ascontiguousarray(ropeml, np.float32))


_CACHE = {}


def _get_nc(key, **kw):
    if key not in _CACHE:
        b = Builder(**kw)
        nc = b.build()
        _CACHE[key] = (b, nc)
    return _CACHE[key]


def _host_inputs(inp):
    f = lambda a: np.ascontiguousarray(a, dtype=np.float32)
    sh = {}
    sh["w_mod"] = f(inp["w_mod"])
    sh["bmodr"] = f(inp["b_mod"].reshape(4, 24, 128).transpose(2, 0, 1))
    sh["ng"] = f(inp["norm_g"].reshape(4, 8, 128).transpose(2, 0, 1))
    sh["fg"] = f(inp["final_g"].reshape(8, 128).T)
    sh["w_in"] = f(inp["w_in"])
    wm = inp["w_in"][:, :, C_MERGE:].reshape(4, 1024, 4, 8, 128).transpose(0, 1, 3, 2, 4).reshape(4, 1024, 4096)
    sh["w_mg"] = f(wm)
    wb = inp["w_branch"].reshape(4, 4, 2, 128, 8, 128).transpose(0, 4, 3, 1, 2, 5).reshape(4, 8, 128, 8, 128)
    sh["wbr"] = f(wb)
    wo = inp["w_out"].reshape(4, 8, 128, 8, 128).transpose(0, 3, 2, 1, 4)
    sh["wout"] = f(wo)
    sh["lbl"] = f(inp["hg_lb_logits"].reshape(4, 512))
    sh["lblT"] = f(inp["hg_lb_logits"].reshape(4, 2, 2, 128).transpose(3, 0, 1, 2).reshape(128, 4, 4))
    sh["hgng"] = f(np.concatenate([inp["hg_norm_g"], inp["hg_norm_g"]], axis=1).T)
    sh["qng"] = f(inp["mla_q_norm_g"].reshape(4, 2, 128).transpose(2, 0, 1))
    sh["kvng"] = f(inp["mla_kv_norm_g"])
    sh["wqb"] = f(inp["mla_w_qb"])
    sh["wkvb"] = f(inp["mla_w_kvb"])
    sh["sink"] = f(inp["wg_sink"].reshape(1, 16))
    rp = np.zeros((4, 60, 127), np.float32)
    rp[:, :, 48:48 + 31] = inp["na_rpb"].reshape(4, 60, 31)
    sh["rpbp"] = rp
    sh.update(_consts())
    maps = []
    for i in range(8):
        b = i // 2
        m = dict(sh)
        m["xT_c"] = f(inp["x_prompt"][2 * i:2 * i + 2].reshape(512, 1024).T)
        m["xT_l"] = f(inp["x_sample"][b].T)
        m["cc"] = f(np.stack([inp["c_ctx"], inp["c"][b]], axis=1))
        m["cnakT"] = f(inp["cache_na_k"][b].reshape(4, 512, 256).transpose(0, 2, 1))
        m["cnav"] = f(inp["cache_na_v"][b].reshape(4, 512, 256))
        st = inp["state_hgrn"][b].reshape(4, 2, 2, 2, 64, 64).transpose(0, 3, 4, 1, 2, 5).reshape(4, 128, 256)
        m["st0"] = f(st)
        m["cckvT"] = f(inp["cache_mla_ckv"][b].transpose(0, 2, 1))
        m["ckrT"] = f(inp["cache_mla_krope"][b].transpose(0, 2, 1))
        m["cwgkT"] = f(inp["cache_wg_k"][b].reshape(4, 512, 128).transpose(0, 2, 1))
        m["cwgv"] = f(inp["cache_wg_v"][b].reshape(4, 512, 128))
        maps.append(m)
    return maps


def _assemble(res):
    y_prompt = np.empty((16, 256, 1024), np.float32)
    y_sample = np.empty((4, 2048, 1024), np.float32)
    na_k = np.empty((16, 4, 256, 4, 64), np.float32)
    na_v = np.empty((16, 4, 256, 4, 64), np.float32)
    st = np.empty((16, 4, 2, 4, 64, 64), np.float32)
    ckv = np.empty((16, 4, 256, 128), np.float32)
    kr = np.empty((16, 4, 256, 32), np.float32)
    wgk = np.empty((16, 4, 256, 2, 64), np.float32)
    wgv = np.empty((16, 4, 256, 2, 64), np.float32)
    for i in range(8):
        r = res[i]
        y_prompt[2 * i:2 * i + 2] = r["ypT"].T.reshape(2, 256, 1024)
        if i % 2 == 0:
            y_sample[i // 2] = r["ysT"].T
        na_k[2 * i:2 * i + 2] = r["o_nak"].transpose(0, 2, 1).reshape(4, 2, 256, 4, 64).transpose(1, 0, 2, 3, 4)
        na_v[2 * i:2 * i + 2] = r["o_nav"].reshape(4, 2, 256, 4, 64).transpose(1, 0, 2, 3, 4)
        s_ = r["o_st"].reshape(4, 2, 64, 2, 2, 2, 64)
        s_ = s_.transpose(3, 0, 4, 5, 1, 2, 6).reshape(2, 4, 2, 4, 64, 64)
        st[2 * i:2 * i + 2] = s_
        kv = r["o_kv"].reshape(4, 2, 256, 160).transpose(1, 0, 2, 3)
        ckv[2 * i:2 * i + 2] = kv[..., :128]
        kr[2 * i:2 * i + 2] = kv[..., 128:]
        wgk[2 * i:2 * i + 2] = r["o_wgk"].transpose(0, 2, 1).reshape(4, 2, 256, 2, 64).transpose(1, 0, 2, 3, 4)
        wgv[2 * i:2 * i + 2] = r["o_wgv"].reshape(4, 2, 256, 2, 64).transpose(1, 0, 2, 3, 4)
    return (y_prompt, y_sample, na_k, na_v, st, ckv, kr, wgk, wgv)


def kernel(**inputs):
    inp = {k: np.asarray(v) for k, v in inputs.items()}
    b, nc = _get_nc("full")
    maps = _host_inputs(inp)
    maps = [{k: m[k] for k in b.in_shapes} for m in maps]
    res = run_bass_kernel_spmd(nc, maps, core_ids=list(range(8)))
    return _assemble(res.results)
```

```python
import contextlib
import math
import os
import numpy as np
import concourse.bass as bass
import concourse.mybir as mybir
from concourse.bass_utils import run_bass_kernel_spmd

F32 = mybir.dt.float32
BF16 = mybir.dt.bfloat16
AF = mybir.ActivationFunctionType
ALU = mybir.AluOpType

SEM_LIMIT = 30000
DEPTH = 4
EPS = 1e-6
NEG = -30000.0


class Tok:
    __slots__ = ("w", "rs", "excl")

    def __init__(self, w=None):
        self.w = w
        self.rs = []
        self.excl = False


class Op:
    __slots__ = ("eng", "fn", "deps", "inc", "sem", "val", "dma", "prev")

    def __init__(self, eng, fn, dma):
        self.eng = eng
        self.fn = fn
        self.deps = []
        self.inc = False
        self.sem = None
        self.val = 0
        self.dma = dma


class B:
    __slots__ = ("ap", "tok")

    def __init__(self, ap, tok=None):
        self.ap = ap
        self.tok = tok if tok is not None else Tok()

    def __getitem__(self, k):
        return B(self.ap[k], self.tok)

    def v(self, f):
        return B(f(self.ap), self.tok)


class Sched:
    ENGS = ("pe", "act", "dve", "pool", "sp")

    def __init__(self, nc, es):
        self.nc = nc
        self.es = es
        self.ops = {e: [] for e in self.ENGS}
        self.nsem = 0
        self.barrier_op = None

    def new_sem(self):
        s = self.es.enter_context(self.nc.semaphore("s%d" % self.nsem))
        self.nsem += 1
        return s

    def _add(self, eng, fn, r, w, dma):
        op = Op(eng, fn, dma)
        deps = []
        for t in r:
            if t.w is not None:
                deps.append(t.w)
            if t.excl:
                deps.extend(o for o in t.rs if o.eng != eng)
        for t in w:
            if t.w is not None:
                deps.append(t.w)
            deps.extend(t.rs)
        seen = set()
        for d in deps:
            if id(d) in seen:
                continue
            seen.add(id(d))
            if d.eng == "pe" and eng == "pe" and not d.dma and not dma:
                continue
            op.deps.append(d)
            d.inc = True
        for t in r:
            t.rs.append(op)
        for t in w:
            t.w = op
            t.rs = []
        self.ops[eng].append(op)
        return op

    def op(self, eng, fn, r=(), w=()):
        return self._add(eng, fn, [b.tok for b in r], [b.tok for b in w], False)

    def dma(self, eng, out, in_, r=(), w=()):
        oa = out.ap if isinstance(out, B) else out
        ia = in_.ap if isinstance(in_, B) else in_
        rr = [b.tok for b in r] + ([in_.tok] if isinstance(in_, B) else [])
        ww = [b.tok for b in w] + ([out.tok] if isinstance(out, B) else [])
        return self._add(eng, lambda e: e.dma_start(out=oa, in_=ia), rr, ww, True)

    def mm(self, out, lhsT, rhs, start=True, stop=True):
        return self.op("pe", lambda e: e.matmul(out.ap, lhsT=lhsT.ap, rhs=rhs.ap, start=start, stop=stop),
                       r=[lhsT, rhs], w=[out])

    def tr(self, out, in_, ident):
        return self.op("pe", lambda e: e.transpose(out.ap, in_.ap, ident.ap), r=[in_, ident], w=[out])

    def act(self, out, in_, func, scale=1.0, bias=None, accum=None, r=()):
        rr = [in_] + list(r)
        kw = {}
        if isinstance(scale, B):
            rr.append(scale)
            kw["scale"] = scale.ap
        else:
            kw["scale"] = scale
        if bias is not None:
            rr.append(bias)
            kw["bias"] = bias.ap
        ww = [out]
        if accum is not None:
            ww.append(accum)
            kw["accum_out"] = accum.ap
        return self.op("act", lambda e: e.activation(out=out.ap, in_=in_.ap, func=func, **kw), r=rr, w=ww)

    def tt(self, eng, out, in0, in1, op):
        return self.op(eng, lambda e: e.tensor_tensor(out=out.ap, in0=in0.ap, in1=in1.ap, op=op),
                       r=[in0, in1], w=[out])

    def ts(self, eng, out, in0, s1, s2=None, op0=ALU.mult, op1=None):
        rr = [in0]
        a1 = s1
        a2 = s2
        if isinstance(s1, B):
            rr.append(s1)
            a1 = s1.ap
        if isinstance(s2, B):
            rr.append(s2)
            a2 = s2.ap
        if op1 is None:
            return self.op(eng, lambda e: e.tensor_scalar(out=out.ap, in0=in0.ap, scalar1=a1, scalar2=None,
                                                          op0=op0), r=rr, w=[out])
        return self.op(eng, lambda e: e.tensor_scalar(out=out.ap, in0=in0.ap, scalar1=a1, scalar2=a2,
                                                      op0=op0, op1=op1), r=rr, w=[out])

    def stt(self, eng, out, in0, scalar, in1, op0, op1):
        rr = [in0, in1]
        a = scalar
        if isinstance(scalar, B):
            rr.append(scalar)
            a = scalar.ap
        return self.op(eng, lambda e: e.scalar_tensor_tensor(out=out.ap, in0=in0.ap, scalar=a, in1=in1.ap,
                                                             op0=op0, op1=op1), r=rr, w=[out])

    def copy(self, eng, out, in_):
        if eng == "act":
            return self.act(out, in_, AF.Identity)
        return self.op(eng, lambda e: e.tensor_copy(out=out.ap, in_=in_.ap), r=[in_], w=[out])

    def recip(self, out, in_):
        return self.op("dve", lambda e: e.reciprocal(out=out.ap, in_=in_.ap), r=[in_], w=[out])

    def memset(self, eng, out, val):
        return self.op(eng, lambda e: e.memset(out.ap, val), w=[out])

    def barrier(self, toks):
        bs = [B(None, t) for t in toks]
        scr = self.scratch
        op = self.op("pool", lambda e: e.memset(scr.ap, 0.0), r=bs, w=bs + [scr])
        return op

    def emit(self):
        nc = self.nc
        NRING = {"sp": 40, "pool": 40, "act": 16, "pe": 1, "dve": 1}
        for e in self.ENGS:
            csem = None
            ccnt = 0
            ring = []
            rcnt = []
            nd = 0
            for op in self.ops[e]:
                if op.dma:
                    op.inc = True
                    if len(ring) < NRING[e]:
                        ring.append(self.new_sem())
                        rcnt.append(0)
                    k = nd % NRING[e]
                    nd += 1
                    op.prev = (ring[k], rcnt[k])
                    rcnt[k] += 16
                    op.sem, op.val = ring[k], rcnt[k]
                elif op.inc:
                    if csem is None or ccnt + 1 > SEM_LIMIT:
                        csem = self.new_sem()
                        ccnt = 0
                    ccnt += 1
                    op.sem, op.val = csem, ccnt
        engobj = {"pe": "tensor", "act": "scalar", "dve": "vector", "pool": "gpsimd", "sp": "sync"}
        stats = {}
        with nc.Block() as block:
            for e in self.ENGS:
                ops = self.ops[e]

                def body(eng, ops=ops, e=e):
                    waited = {}
                    nw = 0
                    for op in ops:
                        need = {}
                        deps = [(d.sem, d.val) for d in op.deps]
                        if op.dma and op.prev[1] > 0:
                            deps.append(op.prev)
                        for (s_, v_) in deps:
                            k = id(s_)
                            if waited.get(k, 0) >= v_:
                                continue
                            if k not in need or need[k][1] < v_:
                                need[k] = (s_, v_)
                        for k, (s, v) in need.items():
                            eng.wait_ge(s, v)
                            waited[k] = v
                            nw += 1
                        ins = op.fn(eng)
                        if op.inc:
                            ins.then_inc(op.sem, 16 if op.dma else 1)
                    if e == "sp":
                        for ee in self.ENGS:
                            last = {}
                            for op in self.ops[ee]:
                                if op.dma:
                                    last[id(op.sem)] = (op.sem, op.val)
                            for k, (s, v) in last.items():
                                if waited.get(k, 0) < v:
                                    eng.wait_ge(s, v)
                    stats[e] = (len(ops), nw)

                getattr(block, engobj[e])(body)
        return stats


class Arena:
    def __init__(self, ap, nwords):
        self.ap = ap
        self.n = nwords
        self.top = 0
        self.peak = 0
        self.live = []
        self.dead = []
        self.last = None

    def release_to(self, top):
        keep = []
        for blk in self.live:
            (self.dead if blk[0] >= top else keep).append(blk)
        self.live = keep
        self.top = top

    def alloc(self, shape, dt, parts=128):
        nel = 1
        for s in shape:
            nel *= s
        words = (nel * (4 if dt == F32 else 2) + 3) // 4
        wal = (words + 7) // 8 * 8
        off = self.top
        self.top += wal
        self.peak = max(self.peak, self.top)
        assert self.top <= self.n, "arena overflow %d > %d" % (self.top, self.n)
        end = off + wal
        inherit = []
        nd = []
        for blk in self.dead:
            if blk[0] < end and off < blk[1]:
                for t in blk[2]:
                    if t.w is not None:
                        inherit.append(t.w)
                    inherit.extend(t.rs)
                if blk[0] >= off and blk[1] <= end:
                    continue
            nd.append(blk)
        self.dead = nd
        uniq = []
        seen = set()
        for o in inherit:
            if id(o) not in seen:
                seen.add(id(o))
                uniq.append(o)
        self.last = [off, end, [], uniq]
        self.live.append(self.last)
        v = self.ap[:, off:off + words]
        if dt != F32:
            v = v.bitcast(dt)
        if len(shape) == 2:
            v = v.rearrange("p (a b) -> p a b", a=shape[0])
        elif len(shape) == 3:
            v = v.rearrange("p (a b c) -> p a b c", a=shape[0], b=shape[1])
        elif len(shape) == 4:
            v = v.rearrange("p (a b c d) -> p a b c d", a=shape[0], b=shape[1], c=shape[2])
        if parts != 128:
            v = v[0:parts]
        return v

    def newtok(self):
        t = Tok()
        t.rs = list(self.last[3])
        self.last[2].append(t)
        return t


class Ring:
    def __init__(self, bufs):
        self.bufs = bufs
        self.i = 0

    def get(self):
        b = self.bufs[self.i % len(self.bufs)]
        self.i += 1
        return b


C_NA_Q, C_NA_K, C_NA_V, C_NA_Z = 0, 256, 512, 768
C_HG_Q, C_HG_FF, C_HG_FB, C_HG_I, C_HG_OG, C_HG_Z = 1024, 1280, 1536, 1792, 2048, 2304
C_ML_QA, C_ML_KVA, C_ML_Z = 2560, 2816, 2976
C_WG_Q, C_WG_K, C_WG_V, C_WG_Z = 3232, 3488, 3616, 3744
C_MERGE = 4000


class Builder:
    def __init__(self, run_ctx=True, run_lat=True, nlayers=DEPTH, dbg=None):
        self.dbg = dbg
        self.run_ctx = run_ctx
        self.run_lat = run_lat
        self.nlayers = nlayers

    def dram_in(self, name, shape):
        self.in_shapes[name] = shape
        return self.nc.dram_tensor(name, list(shape), F32, kind="ExternalInput").ap()

    def dram_out(self, name, shape):
        self.out_shapes[name] = shape
        return self.nc.dram_tensor(name, list(shape), F32, kind="ExternalOutput").ap()

    def build(self):
        nc = bass.Bass("TRN2", target_bir_lowering=False)
        self.nc = nc
        self.in_shapes = {}
        self.out_shapes = {}
        di, do = self.dram_in, self.dram_out
        I = {}
        for name, shape in [
            ("xT_c", (1024, 512)), ("xT_l", (1024, 2048)), ("cc", (1024, 2)),
            ("w_mod", (4, 1024, 3072)), ("bmodr", (128, 4, 24)), ("ng", (128, 4, 8)), ("fg", (128, 8)),
            ("w_in", (4, 1024, 8096)), ("w_mg", (4, 1024, 4096)),
            ("wbr", (4, 8, 128, 8, 128)), ("wout", (4, 8, 128, 8, 128)),
            ("lbl", (4, 512)), ("lblT", (128, 4, 4)), ("hgng", (128, 4)), ("qng", (128, 4, 2)), ("kvng", (4, 128)),
            ("wqb", (4, 256, 384)), ("wkvb", (4, 128, 512)), ("sink", (1, 16)), ("rpbp", (4, 60, 127)),
            ("cnakT", (4, 256, 512)), ("cnav", (4, 512, 256)), ("st0", (4, 128, 256)),
            ("cckvT", (4, 128, 512)), ("ckrT", (4, 32, 512)), ("cwgkT", (4, 128, 512)), ("cwgv", (4, 512, 128)),
            ("cmat", (128, 5, 128)), ("ind", (128, 4)), ("namask", (64, 64)), ("antiI", (64, 64)),
            ("wgmask", (128, 2, 128)), ("ropewg", (128, 2, 2048)), ("ropeml", (32, 2, 2048)),
        ]:
            I[name] = di(name, shape)
        O = {}
        for name, shape in [
            ("ypT", (1024, 512)), ("ysT", (1024, 2048)), ("o_nak", (4, 256, 512)), ("o_nav", (4, 512, 256)),
            ("o_st", (4, 128, 512)), ("o_kv", (4, 512, 160)), ("o_wgk", (4, 128, 512)), ("o_wgv", (4, 512, 128)),
        ]:
            O[name] = do(name, shape)
        self.I, self.O = I, O
        self.xscr = nc.dram_tensor("xscr", [1024, 2048], F32, kind="Internal").ap()

        with contextlib.ExitStack() as es:
            S = Sched(nc, es)
            self.S = S
            NW = 206 * 256
            arena_t = es.enter_context(nc.sbuf_tensor("arena", [128, NW], F32))
            self.A = Arena(arena_t[:, :], NW)
            self.ps = []
            for i in range(8):
                t = es.enter_context(nc.psum_tensor("ps%d" % i, [128, 512], F32))
                self.ps.append(B(t[:, :]))
                self.ps[-1].tok.excl = True
            self.psA = Ring(self.ps[0:2])
            self.psB = Ring(self.ps[2:4])
            self.psC = Ring(self.ps[4:6])
            self.psD = Ring(self.ps[6:8])
            self.psS = Ring(self.ps[0:4])
            self.psO = Ring(self.ps[4:8])
            self.psAll = Ring(self.ps)
            self.pendq = []
            self.setup_consts()
            base_top = self.A.top
            self.prologue_tables()
            self.prologue_mod([0] if self.run_ctx else list(range(DEPTH)))
            if self.run_ctx:
                self.A.release_to(base_top)
                self.run_pass(False)
            if self.run_lat:
                self.A.release_to(base_top)
                self.run_pass(True)
            self.stats = S.emit()
            self.stats["peak_words"] = self.A.peak
        return nc

    def nb(self, ap):
        return B(ap, self.A.newtok())

    def mark(self, name):
        if not hasattr(self, "marks"):
            self.marks = []
        self.marks.append((("lat " if getattr(self, "lat", False) else "ctx ") + name, len(self.S.ops["pe"])))

    def all_toks(self):
        return list(self.pass_toks) + [p.tok for p in self.ps]

    def alloc(self, shape, dt, parts=128):
        return self.nb(self.A.alloc(shape, dt, parts))

    def ring(self, n, shape, dt, parts=128):
        return Ring([self.alloc(shape, dt, parts) for _ in range(n)])

    def setup_consts(self):
        S, A, I = self.S, self.A, self.I
        S.scratch = B(A.alloc([8], F32))
        c = {}
        c["cmat"] = B(A.alloc([5, 128], F32))
        S.dma("sp", c["cmat"], I["cmat"])
        c["ident"] = c["cmat"][:, 0, :]
        c["ind"] = B(A.alloc([4], F32))
        S.dma("sp", c["ind"], I["ind"])
        c["onesD"] = B(A.alloc([128], F32))
        S.memset("dve", c["onesD"], 1.0 / 1024.0)
        c["ones256"] = B(A.alloc([128], F32))
        S.memset("dve", c["ones256"], 1.0 / 256.0)
        c["blk64"] = B(A.alloc([128], F32))
        S.memset("dve", c["blk64"], 0.0)
        S.memset("dve", c["blk64"][0:64, 0:64], 1.0 / 64.0)
        S.memset("dve", c["blk64"][64:128, 64:128], 1.0 / 64.0)
        c["eps"] = B(A.alloc([1], F32))
        S.memset("dve", c["eps"], EPS)
        c["onesb"] = B(A.alloc([128], BF16))
        S.memset("dve", c["onesb"], 1.0)
        c["onesrow"] = B(A.alloc([512], BF16))
        S.memset("pool", c["onesrow"], 1.0)
        c["identb"] = B(A.alloc([128], BF16))
        S.dma("pool", c["identb"], I["cmat"][:, 0, :])
        c["ng"] = B(A.alloc([4, 8], F32))
        S.dma("sp", c["ng"], I["ng"])
        c["fg"] = B(A.alloc([8], F32))
        S.dma("sp", c["fg"], I["fg"])
        c["bmodr"] = B(A.alloc([4, 24], F32))
        S.dma("sp", c["bmodr"], I["bmodr"])
        c["hgng"] = B(A.alloc([4], F32))
        S.dma("sp", c["hgng"], I["hgng"])
        c["qng"] = B(A.alloc([4, 2], F32))
        S.dma("sp", c["qng"], I["qng"])
        c["kvng"] = B(A.alloc([4, 128], F32))
        S.dma("sp", c["kvng"].v(lambda a: a.rearrange("p a b -> p (a b)")),
              I["kvng"].rearrange("l n -> (l n)").partition_broadcast(128))
        c["esink"] = B(A.alloc([16], F32))
        S.dma("sp", c["esink"], I["sink"].rearrange("a n -> (a n)").partition_broadcast(128))
        S.act(c["esink"], c["esink"], AF.Exp)
        c["lb_fm"] = B(A.alloc([4, 4], F32))
        c["oml_fm"] = B(A.alloc([4, 4], F32))
        c["noml_fm"] = B(A.alloc([4, 4], F32))
        c["ccs"] = B(A.alloc([8, 2], F32))
        S.dma("sp", c["ccs"], I["cc"].rearrange("(kc p) c -> p kc c", p=128))
        S.act(c["ccs"], c["ccs"], AF.Silu)
        self.Am = B(A.alloc([2, 4, 8], F32))
        self.Bm = B(A.alloc([2, 4, 8], F32))
        self.Gm = B(A.alloc([2, 4, 8], F32))
        self.c = c

    def lb_tables(self, n, src, lb, oml, layers):
        S = self.S
        raw = self.alloc([4, n], F32)
        S.dma("sp", raw.v(lambda a: a.rearrange("p a b -> p (a b)")), src)
        S.act(raw, raw, AF.Exp)
        tot = self.alloc([n], F32)
        S.tt("dve", tot, raw[:, 0, :], raw[:, 1, :], ALU.add)
        S.tt("dve", tot, tot, raw[:, 2, :], ALU.add)
        S.tt("dve", tot, tot, raw[:, 3, :], ALU.add)
        S.recip(tot, tot)
        acc = self.alloc([n], F32)
        S.memset("dve", acc, 0.0)
        pl = self.alloc([n], F32)
        for l in range(4):
            if l > 0:
                S.tt("dve", pl, raw[:, l, :], tot, ALU.mult)
                S.tt("dve", acc, acc, pl, ALU.add)
            if l in layers:
                S.copy("dve", lb(l), acc)
                S.ts("dve", oml(l), acc, -1.0, 1.0, ALU.mult, ALU.add)

    def prologue_tables(self):
        S, c, I = self.S, self.c, self.I
        self.lb_tables(4, I["lblT"].rearrange("p a b -> p (a b)"), lambda l: c["lb_fm"][:, l, :],
                       lambda l: c["oml_fm"][:, l, :], range(4))
        S.ts("dve", c["noml_fm"], c["oml_fm"], -1.0)

    def prologue_mod(self, layers):
        S, A, I, c = self.S, self.A, self.I, self.c
        ccs = c["ccs"]
        mod = self.alloc([4, 24, 2], F32)
        wring = self.ring(2, [8, 768], F32)
        for l in layers:
            for pc in range(4):
                wt = wring.get()
                S.dma("sp", wt, I["w_mod"][l][:, pc * 768:(pc + 1) * 768].rearrange("(kc p) n -> p kc n", p=128))
                ps = self.psAll.get()
                for jj in range(6):
                    for kc in range(8):
                        S.mm(ps[:, jj * 2:(jj + 1) * 2], wt[:, kc, jj * 128:(jj + 1) * 128], ccs[:, kc, :],
                             start=(kc == 0), stop=(kc == 7))
                S.tt("dve", mod[:, l, pc * 6:(pc + 1) * 6, :],
                     ps[:, 0:12].v(lambda a: a.rearrange("p (j c) -> p j c", c=2)),
                     c["bmodr"][:, l, pc * 6:(pc + 1) * 6].v(lambda a: a.unsqueeze(2).broadcast_to([128, 6, 2])),
                     ALU.add)
        Am, Bm, Gm = self.Am, self.Bm, self.Gm
        for p in range(2):
            for l in layers:
                S.ts("dve", Am[:, p, l], mod[:, l, 8:16, p], 1.0, None, ALU.add)
                S.tt("dve", Am[:, p, l], Am[:, p, l], c["ng"][:, l], ALU.mult)
                S.copy("dve", Bm[:, p, l], mod[:, l, 0:8, p])
                S.copy("dve", Gm[:, p, l], mod[:, l, 16:24, p])

    def run_pass(self, lat):
        S, A, I, O, c = self.S, self.A, self.I, self.O, self.c
        TOK = 2048 if lat else 512
        NT = TOK // 512
        pidx = 1 if lat else 0
        self.lat, self.TOK, self.NT, self.pidx = lat, TOK, NT, pidx
        xsrc = I["xT_l"] if lat else I["xT_c"]
        self.xdram = lat
        if lat:
            self.xd = [[B(self.xscr[kc * 128:(kc + 1) * 128, t * 512:(t + 1) * 512]) for t in range(NT)]
                       for kc in range(8)]
            self.xin = [[B(xsrc[kc * 128:(kc + 1) * 128, t * 512:(t + 1) * 512]) for t in range(NT)]
                        for kc in range(8)]
            self.xcur = self.xin
        else:
            xs = A.alloc([8, TOK], F32)
            self.x = [[self.nb(xs[:, kc, t * 512:(t + 1) * 512]) for t in range(NT)] for kc in range(8)]
            for kc in range(8):
                for t in range(NT):
                    S.dma("sp", self.x[kc][t], xsrc[kc * 128:(kc + 1) * 128, t * 512:(t + 1) * 512])
        hs = A.alloc([8, TOK], BF16)
        self.h = [[self.nb(hs[:, kc, t * 512:(t + 1) * 512]) for t in range(NT)] for kc in range(8)]
        self.hall = hs
        self.y = []
        for b in range(4):
            ya = A.alloc([2, TOK], BF16)
            self.y.append([[self.nb(ya[:, fc, t * 512:(t + 1) * 512]) for t in range(NT)] for fc in range(2)])
        self.wpool = self.ring(3, [8, 512], BF16)
        self.f32pool = self.ring(4, [512], F32)
        self.rspool = self.ring(2, [512], F32)
        self.bfpool = self.ring(4, [512], BF16)
        self.A_l = lambda l: self.Am[:, pidx, l, :]
        self.B_l = lambda l: self.Bm[:, pidx, l, :]
        self.G_l = lambda l: self.Gm[:, pidx, l, :]
        ptop = A.top
        for l in range(self.nlayers):
            A.release_to(ptop)
            self.mark("L%d make_h" % l)
            self.make_h(l)
            on = (lambda k: True) if self.dbg is None else (lambda k: k in self.dbg.split(","))
            if on("na"):
                self.mark("L%d na" % l)
                self.branch_na(l)
            A.release_to(ptop)
            if (not lat) and l == 0 and self.run_ctx:
                self.prologue_mod([1, 2, 3])
                A.release_to(ptop)
            if on("hg"):
                self.mark("L%d hg" % l)
                self.branch_hg(l)
            A.release_to(ptop)
            if on("mla"):
                self.mark("L%d mla" % l)
                self.branch_mla(l)
            A.release_to(ptop)
            if on("wg"):
                self.mark("L%d wg" % l)
                self.branch_wg(l)
            A.release_to(ptop)
            if on("p3"):
                self.mark("L%d p3" % l)
                self.phase3(l)
                if self.xdram:
                    self.xcur = self.xd
        A.release_to(ptop)
        self.mark("final")
        self.final_norm(O["ysT"] if lat else O["ypT"])

    def rms_stats(self, chunks, ones, nparts=128):
        S, c = self.S, self.c
        n = chunks[0].ap.shape[-1]
        ps = self.psD.get()
        for i, ch in enumerate(chunks):
            sq = self.f32pool.get()
            S.act(sq[:, 0:n], ch, AF.Square)
            S.mm(ps[:, 0:n], ones, sq[:, 0:n], start=(i == 0), stop=(i == len(chunks) - 1))
        rs = self.rspool.get()
        S.act(rs[:, 0:n], ps[:, 0:n], AF.Sqrt, bias=c["eps"][:, 0:1])
        S.recip(rs[:, 0:n], rs[:, 0:n])
        return rs[:, 0:n]

    def x_tiles(self, t, ring):
        if not self.xdram:
            return [self.x[kc][t] for kc in range(8)]
        out = []
        for kc in range(8):
            xb = ring.get()
            self.S.dma("sp", xb, self.xcur[kc][t])
            out.append(xb)
        return out

    def make_h(self, l):
        S, c = self.S, self.c
        Al, Bl = self.A_l(l), self.B_l(l)
        mk = self.A.top
        ring = self.ring(8, [512], F32) if self.xdram else None
        for t in range(self.NT):
            xt = self.x_tiles(t, ring)
            rs = self.rms_stats(xt, c["onesD"])
            for kc in range(8):
                tmp = self.f32pool.get()
                S.tt("dve", tmp, xt[kc], rs, ALU.mult)
                S.act(self.h[kc][t], tmp, AF.Identity, scale=Al[:, kc:kc + 1], bias=Bl[:, kc:kc + 1])
        self.A.release_to(mk)

    def load_w(self, l, c0, n, src=None):
        wb = self.wpool.get()
        w = self.I["w_in"] if src is None else src
        self.S.dma("pool", wb[:, :, 0:n], w[l][:, c0:c0 + n].rearrange("(kc p) n -> p kc n", p=128))
        return wb

    def proj_fm(self, wb, co, ncols, t, ps=None, pcol=0, ntok=512, tok0=0):
        S = self.S
        if ps is None:
            ps = self.psAll.get()
        for kc in range(8):
            S.mm(ps[0:ncols, pcol:pcol + ntok], wb[:, kc, co:co + ncols], self.h[kc][t][:, tok0:tok0 + ntok],
                 start=(kc == 0), stop=(kc == 7))
        return ps

    def proj_tm(self, wb, co, ncols, t, sub, ps=None, pcol=0, shift=0):
        S = self.S
        if ps is None:
            ps = self.psAll.get()
        a = sub * 128 + shift
        for kc in range(8):
            S.mm(ps[:, pcol:pcol + ncols], self.h[kc][t][:, a:a + 128], wb[:, kc, co:co + ncols],
                 start=(kc == 0), stop=(kc == 7))
        return ps

    def attend(self, sc_list, v_list, nq, scale, fin, tail=None):
        S, c = self.S, self.c
        g = 512 // nq
        nk = len(sc_list)
        O = self.psO.get()
        Dn = None
        for k0 in range(0, nk, g):
            k1 = min(nk, k0 + g)
            sp = self.psS.get()
            for k in range(k0, k1):
                terms = sc_list[k]
                for i, (lt, rh) in enumerate(terms):
                    S.mm(sp[:, (k - k0) * nq:(k - k0 + 1) * nq], lt, rh, start=(i == 0), stop=(i == len(terms) - 1))

            def stage2(k0=k0, k1=k1, sp=sp):
                w = (k1 - k0) * nq
                pt = self.bfpool.get()
                S.act(pt[:, 0:w], sp[:, 0:w], AF.Exp, scale=scale)
                for k in range(k0, k1):
                    blk = pt[:, (k - k0) * nq:(k - k0 + 1) * nq]
                    S.mm(O[:, 0:nq], v_list[k], blk, start=(k == 0), stop=(k == nk - 1 and tail is None))
                if k1 == nk and tail is not None:
                    S.mm(O[:, 0:nq], tail[0], tail[1], start=False, stop=True)
                if k1 == nk:
                    fin(O, Dn)

            self.pendq.append(stage2)
            while len(self.pendq) > self.PIPE_DEPTH:
                self.pendq.pop(0)()

    PIPE_DEPTH = 3

    def flush(self):
        while self.pendq:
            self.pendq.pop(0)()

    def finish_head(self, O, Dn, parity, nq, ydst, sz, extra=None):
        S = self.S
        r0 = parity * 64
        o0 = 64 - r0
        rec = self.f32pool.get()
        if extra is not None:
            S.copy("dve", rec[r0:r0 + 64, 0:nq], O[o0:o0 + 64, 0:nq])
            S.ts("dve", rec[r0:r0 + 64, 0:nq], rec[r0:r0 + 64, 0:nq], extra[r0:r0 + 64], None, ALU.add)
            S.recip(rec[r0:r0 + 64, 0:nq], rec[r0:r0 + 64, 0:nq])
        else:
            S.recip(rec[r0:r0 + 64, 0:nq], O[o0:o0 + 64, 0:nq])
        S.tt("dve", rec[r0:r0 + 64, 0:nq], rec[r0:r0 + 64, 0:nq], sz[r0:r0 + 64], ALU.mult)
        S.tt("dve", ydst[r0:r0 + 64], O[r0:r0 + 64, 0:nq], rec[r0:r0 + 64, 0:nq], ALU.mult)

    def alloc_vaug(self, n):
        a = self.A.alloc([n, 2, 2, 128], BF16)
        tiles = [self.nb(a[:, i]) for i in range(n)]
        for t in tiles:
            self.S.memset("dve", t, 1.0)
        return tiles

    def fill_vaug(self, eng, vt, src):
        sv = src.v(lambda a: a.rearrange("p (f q e) -> p f q e", f=2, q=2))
        for par in range(2):
            self.S.copy(eng, vt[:, :, par, par * 64:(par + 1) * 64], sv[:, :, par, :])

    def fm_group(self, wb, co, nch, dst, func=AF.Identity, scale=1.0, f32dst=None):
        S = self.S
        for t in range(self.NT):
            for fc in range(nch):
                ps = self.proj_fm(wb, co + fc * 128, 128, t)
                S.act(dst[fc][t], ps, func, scale=scale)
                if f32dst is not None:
                    S.copy("dve", f32dst[fc][t], ps)

    def tile2(self, shape_cols, dt, nch, parts=128, shared=False):
        a = self.A.alloc([nch, self.TOK], dt)
        if not shared:
            return [[self.nb(a[0:parts, fc, t * 512:(t + 1) * 512]) for t in range(self.NT)] for fc in range(nch)]
        full = [self.nb(a[0:parts, fc, :]) for fc in range(nch)]
        self.last_full = full
        return [[full[fc][:, t * 512:(t + 1) * 512] for t in range(self.NT)] for fc in range(nch)]

    def branch_na(self, l):
        S, A, I, O, c = self.S, self.A, self.I, self.O, self.c
        lat, NT, TOK = self.lat, self.NT, self.TOK
        y = self.y[0]
        qT = self.tile2(512, BF16, 2)
        kT = self.tile2(512, BF16, 2, shared=lat)
        kTf = self.last_full if lat else None
        wb = self.load_w(l, C_NA_Q, 512)
        self.fm_group(wb, 0, 2, qT, scale=(0.125 if lat else 1.0))
        if lat:
            self.fm_group(wb, 256, 2, kT)
        else:
            stg = self.ring(2, [512], F32)
            for t in range(NT):
                for fc in range(2):
                    ps = self.proj_fm(wb, 256 + fc * 128, 128, t)
                    st = stg.get()
                    S.copy("dve", st, ps)
                    S.copy("dve", kT[fc][t], st)
                    S.dma("sp", O["o_nak"][l, fc * 128:(fc + 1) * 128, :], st)
        if os.environ.get("NA_STOP") == "1":
            return
        wb = self.load_w(l, C_NA_V, 512)
        self.fm_group(wb, 256, 2, y, func=AF.Silu)
        ntile = TOK // 128
        V = self.alloc_vaug(ntile)
        if not lat:
            stg = self.ring(2, [256], F32)
        for i in range(ntile):
            ps = self.proj_tm(wb, 0, 256, i // 4, i % 4)
            self.fill_vaug("act", V[i], ps[:, 0:256])
            if not lat:
                st = stg.get()
                S.copy("dve", st, ps[:, 0:256])
                S.dma("sp", O["o_nav"][l, i * 128:(i + 1) * 128, :], st)
        if os.environ.get("NA_STOP") == "2":
            return
        if not lat:
            for s in range(2):
                for hh in range(4):
                    fc, par = hh // 2, hh % 2
                    r = slice(par * 64, par * 64 + 64)
                    q = qT[fc][0][r, s * 256:(s + 1) * 256]
                    sc = [[(kT[fc][0][r, s * 256 + kt * 128: s * 256 + (kt + 1) * 128], q)] for kt in range(2)]
                    vl = [V[s * 2 + kt][:, fc, par, :] for kt in range(2)]
                    yd = y[fc][0][:, s * 256:(s + 1) * 256]
                    self.attend(sc, vl, 256, 0.125,
                                fin=lambda O_, D_, par=par, yd=yd, hh=hh: self.finish_head(O_, D_, par, 256, yd, yd))
            self.flush()
            return
        Vodd = self.alloc_vaug(ntile - 1)
        for i in range(ntile - 1):
            tok0 = i * 128 + 64
            t, off = tok0 // 512, tok0 % 512
            if off + 128 <= 512:
                ps = self.psA.get()
                for kc in range(8):
                    S.mm(ps[:, 0:256], self.h[kc][t][:, off:off + 128], wb[:, kc, 0:256], start=(kc == 0), stop=(kc == 7))
                self.fill_vaug("act", Vodd[i], ps[:, 0:256])
            else:
                ps = self.psA.get()
                for kc in range(8):
                    S.mm(ps[0:64, 0:256], self.h[kc][t][:, 448:512], wb[:, kc, 0:256], start=(kc == 0), stop=(kc == 7))
                ps2 = self.psA.get()
                for kc in range(8):
                    S.mm(ps2[0:64, 0:256], self.h[kc][t + 1][:, 0:64], wb[:, kc, 0:256], start=(kc == 0), stop=(kc == 7))
                self.fill_vaug("act", Vodd[i][0:64], ps[0:64, 0:256])
                self.fill_vaug("act", Vodd[i][64:128], ps2[0:64, 0:256])
        if os.environ.get("NA_STOP") == "3":
            return
        kc_a = A.alloc([2, 512], BF16)
        KcT = [self.nb(kc_a[:, fc, :]) for fc in range(2)]
        for fc in range(2):
            S.dma("pool", KcT[fc], I["cnakT"][l, fc * 128:(fc + 1) * 128, :])
        Vc = self.alloc_vaug(4)
        vtmp = self.ring(2, [256], BF16)
        for i in range(4):
            vt_ = vtmp.get()
            S.dma("pool", vt_, I["cnav"][l, i * 128:(i + 1) * 128, :])
            self.fill_vaug("dve", Vc[i], vt_)
        if os.environ.get("NA_STOP") == "4":
            return
        msk = self.nb(A.alloc([64], F32))
        tab = self.nb(A.alloc([60, 64], BF16))
        anti = self.nb(A.alloc([64], BF16))
        for hf in range(2):
            S.dma("sp", msk[hf * 64:(hf + 1) * 64], I["namask"])
            S.dma("pool", anti[hf * 64:(hf + 1) * 64], I["antiI"])
        tabf_r = self.ring(2, [15, 64], F32)
        for hh in range(4):
            tabf = tabf_r.get()
            src = bass.AP(tensor=I["rpbp"].tensor, offset=I["rpbp"][l, hh * 15, 0].offset,
                          ap=[[1, 64], [127, 15], [1, 64]])
            for hf in range(2):
                S.dma("sp", tabf[hf * 64:(hf + 1) * 64], src)
            S.tt("dve", tab[:, hh * 15:(hh + 1) * 15, :], tabf,
                 msk.v(lambda a: a.unsqueeze(1).broadcast_to([128, 15, 64])), ALU.add)
        if os.environ.get("NA_STOP") == "5":
            return
        for r_ in range(32 if os.environ.get("NA_VAR") != "1row" else 1):
            ws = min(max(r_ - 4, 0), 24)
            t, qo = (r_ * 64) // 512, (r_ * 64) % 512
            for hh in range(4):
                fc, par = hh // 2, hh % 2
                rr = slice(par * 64, par * 64 + 64)
                q = qT[fc][t][rr, qo:qo + 64]
                sc, vl = [], []
                for kt in range(4):
                    row0 = ws + 2 * kt
                    tk0 = row0 * 64
                    dr = row0 - r_ + 7
                    terms = [(kTf[fc][rr, tk0:tk0 + 128], q)]
                    if os.environ.get("NA_VAR") != "nobias":
                        terms.append((tab[rr, hh * 15 + dr: hh * 15 + dr + 2, :].v(lambda a: a.rearrange("p a b -> p (a b)")), anti[rr]))
                    sc.append(terms)
                    vt = V[row0 // 2] if row0 % 2 == 0 else Vodd[row0 // 2]
                    vl.append(vt[:, fc, par, :])
                for ct in range(4):
                    sc.append([(KcT[fc][rr, ct * 128:(ct + 1) * 128], q)])
                    vl.append(Vc[ct][:, fc, par, :])
                yd = y[fc][t][:, qo:qo + 64]
                self.attend(sc, vl, 64, 1.0,
                            fin=lambda O_, D_, par=par, yd=yd, hh=hh: self.finish_head(O_, D_, par, 64, yd, yd))
        self.flush()

    def rope_fm(self, dst, ps_x, ps_r, cs, sn):
        S = self.S
        t1 = self.f32pool.get()
        t2 = self.f32pool.get()
        n = ps_x.ap.shape[-1]
        p = ps_x.ap.shape[0]
        S.tt("dve", t1[0:p, 0:n], ps_x, cs, ALU.mult)
        S.tt("dve", t2[0:p, 0:n], ps_r, sn, ALU.mult)
        S.tt("dve", dst, t1[0:p, 0:n], t2[0:p, 0:n], ALU.add)

    def make_rot(self, wb, co, n, half):
        S = self.S
        wr = self.alloc([8, n], BF16)
        src = wb[:, :, co:co + n].v(lambda a: a.rearrange("p kc (g two h) -> p kc g two h", two=2, h=half))
        dst = wr.v(lambda a: a.rearrange("p kc (g two h) -> p kc g two h", two=2, h=half))
        S.copy("dve", dst[:, :, :, 0, :], src[:, :, :, 1, :])
        S.copy("act", dst[:, :, :, 1, :], src[:, :, :, 0, :])
        return wr

    def branch_wg(self, l):
        S, A, I, O, c = self.S, self.A, self.I, self.O, self.c
        lat, NT, TOK = self.lat, self.NT, self.TOK
        y = self.y[3]
        qT = self.tile2(512, BF16, 2)
        nkt = (TOK + 512) // 512 if lat else NT
        ka = A.alloc([TOK + (512 if lat else 0)], BF16)
        kT = [self.nb(ka[:, t * 512:(t + 1) * 512]) for t in range(nkt)]
        wb6 = self.load_w(l, C_ML_Z, 512)
        wb7 = self.load_w(l, C_WG_K, 512)
        if lat:
            wr6 = self.make_rot(wb6, 256, 256, 32)
            wr7 = self.make_rot(wb7, 0, 128, 32)
            rp = self.ring(2, [2, 512], F32)
            for t in range(NT):
                rt = rp.get()
                S.dma("sp", rt, I["ropewg"][:, :, t * 512:(t + 1) * 512])
                for fc in range(2):
                    ps = self.proj_fm(wb6, 256 + fc * 128, 128, t)
                    ps2 = self.proj_fm(wr6, fc * 128, 128, t)
                    self.rope_fm(qT[fc][t], ps, ps2, rt[:, 0, :], rt[:, 1, :])
                ps = self.proj_fm(wb7, 0, 128, t)
                ps2 = self.proj_fm(wr7, 0, 128, t)
                self.rope_fm(kT[t], ps, ps2, rt[:, 0, :], rt[:, 1, :])
            S.dma("pool", kT[NT], I["cwgkT"][l])
        else:
            self.fm_group(wb6, 256, 2, qT)
            stg = self.ring(2, [512], F32)
            for t in range(NT):
                ps = self.proj_fm(wb7, 0, 128, t)
                S.act(kT[t], ps, AF.Identity)
                st = stg.get()
                S.copy("dve", st, ps)
                S.dma("sp", O["o_wgk"][l], st)
        self.fm_group(wb7, 256, 2, y, func=AF.Silu)
        ntile = TOK // 128
        nvt = ntile + (4 if lat else 0)
        va = A.alloc([nvt, 128], BF16)
        V = [self.nb(va[:, i, :]) for i in range(nvt)]
        if not lat:
            stg2 = self.ring(2, [128], F32)
        for i in range(ntile):
            ps = self.proj_tm(wb7, 128, 128, i // 4, i % 4)
            S.copy("act", V[i], ps[:, 0:128])
            if not lat:
                st = stg2.get()
                S.copy("dve", st, ps[:, 0:128])
                S.dma("sp", O["o_wgv"][l, i * 128:(i + 1) * 128, :], st)
        if lat:
            for i in range(4):
                S.dma("pool", V[ntile + i], I["cwgv"][l, i * 128:(i + 1) * 128, :])
            wm = self.nb(A.alloc([2, 128], BF16))
            S.dma("pool", wm, I["wgmask"])
        v2a = A.alloc([nvt, 2, 2, 128], BF16)
        V2 = [self.nb(v2a[:, i]) for i in range(nvt)]
        for i in range(nvt):
            S.memset("dve", V2[i], 1.0)
            for kvh in range(2):
                for par in range(2):
                    S.copy("dve", V2[i][:, kvh, par, par * 64:(par + 1) * 64], V[i][:, kvh * 64:(kvh + 1) * 64])
        esink = c["esink"]
        qalt_a = A.alloc([TOK], BF16)
        qalt = [self.nb(qalt_a[:, t * 512:(t + 1) * 512]) for t in range(NT)]
        for t in range(NT):
            S.copy("act", qalt[t][0:64], qT[0][t][64:128])
            S.copy("dve", qalt[t][64:128], qT[1][t][0:64])
        self.wg_qalt = qalt
        srow = self.nb(A.alloc([4, 128], BF16, parts=1))
        S.memset("dve", srow, 0.0)
        for hh in range(4):
            dh = 64 - (hh % 2) * 64
            S.copy("dve", srow[:, hh, dh:dh + 64],
                   esink[0:1, l * 4 + hh:l * 4 + hh + 1].v(lambda a: a.broadcast_to([1, 64])))
        if not lat:
            for s in range(2):
                for hh in range(4):
                    fc, par, kvh = hh // 2, hh % 2, hh // 2
                    r = slice(par * 64, par * 64 + 64)
                    kr = slice(kvh * 64, kvh * 64 + 64)
                    q = self.wgq_rows(qT, hh, 0, s * 256, 256)
                    sc = [[(kT[0][kr, s * 256 + kt * 128: s * 256 + (kt + 1) * 128], q)] for kt in range(2)]
                    vl = [V2[s * 2 + kt][:, kvh, par, :] for kt in range(2)]
                    yd = y[fc][0][:, s * 256:(s + 1) * 256]
                    self.attend(sc, vl, 256, 0.125,
                                fin=lambda O_, D_, par=par, yd=yd, hh=hh: self.finish_head(O_, D_, par, 256, yd, yd),
                                tail=(srow[:, hh, :], c["onesrow"][0:1, 0:256]))
            self.flush()
            return
        for n in range(TOK // 128):
            t, qo = n // 4, (n % 4) * 128
            for hh in range(4):
                fc, par, kvh = hh // 2, hh % 2, hh // 2
                kr = slice(kvh * 64, kvh * 64 + 64)
                q = self.wgq_rows(qT, hh, t, qo, 128)
                sc, vl = [], []
                for dn in (-1, 0, 1):
                    kb = n + dn
                    if kb < 0 or kb >= TOK // 128:
                        continue
                    terms = [(kT[kb // 4][kr, (kb % 4) * 128:(kb % 4 + 1) * 128], q)]
                    if dn != 0:
                        terms.append((wm[:, 0 if dn < 0 else 1, :], c["identb"]))
                    sc.append(terms)
                    vl.append(V2[kb][:, kvh, par, :])
                for ct in range(4):
                    sc.append([(kT[NT][kr, ct * 128:(ct + 1) * 128], q)])
                    vl.append(V2[ntile + ct][:, kvh, par, :])
                yd = y[fc][t][:, qo:qo + 128]
                self.attend(sc, vl, 128, 0.125,
                            fin=lambda O_, D_, par=par, yd=yd, hh=hh: self.finish_head(O_, D_, par, 128, yd, yd),
                            tail=(srow[:, hh, :], c["onesrow"][0:1, 0:128]))
        self.flush()

    def wgq_rows(self, qT, hh, t, o, n):
        if hh == 0:
            return qT[0][t][0:64, o:o + n]
        if hh == 1:
            return self.wg_qalt[t][0:64, o:o + n]
        if hh == 2:
            return self.wg_qalt[t][64:128, o:o + n]
        return qT[1][t][64:128, o:o + n]

    def branch_mla(self, l):
        S, A, I, O, c = self.S, self.A, self.I, self.O, self.c
        lat, NT, TOK = self.lat, self.NT, self.TOK
        y = self.y[2]
        ntile = TOK // 128
        nkt = NT + (1 if lat else 0)
        wb5 = self.load_w(l, C_ML_QA, 416)
        wb6 = self.load_w(l, C_ML_Z, 256)
        wqn = self.nb(A.alloc([2, 4, 64], BF16))
        wqr = self.nb(A.alloc([2, 4, 32], BF16))
        wq4 = I["wqb"][l].rearrange("(kc p) (h e) -> p kc h e", p=128, e=96)
        for kc in range(2):
            S.dma("pool", wqn[:, kc], wq4[:, kc, :, 0:64])
            S.dma("pool", wqr[:, kc], wq4[:, kc, :, 64:96])
        wkn = self.nb(A.alloc([4, 64], BF16))
        wkv = self.nb(A.alloc([4, 64], BF16))
        wk4 = I["wkvb"][l].rearrange("p (h e) -> p h e", e=128)
        S.dma("pool", wkn, wk4[:, :, 0:64])
        S.dma("pool", wkv, wk4[:, :, 64:128])
        if lat:
            wqrr = self.nb(A.alloc([2, 4, 32], BF16))
            S.copy("dve", wqrr[:, :, :, 0:16], wqr[:, :, :, 16:32])
            S.copy("dve", wqrr[:, :, :, 16:32], wqr[:, :, :, 0:16])
            wr5 = self.make_rot(wb5, 384, 32, 16)
        qha = A.alloc([4, TOK], BF16)
        qh = [[self.nb(qha[0:96, hh, t * 512:(t + 1) * 512]) for t in range(NT)] for hh in range(4)]
        kha = A.alloc([4, nkt * 512], BF16)
        kh = [[self.nb(kha[0:96, hh, t * 512:(t + 1) * 512]) for t in range(nkt)] for hh in range(4)]
        kr_r = self.ring(3, [512], BF16, parts=32)
        nvt = ntile + (4 if lat else 0)
        V = self.alloc_vaug(nvt)
        ckvT = self.ring(2, [512], BF16)
        qan_r = self.ring(4, [512], BF16)
        qaf_r = self.ring(3, [512], F32)
        tq_r = self.ring(2, [512], BF16, parts=32)
        if lat:
            rp = self.ring(2, [2, 512], F32)
        stg = self.ring(8, [160], F32)
        gq = c["qng"]
        def stageA(t):
            rt = None
            if lat:
                rt = rp.get()
                S.dma("sp", rt[0:32], I["ropeml"][:, :, t * 512:(t + 1) * 512])
            qaf = []
            for kc in range(2):
                ps = self.proj_fm(wb5, kc * 128, 128, t)
                f = qaf_r.get()
                S.copy("act", f, ps)
                qaf.append(f)
            rs = self.rms_stats(qaf, c["ones256"])
            qan = []
            for kc in range(2):
                qb = qan_r.get()
                S.stt("dve", qb, qaf[kc], gq[:, l, kc:kc + 1], rs, ALU.mult, ALU.mult)
                qan.append(qb)
            sts = []
            for sub in range(4):
                i = t * 4 + sub
                ps = self.proj_tm(wb5, 256, 160, t, sub)
                ssq = self.f32pool.get()
                junk = self.f32pool.get()
                S.memset("dve", ssq[:, 0:1], 0.0)
                S.act(junk[:, 0:128], ps[:, 0:128], AF.Square, accum=ssq[:, 0:1])
                S.act(ssq[:, 1:2], ssq[:, 0:1], AF.Sqrt, scale=1.0 / 128.0, bias=c["eps"][:, 0:1])
                S.recip(ssq[:, 2:3], ssq[:, 1:2])
                st = stg.get()
                S.stt("dve", st[:, 0:128], ps[:, 0:128], ssq[:, 2:3], c["kvng"][:, l, :], ALU.mult, ALU.mult)
                S.copy("dve", st[:, 128:160], ps[:, 128:160])
                if not lat:
                    S.dma("sp", O["o_kv"][l, i * 128:(i + 1) * 128, :], st)
                sts.append(st)
            kps = None
            if lat:
                kps = (self.proj_fm(wb5, 384, 32, t, ps=self.psAll.get()), self.proj_fm(wr5, 0, 32, t, ps=self.psAll.get()))
                krs = kr_r.get()
                self.rope_fm(krs, kps[0][0:32], kps[1][0:32], rt[0:32, 0, :], rt[0:32, 1, :])
                kps = krs
            return rt, qan, sts, kps

        def stageB(t, rt, qan, sts, krs_lat):
            for fc in range(2):
                ps = self.psA.get()
                for kc in range(2):
                    S.mm(ps, wqn[:, kc, 2 * fc:2 * fc + 2, :].v(lambda a: a.rearrange("p a b -> p (a b)")), qan[kc],
                         start=(kc == 0), stop=(kc == 1))
                S.copy("act", qh[2 * fc][t][0:64], ps[0:64])
                S.copy("dve", qh[2 * fc + 1][t][0:64], ps[64:128])
            for hh in range(4):
                ps = self.psA.get()
                for kc in range(2):
                    S.mm(ps[0:32], wqr[:, kc, hh, :], qan[kc], start=(kc == 0), stop=(kc == 1))
                if lat:
                    ps2 = self.psA.get()
                    for kc in range(2):
                        S.mm(ps2[0:32], wqrr[:, kc, hh, :], qan[kc], start=(kc == 0), stop=(kc == 1))
                    tq = tq_r.get()
                    self.rope_fm(tq, ps[0:32], ps2[0:32], rt[0:32, 0, :], rt[0:32, 1, :])
                    S.copy("act", qh[hh][t][64:96], tq)
                else:
                    S.copy("act", qh[hh][t][64:96], ps[0:32])
            ck = ckvT.get()
            krs = krs_lat if lat else kr_r.get()
            for sub in range(4):
                st = sts[sub]
                pt = self.psA.get()
                S.tr(pt[:, 0:128], st[:, 0:128], c["ident"])
                S.copy("act", ck[:, sub * 128:(sub + 1) * 128], pt[:, 0:128])
                if not lat:
                    pt2 = self.psA.get()
                    S.tr(pt2[0:32, 0:128], st[:, 128:160], c["ident"])
                    S.copy("act", krs[:, sub * 128:(sub + 1) * 128], pt2[0:32, 0:128])
            for hh in range(4):
                S.copy("act" if hh % 2 else "dve", kh[hh][t][64:96], krs)
            self.mla_expand(ck, t, kh, V, wkn, wkv)

        prevA = None
        for t in range(NT):
            curA = (t,) + stageA(t)
            if prevA is not None:
                stageB(*prevA)
            prevA = curA
        stageB(*prevA)
        if lat:
            ck = ckvT.get()
            S.dma("pool", ck, I["cckvT"][l])
            for hh in range(4):
                S.dma("pool", kh[hh][NT][64:96], I["ckrT"][l])
            self.mla_expand(ck, NT, kh, V, wkn, wkv)
        self.fm_group(wb6, 0, 2, y, func=AF.Silu)
        scale = 96.0 ** -0.5
        if not lat:
            for s in range(2):
                for hh in range(4):
                    fc, par = hh // 2, hh % 2
                    r = slice(par * 64, par * 64 + 64)
                    q1 = qh[hh][0][:, s * 256:(s + 1) * 256]
                    sc = [[(kh[hh][0][:, s * 256 + kt * 128: s * 256 + (kt + 1) * 128], q1)] for kt in range(2)]
                    vl = [V[s * 2 + kt][:, fc, par, :] for kt in range(2)]
                    yd = y[fc][0][:, s * 256:(s + 1) * 256]
                    self.attend(sc, vl, 256, scale,
                                fin=lambda O_, D_, par=par, yd=yd, hh=hh: self.finish_head(O_, D_, par, 256, yd, yd))
            self.flush()
            return
        for t in range(NT):
            for hh in range(4):
                fc, par = hh // 2, hh % 2
                r = slice(par * 64, par * 64 + 64)
                q1 = qh[hh][t]
                sc, vl = [], []
                for kt in range(nkt * 4):
                    k5, ko = kt // 4, (kt % 4) * 128
                    sc.append([(kh[hh][k5][:, ko:ko + 128], q1)])
                    vl.append(V[kt][:, fc, par, :])
                self.attend(sc, vl, 512, scale,
                            fin=lambda O_, D_, par=par, yd=y[fc][t]: self.finish_head(O_, D_, par, 512, yd, yd))
        self.flush()

    def mla_expand(self, ck, t, kh, V, wkn, wkv):
        S = self.S
        for fc in range(2):
            ps = self.psA.get()
            S.mm(ps, wkn[:, 2 * fc:2 * fc + 2, :].v(lambda a: a.rearrange("p a b -> p (a b)")), ck)
            S.copy("act", kh[2 * fc][t][0:64], ps[0:64])
            S.copy("dve", kh[2 * fc + 1][t][0:64], ps[64:128])
        for sub in range(4):
            ps = self.psA.get()
            S.mm(ps[:, 0:256], ck[:, sub * 128:(sub + 1) * 128], wkv.v(lambda a: a.rearrange("p a b -> p (a b)")))
            self.fill_vaug("dve", V[t * 4 + sub], ps[:, 0:256])

    def branch_hg(self, l):
        S, A, I, O, c = self.S, self.A, self.I, self.O, self.c
        lat, NT, TOK = self.lat, self.NT, self.TOK
        y = self.y[1]
        ntile = TOK // 128
        nseq = 1 if lat else 2
        tps = ntile // nseq
        wb2 = self.load_w(l, C_HG_Q, 512)
        wb3 = self.load_w(l, C_HG_FB, 512)
        wb4 = self.load_w(l, C_HG_OG, 512)
        qs = self.tile2(512, BF16, 2)
        self.fm_group(wb2, 0, 2, qs, func=AF.Silu)
        va = A.alloc([ntile, 256], BF16)
        V = [self.nb(va[:, i, :]) for i in range(ntile)]
        for i in range(ntile):
            ps = self.proj_tm(wb3, 256, 256, i // 4, i % 4)
            S.copy("act", V[i], ps[:, 0:256])
        oacc_a = A.alloc([2, TOK], F32)
        oacc = [[self.nb(oacc_a[:, fc, i * 128:(i + 1) * 128]) for i in range(ntile)] for fc in range(2)]
        st0 = [self.alloc([2, 64], F32) for _ in range(2)]
        r_sst = self.ring(8 * (1 if lat else 2) + 4, [2, 64], F32)
        lbt1 = self.alloc([512], F32)
        omlt1 = self.alloc([512], F32)
        mk = A.top
        self.lb_tables(512, I["lbl"].rearrange("l n -> (l n)").partition_broadcast(128),
                       lambda l_: lbt1, lambda l_: omlt1, [l])
        A.release_to(mk)
        omlf, nomlf = c["oml_fm"], c["noml_fm"]
        r_sg = self.ring(2, [256], F32)
        r_t = self.ring(2, [256], F32)
        r_k = self.ring(3, [256], F32)
        r_lf = self.ring(3, [256], F32)
        r_eq = self.ring(3, [2, 128], F32)
        r_ek = self.ring(2, [2, 128], F32)
        r_ee = self.ring(2, [256], F32)
        r_kf = self.ring(3, [2, 128], F32)
        r_qd = self.ring(4, [2, 128], BF16)
        r_kd = self.ring(4, [2, 128], BF16)
        r_kh = self.ring(3, [256], BF16)
        r_vb = self.ring(3, [4, 4, 64], BF16)
        r_us = self.ring(2, [2, 4, 64], F32)
        r_sb = self.ring(3, [2, 4, 128], BF16)
        for sbz in r_sb.bufs:
            S.memset("dve", sbz, 0.0)
        r_at = self.ring(6, [128], BF16)
        hg_pending = [None]
        oacc_init = set()
        states = {}
        for s in range(nseq):
            for d in range(2):
                if lat:
                    S.dma("sp", st0[d], I["st0"][l][:, d * 128:(d + 1) * 128].rearrange("p (a b) -> p a b", a=2))
                    states[(s, d)] = st0[d]
                else:
                    z0 = self.alloc([2, 64], F32)
                    S.memset("dve", z0, 0.0)
                    states[(s, d)] = z0
        q2, q3, q4 = [], [], []

        def run_queues(drain=False):
            lim = 0 if drain else 1
            while len(q2) > lim:
                q2.pop(0)()
            while len(q3) > lim:
                q3.pop(0)()
            while len(q4) > lim:
                q4.pop(0)()

        for step in range(tps):
            for s in range(nseq):
                for d in range(2):
                    wbd, cod = (wb2, 256) if d == 0 else (wb3, 0)
                    ti = step if d == 0 else tps - 1 - step
                    i = s * tps + ti
                    t5, sub = i // 4, i % 4
                    Um = c["cmat"][:, 1 + d, :]
                    Lm = c["cmat"][:, 3 + d, :]
                    ps = self.proj_tm(wbd, cod, 256, t5, sub)
                    sg = r_sg.get()
                    S.act(sg, ps[:, 0:256], AF.Sigmoid)
                    kf = r_kf.get()
                    for fc in range(2):
                        pk = self.proj_fm(wbd, cod + fc * 128, 128, t5, ntok=128, tok0=sub * 128)
                        sgf = self.f32pool.get()
                        S.act(sgf[:, 0:128], pk[:, 0:128], AF.Sigmoid)
                        S.ts("dve", kf[:, fc, :], sgf[:, 0:128], nomlf[:, l, d * 2 + fc:d * 2 + fc + 1],
                             omlf[:, l, d * 2 + fc:d * 2 + fc + 1], ALU.mult, ALU.add)
                    tt_ = r_t.get()
                    S.tt("dve", tt_, sg, omlt1[:, d * 256:(d + 1) * 256], ALU.mult)
                    ktm = r_k.get()
                    S.tt("dve", ktm, omlt1[:, d * 256:(d + 1) * 256], tt_, ALU.subtract)
                    lf = r_lf.get()
                    S.stt("dve", lf, tt_, 1e-30, lbt1[:, d * 256:(d + 1) * 256], ALU.max, ALU.add)
                    S.act(lf, lf, AF.Ln)

                    def stage2(s=s, d=d, i=i, t5=t5, sub=sub, Um=Um, Lm=Lm, lf=lf, kf=kf, ktm=ktm):
                        pb = self.psB.get()
                        for fc in range(2):
                            S.mm(pb[:, fc * 128:(fc + 1) * 128], lf[:, fc * 128:(fc + 1) * 128], Um)
                        S.mm(pb[:, 256:512], Lm, lf)
                        eq = r_eq.get()
                        ek = r_ek.get()
                        ee = r_ee.get()
                        pbv = pb[:, 0:256].v(lambda a: a.rearrange("p (a b) -> p a b", a=2))
                        S.act(eq, pbv, AF.Exp)
                        S.act(ek, pbv, AF.Exp, scale=-1.0)
                        S.act(ee, pb[:, 256:512], AF.Exp)
                        qd = r_qd.get()
                        kd = r_kd.get()
                        for fc in range(2):
                            S.tt("dve", qd[:, fc, :], qs[fc][t5][:, sub * 128:(sub + 1) * 128], eq[:, fc, :], ALU.mult)
                        S.tt("dve", kd, kf, ek, ALU.mult)
                        kh = r_kh.get()
                        S.tt("dve", kh, ktm, ee, ALU.mult)
                        vb = r_vb.get()
                        for cch in range(4):
                            S.ts("dve", vb[:, :, cch, :], V[i].v(lambda a: a.rearrange("p (h e) -> p h e", h=4)),
                                 c["ind"][:, cch:cch + 1], None, ALU.mult)

                        def stage3():
                            us = r_us.get()
                            for pr in range(2):
                                pu = self.psA.get()
                                S.mm(pu, kh[:, pr * 128:(pr + 1) * 128],
                                     vb[:, 2 * pr:2 * pr + 2].v(lambda a: a.rearrange("p a b c -> p (a b c)")))
                                puv = pu.v(lambda a: a.rearrange("p (a b c) -> p a b c", a=2, b=4))
                                S.copy("act", us[0:64, pr], puv[0:64, 0])
                                S.copy("act", us[64:128, pr], puv[64:128, 1])
                            sb = r_sb.get()
                            state = states[(s, d)]
                            corder = range(4) if d == 0 else range(3, -1, -1)
                            for cch in corder:
                                S.copy("act", sb[0:64, :, cch, 0:64], state[0:64])
                                S.copy("act", sb[64:128, :, cch, 64:128], state[64:128])
                                dcol = cch * 32 + (31 if d == 0 else 0)
                                nxt = r_sst.get()
                                for pr in range(2):
                                    S.stt("dve", nxt[:, pr, :], state[:, pr, :], eq[:, pr, dcol:dcol + 1],
                                          us[:, pr, cch, :], ALU.mult, ALU.add)
                                state = nxt
                            states[(s, d)] = state

                            def stageC():
                                ats = []
                                for hh in range(4):
                                    fc, par = hh // 2, hh % 2
                                    r = slice(par * 64, par * 64 + 64)
                                    pa = self.psAll.get()
                                    S.mm(pa[:, 0:128], kd[r, fc, :], qd[r, fc, :])
                                    at = r_at.get()
                                    S.tt("dve", at, pa[:, 0:128], Um, ALU.mult)
                                    ats.append(at)
                                for hh in range(4):
                                    fc, par = hh // 2, hh % 2
                                    r = slice(par * 64, par * 64 + 64)
                                    at = ats[hh]
                                    po = self.psAll.get()
                                    for cch in range(4):
                                        cs = slice(cch * 32, (cch + 1) * 32)
                                        S.mm(po[:, cs], V[i][:, fc * 128:(fc + 1) * 128], at[:, cs], start=True, stop=False)
                                        S.mm(po[:, cs], sb[r, fc, cch, :], qd[r, fc, cs], start=False, stop=True)
                                    if (fc, i, par) not in oacc_init:
                                        oacc_init.add((fc, i, par))
                                        S.copy("act", oacc[fc][i][r], po[r, 0:128])
                                    else:
                                        S.tt("dve", oacc[fc][i][r], oacc[fc][i][r], po[r, 0:128], ALU.add)

                            q4.append(stageC)

                        q3.append(stage3)

                    q2.append(stage2)
                    run_queues()
        run_queues(drain=True)
        if not lat:
            for s in range(nseq):
                for d in range(2):
                    S.dma("sp", O["o_st"][l][:, (s * 2 + d) * 128:(s * 2 + d + 1) * 128].rearrange("p (a b) -> p a b", a=2),
                          states[(s, d)])
        ogz = self.ring(2, [512], F32)
        for t in range(NT):
            for fc in range(2):
                src = B(oacc_a[:, fc, t * 512:(t + 1) * 512])
                toks = [oacc[fc][t * 4 + j] for j in range(4)]
                sq = self.f32pool.get()
                S.op("act", lambda e, o=sq, i_=src: e.activation(out=o.ap, in_=i_.ap, func=AF.Square), r=toks, w=[sq])
                pn = self.psD.get()
                S.mm(pn, c["blk64"], sq)
                rs = self.f32pool.get()
                S.act(rs, pn, AF.Sqrt, bias=c["eps"][:, 0:1])
                S.recip(rs, rs)
                on = self.f32pool.get()
                S.op("dve", lambda e, o=on, i_=src, g=c["hgng"][:, l:l + 1], r_=rs:
                     e.scalar_tensor_tensor(out=o.ap, in0=i_.ap, scalar=g.ap, in1=r_.ap, op0=ALU.mult, op1=ALU.mult),
                     r=toks + [rs, c["hgng"]], w=[on])
                pg = self.proj_fm(wb4, fc * 128, 128, t)
                g1 = ogz.get()
                S.act(g1, pg, AF.Sigmoid)
                pz = self.proj_fm(wb4, 256 + fc * 128, 128, t)
                g2 = ogz.get()
                S.act(g2, pz, AF.Silu)
                S.tt("dve", g1, g1, g2, ALU.mult)
                S.tt("dve", y[fc][t], on, g1, ALU.mult)

    def phase3(self, l):
        S, A, I, c = self.S, self.A, self.I, self.c
        NT = self.NT
        Gl = self.G_l(l)
        GT = 2
        wbr_r = self.ring(2, [8, 128], BF16)
        wo_r = self.ring(2, [8, 128], BF16)
        g_r = self.ring(3, [512], BF16)
        p_r = self.ring(4, [512], F32)
        acc_r = self.ring(3, [512], F32)
        x_r = self.ring(6, [512], F32) if self.xdram else None
        ngt = min(NT, GT)
        ma = A.alloc([8, ngt * 512], BF16)
        mg = [[self.nb(ma[:, j, k * 512:(k + 1) * 512]) for k in range(ngt)] for j in range(8)]
        def ldj(j):
            wm_ = self.load_w(l, j * 512, 512, src=I["w_mg"])
            wbt_ = wbr_r.get()
            S.dma("pool", wbt_, I["wbr"][l, j])
            return wm_, wbt_

        def ldo(jo):
            wo_ = wo_r.get()
            S.dma("pool", wo_, I["wout"][l, jo])
            return wo_

        for g0 in range(0, NT, GT):
            tiles = list(range(g0, min(NT, g0 + GT)))
            nxt = ldj(0)
            for j in range(8):
                wm, wbt = nxt
                if j + 1 < 8:
                    nxt = ldj(j + 1)
                for k, t in enumerate(tiles):
                    acc = acc_r.get()
                    for b in range(4):
                        pg = self.proj_fm(wm, b * 128, 128, t, ps=self.psAll.get())
                        gs = g_r.get()
                        S.act(gs, pg, AF.Sigmoid)
                        pp = self.psAll.get()
                        for kc in range(2):
                            S.mm(pp, wbt[:, b * 2 + kc, :], self.y[b][kc][t], start=(kc == 0), stop=(kc == 1))
                        if b == 0:
                            S.tt("dve", acc, pp, gs, ALU.mult)
                        else:
                            pr = p_r.get()
                            S.tt("dve", pr, pp, gs, ALU.mult)
                            if b < 3:
                                S.tt("dve", acc, acc, pr, ALU.add)
                            else:
                                S.tt("dve", mg[j][k], acc, pr, ALU.add)
            nxo = ldo(0)
            for jo in range(8):
                wo = nxo
                if jo + 1 < 8:
                    nxo = ldo(jo + 1)
                for k, t in enumerate(tiles):
                    po = self.psAll.get()
                    for kc in range(8):
                        S.mm(po, wo[:, kc, :], mg[kc][k], start=(kc == 0), stop=(kc == 7))
                    if self.xdram:
                        xb = x_r.get()
                        S.dma("sp", xb, self.xcur[jo][t])
                        S.stt("dve", xb, po, Gl[:, jo:jo + 1], xb, ALU.mult, ALU.add)
                        S.dma("act", self.xd[jo][t], xb)
                    else:
                        S.stt("dve", self.x[jo][t], po, Gl[:, jo:jo + 1], self.x[jo][t], ALU.mult, ALU.add)

    def final_norm(self, dst):
        S, c = self.S, self.c
        stg = self.ring(3, [512], F32)
        ring = self.ring(8, [512], F32) if self.xdram else None
        for t in range(self.NT):
            xt = self.x_tiles(t, ring)
            rs = self.rms_stats(xt, c["onesD"])
            for kc in range(8):
                st = stg.get()
                S.stt("dve", st, xt[kc], c["fg"][:, kc:kc + 1], rs, ALU.mult, ALU.mult)
                S.dma("pool" if self.xdram else "sp", dst[kc * 128:(kc + 1) * 128, t * 512:(t + 1) * 512], st)


def _consts():
    idx = np.arange(128)
    ch = idx // 32
    same = ch[:, None] == ch[None, :]
    ident = np.eye(128, dtype=np.float32)
    Uf = (same & (idx[:, None] <= idx[None, :])).astype(np.float32)
    Ub = (same & (idx[:, None] >= idx[None, :])).astype(np.float32)
    Lf = (same & (idx[:, None] > idx[None, :])).astype(np.float32)
    Lb = (same & (idx[:, None] < idx[None, :])).astype(np.float32)
    cmat = np.stack([ident, Uf, Ub, Lf, Lb], axis=1).astype(np.float32)
    ind = (ch[:, None] == np.arange(4)[None, :]).astype(np.float32)
    cq = 63 - np.arange(64)
    qstart = np.clip(cq - 8, 0, 48)
    ck = np.arange(64)
    ok = (ck[None, :] >= qstart[:, None]) & (ck[None, :] < qstart[:, None] + 16)
    namask = np.where(ok, 0.0, NEG).astype(np.float32)
    antiI = np.zeros((64, 64), np.float32)
    antiI[63 - np.arange(64), np.arange(64)] = 1.0
    a = np.arange(128)
    prevT = np.where(a[:, None] <= a[None, :], 0.0, NEG).astype(np.float32)
    nextT = np.where(a[None, :] <= a[:, None], 0.0, NEG).astype(np.float32)
    wgmask = np.stack([prevT, nextT], axis=1).astype(np.float32)
    tpos = np.arange(2048)
    row = (tpos // 64).astype(np.float32)
    col = (tpos % 64).astype(np.float32)

    def tab(rot):
        nf = rot // 4
        inv = (np.float32(10000.0) ** (-np.arange(nf, dtype=np.float32) / np.float32(nf))).astype(np.float32)
        ang = np.concatenate([row[:, None] * inv, col[:, None] * inv], axis=-1).astype(np.float32)
        cs, sn = np.cos(ang).astype(np.float32), np.sin(ang).astype(np.float32)
        cs2 = np.concatenate([cs, cs], axis=1).T
        sn2 = np.concatenate([-sn, sn], axis=1).T
        return cs2, sn2

    c64, s64 = tab(64)
    ropewg = np.stack([np.concatenate([c64, c64], 0), np.concatenate([s64, s64], 0)], axis=1)
    c32, s32 = tab(32)
    ropeml = np.stack([c32, s32], axis=1)
    return dict(cmat=cmat, ind=ind, namask=namask, antiI=antiI, wgmask=wgmask,
                ropewg=np.ascontiguousarray(ropewg, np.float32), ropeml=np.ascontiguousarray(ropeml, np.float32))


_CACHE = {}


def _get_nc(key, **kw):
    if key not in _CACHE:
        b = Builder(**kw)
        nc = b.build()
        _CACHE[key] = (b, nc)
    return _CACHE[key]


def _host_inputs(inp):
    f = lambda a: np.ascontiguousarray(a, dtype=np.float32)
    sh = {}
    sh["w_mod"] = f(inp["w_mod"])
    sh["bmodr"] = f(inp["b_mod"].reshape(4, 24, 128).transpose(2, 0, 1))
    sh["ng"] = f(inp["norm_g"].reshape(4, 8, 128).transpose(2, 0, 1))
    sh["fg"] = f(inp["final_g"].reshape(8, 128).T)
    sh["w_in"] = f(inp["w_in"])
    wm = inp["w_in"][:, :, C_MERGE:].reshape(4, 1024, 4, 8, 128).transpose(0, 1, 3, 2, 4).reshape(4, 1024, 4096)
    sh["w_mg"] = f(wm)
    wb = inp["w_branch"].reshape(4, 4, 2, 128, 8, 128).transpose(0, 4, 3, 1, 2, 5).reshape(4, 8, 128, 8, 128)
    sh["wbr"] = f(wb)
    wo = inp["w_out"].reshape(4, 8, 128, 8, 128).transpose(0, 3, 2, 1, 4)
    sh["wout"] = f(wo)
    sh["lbl"] = f(inp["hg_lb_logits"].reshape(4, 512))
    sh["lblT"] = f(inp["hg_lb_logits"].reshape(4, 2, 2, 128).transpose(3, 0, 1, 2).reshape(128, 4, 4))
    sh["hgng"] = f(np.concatenate([inp["hg_norm_g"], inp["hg_norm_g"]], axis=1).T)
    sh["qng"] = f(inp["mla_q_norm_g"].reshape(4, 2, 128).transpose(2, 0, 1))
    sh["kvng"] = f(inp["mla_kv_norm_g"])
    sh["wqb"] = f(inp["mla_w_qb"])
    sh["wkvb"] = f(inp["mla_w_kvb"])
    sh["sink"] = f(inp["wg_sink"].reshape(1, 16))
    rp = np.zeros((4, 60, 127), np.float32)
    rp[:, :, 48:48 + 31] = inp["na_rpb"].reshape(4, 60, 31)
    sh["rpbp"] = rp
    sh.update(_consts())
    maps = []
    for i in range(8):
        b = i // 2
        m = dict(sh)
        m["xT_c"] = f(inp["x_prompt"][2 * i:2 * i + 2].reshape(512, 1024).T)
        m["xT_l"] = f(inp["x_sample"][b].T)
        m["cc"] = f(np.stack([inp["c_ctx"], inp["c"][b]], axis=1))
        m["cnakT"] = f(inp["cache_na_k"][b].reshape(4, 512, 256).transpose(0, 2, 1))
        m["cnav"] = f(inp["cache_na_v"][b].reshape(4, 512, 256))
        st = inp["state_hgrn"][b].reshape(4, 2, 2, 2, 64, 64).transpose(0, 3, 4, 1, 2, 5).reshape(4, 128, 256)
        m["st0"] = f(st)
        m["cckvT"] = f(inp["cache_mla_ckv"][b].transpose(0, 2, 1))
        m["ckrT"] = f(inp["cache_mla_krope"][b].transpose(0, 2, 1))
        m["cwgkT"] = f(inp["cache_wg_k"][b].reshape(4, 512, 128).transpose(0, 2, 1))
        m["cwgv"] = f(inp["cache_wg_v"][b].reshape(4, 512, 128))
        maps.append(m)
    return maps


def _assemble(res):
    y_prompt = np.empty((16, 256, 1024), np.float32)
    y_sample = np.empty((4, 2048, 1024), np.float32)
    na_k = np.empty((16, 4, 256, 4, 64), np.float32)
    na_v = np.empty((16, 4, 256, 4, 64), np.float32)
    st = np.empty((16, 4, 2, 4, 64, 64), np.float32)
    ckv = np.empty((16, 4, 256, 128), np.float32)
    kr = np.empty((16, 4, 256, 32), np.float32)
    wgk = np.empty((16, 4, 256, 2, 64), np.float32)
    wgv = np.empty((16, 4, 256, 2, 64), np.float32)
    for i in range(8):
        r = res[i]
        y_prompt[2 * i:2 * i + 2] = r["ypT"].T.reshape(2, 256, 1024)
        if i % 2 == 0:
            y_sample[i // 2] = r["ysT"].T
        na_k[2 * i:2 * i + 2] = r["o_nak"].transpose(0, 2, 1).reshape(4, 2, 256, 4, 64).transpose(1, 0, 2, 3, 4)
        na_v[2 * i:2 * i + 2] = r["o_nav"].reshape(4, 2, 256, 4, 64).transpose(1, 0, 2, 3, 4)
        s_ = r["o_st"].reshape(4, 2, 64, 2, 2, 2, 64)
        s_ = s_.transpose(3, 0, 4, 5, 1, 2, 6).reshape(2, 4, 2, 4, 64, 64)
        st[2 * i:2 * i + 2] = s_
        kv = r["o_kv"].reshape(4, 2, 256, 160).transpose(1, 0, 2, 3)
        ckv[2 * i:2 * i + 2] = kv[..., :128]
        kr[2 * i:2 * i + 2] = kv[..., 128:]
        wgk[2 * i:2 * i + 2] = r["o_wgk"].transpose(0, 2, 1).reshape(4, 2, 256, 2, 64).transpose(1, 0, 2, 3, 4)
        wgv[2 * i:2 * i + 2] = r["o_wgv"].reshape(4, 2, 256, 2, 64).transpose(1, 0, 2, 3, 4)
    return (y_prompt, y_sample, na_k, na_v, st, ckv, kr, wgk, wgv)


def kernel(**inputs):
    inp = {k: np.asarray(v) for k, v in inputs.items()}
    b, nc = _get_nc("full")
    maps = _host_inputs(inp)
    maps = [{k: m[k] for k in b.in_shapes} for m in maps]
    res = run_bass_kernel_spmd(nc, maps, core_ids=list(range(8)))
    return _assemble(res.results)
```
